# Optimizing a Trainium2 kernel written in Bass

```python
import jax, jax.numpy as jnp
from jax import lax
import numpy as np

D_MODEL = 1024
BATCH = 8
SEQ = 4096
DEPTH = 4

HEAD_DIM = 64
NSA_HEADS = 8
NSA_KV_GROUPS = 2
NSA_HPG = NSA_HEADS // NSA_KV_GROUPS
CMP_LEN = 32
CMP_STRIDE = 16
CMP_HIDDEN = 256
SEL_BLOCK = 64
SEL_TOPK = 16
WINDOW = 512
NSA_QBLOCK = 64
FORCE_BONUS = 1e4
GLA_HEADS = 4
GLA_DK = 32
GLA_DV = 64
GLA_RANK = 16
GLA_TAU = 16.0
GLA_CHUNK = 64
SG_GROUPS = 4
SG_CH = 64
SG_CHUNK = 128
NSA_W = NSA_HEADS * HEAD_DIM
GLA_W = GLA_HEADS * GLA_DV
SG_W = SG_GROUPS * SG_CH
D_MIX = NSA_W + GLA_W + SG_W
KV_W = NSA_KV_GROUPS * HEAD_DIM
IN_SIZES = (NSA_W, KV_W, KV_W, KV_W, KV_W, KV_W, KV_W, NSA_HEADS * 3,
            GLA_HEADS * GLA_DK, GLA_HEADS * GLA_DK, GLA_W, GLA_RANK, GLA_W, 2 * SG_W)
D_IN = sum(IN_SIZES)
D_FF = -(-8 * D_MODEL // (3 * 256)) * 256
ROPE_THETA = 10000.0
NORM_EPS = 1e-6

kernel_name = "hybrid_nsa_gla_gmlp_trunk"


def rmsnorm(x, g):
    xf = x.astype(jnp.float32)
    y = xf * lax.rsqrt(jnp.mean(xf * xf, axis=-1, keepdims=True) + NORM_EPS)
    return (y * g.astype(jnp.float32)).astype(x.dtype)


def layernorm(x, g, b):
    xf = x.astype(jnp.float32)
    mu = jnp.mean(xf, axis=-1, keepdims=True)
    var = jnp.mean(jnp.square(xf - mu), axis=-1, keepdims=True)
    y = (xf - mu) * lax.rsqrt(var + NORM_EPS)
    return (y * g.astype(jnp.float32) + b.astype(jnp.float32)).astype(x.dtype)


def rope(x, pos):
    half = x.shape[-1] // 2
    inv = 1.0 / (ROPE_THETA ** (jnp.arange(half, dtype=jnp.float32) / half))
    ang = pos[:, None] * inv[None, :]
    cos = jnp.cos(ang)[None, :, None, :]
    sin = jnp.sin(ang)[None, :, None, :]
    xf = x.astype(jnp.float32)
    x1, x2 = xf[..., :half], xf[..., half:]
    return jnp.concatenate([x1 * cos - x2 * sin, x2 * cos + x1 * sin], axis=-1).astype(x.dtype)


def masked_softmax(s, mask):
    s = jnp.where(mask, s.astype(jnp.float32), -jnp.inf)
    m = jnp.max(s, axis=-1, keepdims=True)
    m = jnp.where(jnp.isfinite(m), m, 0.0)
    e = jnp.exp(s - m)
    d = jnp.sum(e, axis=-1, keepdims=True)
    return e / jnp.where(d > 0, d, 1.0)


def compress_blocks(k_raw, pos_emb, w1, w2):
    B, S, G, hd = k_raw.shape
    nc = (S - CMP_LEN) // CMP_STRIDE + 1
    idx = np.arange(nc)[:, None] * CMP_STRIDE + np.arange(CMP_LEN)[None, :]
    blk = k_raw[:, idx] + pos_emb[None, None, :, None, :]
    blk = blk.transpose(0, 1, 3, 2, 4).reshape(B, nc, G, CMP_LEN * hd)
    return jax.nn.gelu(blk @ w1) @ w2


def importance_map(nc, nsb):
    cs = np.arange(nc) * CMP_STRIDE
    ce = cs + CMP_LEN
    bs = np.arange(nsb) * SEL_BLOCK
    be = bs + SEL_BLOCK
    ov = np.clip(np.minimum(ce[:, None], be[None, :]) - np.maximum(cs[:, None], bs[None, :]), 0, None)
    return jnp.asarray(ov / CMP_LEN, jnp.float32)


def nsa_mixer(q, k_cmp_raw, v_cmp_raw, k_slc, v_slc, k_win, v_win, gates,
              cmp_pos_k, cmp_w1_k, cmp_w2_k, cmp_pos_v, cmp_w1_v, cmp_w2_v):
    B, S, _ = q.shape
    G, HPG, hd, QB = NSA_KV_GROUPS, NSA_HPG, HEAD_DIM, NSA_QBLOCK
    pos = jnp.arange(S, dtype=jnp.float32)
    q = rope(q.reshape(B, S, NSA_HEADS, hd), pos).reshape(B, S, G, HPG, hd) * (hd ** -0.5)
    nc = (S - CMP_LEN) // CMP_STRIDE + 1
    cmp_end_np = np.arange(nc) * CMP_STRIDE + CMP_LEN - 1
    kc = compress_blocks(k_cmp_raw.reshape(B, S, G, hd), cmp_pos_k, cmp_w1_k, cmp_w2_k)
    kc = rope(kc, jnp.asarray(cmp_end_np, jnp.float32))
    vc = compress_blocks(v_cmp_raw.reshape(B, S, G, hd), cmp_pos_v, cmp_w1_v, cmp_w2_v)
    cmp_end = jnp.asarray(cmp_end_np, jnp.int32)
    nsb = S // SEL_BLOCK
    topk = min(SEL_TOPK, nsb)
    ks = rope(k_slc.reshape(B, S, G, hd), pos)
    kb = ks.reshape(B, nsb, SEL_BLOCK, G, hd).transpose(0, 3, 1, 2, 4)
    vb = v_slc.reshape(B, nsb, SEL_BLOCK, G, hd).transpose(0, 3, 1, 2, 4)
    imp_map = importance_map(nc, nsb)
    blk_id = jnp.arange(nsb)
    bi = jnp.arange(B)[:, None, None, None]
    gi = jnp.arange(G)[None, :, None, None]
    kw = jnp.pad(rope(k_win.reshape(B, S, G, hd), pos), ((0, 0), (WINDOW, 0), (0, 0), (0, 0)))
    vw = jnp.pad(v_win.reshape(B, S, G, hd), ((0, 0), (WINDOW, 0), (0, 0), (0, 0)))
    gates = jax.nn.sigmoid(gates).reshape(B, S, G, HPG, 3)

    def query_block(i):
        s0 = i * QB
        t = s0 + jnp.arange(QB)
        qb = lax.dynamic_slice_in_dim(q, s0, QB, axis=1)
        gb = lax.dynamic_slice_in_dim(gates, s0, QB, axis=1)
        s = jnp.einsum('bqghd,bcgd->bghqc', qb, kc)
        p_cmp = masked_softmax(s, (cmp_end[None, :] <= t[:, None])[None, None, None])
        o_cmp = jnp.einsum('bghqc,bcgd->bqghd', p_cmp.astype(vc.dtype), vc)
        imp = jnp.einsum('bghqc,cn->bgqn', p_cmp, imp_map)
        tb = t // SEL_BLOCK
        valid = blk_id[None, :] * SEL_BLOCK <= t[:, None]
        forced = (blk_id[None, :] == 0) | (blk_id[None, :] == tb[:, None]) | (blk_id[None, :] == tb[:, None] - 1)
        score = jnp.where(valid, imp + jnp.where(forced, FORCE_BONUS, 0.0), -jnp.inf)
        _, idx = lax.top_k(score, topk)
        ksel = kb[bi, gi, idx].reshape(B, G, QB, topk * SEL_BLOCK, hd)
        vsel = vb[bi, gi, idx].reshape(B, G, QB, topk * SEL_BLOCK, hd)
        tok = (idx[..., None] * SEL_BLOCK + jnp.arange(SEL_BLOCK)).reshape(B, G, 1, QB, topk * SEL_BLOCK)
        s = jnp.einsum('bqghd,bgqmd->bghqm', qb, ksel)
        p = masked_softmax(s, tok <= t[None, None, None, :, None])
        o_slc = jnp.einsum('bghqm,bgqmd->bqghd', p.astype(vsel.dtype), vsel)
        kwb = lax.dynamic_slice_in_dim(kw, s0, WINDOW + QB, axis=1)
        vwb = lax.dynamic_slice_in_dim(vw, s0, WINDOW + QB, axis=1)
        kpos = s0 - WINDOW + jnp.arange(WINDOW + QB)
        wmask = (kpos[None, :] <= t[:, None]) & (kpos[None, :] > t[:, None] - WINDOW) & (kpos[None, :] >= 0)
        s = jnp.einsum('bqghd,bkgd->bghqk', qb, kwb)
        p = masked_softmax(s, wmask[None, None, None])
        o_win = jnp.einsum('bghqk,bkgd->bqghd', p.astype(vwb.dtype), vwb)
        return gb[..., 0:1] * o_cmp + gb[..., 1:2] * o_slc + gb[..., 2:3] * o_win

    out = lax.map(query_block, jnp.arange(S // QB))
    return out.transpose(1, 0, 2, 3, 4, 5).reshape(B, S, NSA_W)


def gla_mixer(q, k, v, a_lr, r, w_up, b_up, norm_g):
    B, S, _ = q.shape
    H, DK, DV, C = GLA_HEADS, GLA_DK, GLA_DV, GLA_CHUNK
    f32 = jnp.float32
    q = q.reshape(B, S, H, DK).astype(f32) * (DK ** -0.5)
    k = k.reshape(B, S, H, DK).astype(f32)
    v = v.reshape(B, S, H, DV).astype(f32)
    g = jax.nn.log_sigmoid((a_lr @ w_up + b_up).astype(f32)).reshape(B, S, H, DK) / GLA_TAU
    n = S // C

    def to_chunks(a):
        return a.reshape(B, n, C, H, a.shape[-1]).transpose(1, 0, 3, 2, 4)

    causal = jnp.tril(jnp.ones((C, C), dtype=bool))

    def step(state, inp):
        qc, kc, vc, gc = inp
        bcum = jnp.cumsum(gc, axis=2)
        o_inter = jnp.einsum('bhck,bhkv->bhcv', qc * jnp.exp(bcum), state)
        diff = bcum[:, :, :, None, :] - bcum[:, :, None, :, :]
        decay = jnp.exp(jnp.where(causal[None, None, :, :, None], diff, -jnp.inf))
        attn = jnp.einsum('bhik,bhjk,bhijk->bhij', qc, kc, decay)
        o_intra = jnp.einsum('bhij,bhjv->bhiv', attn, vc)
        blast = bcum[:, :, -1:, :]
        state = jnp.exp(blast[:, :, 0, :])[..., None] * state + jnp.einsum(
            'bhjk,bhjv->bhkv', kc * jnp.exp(blast - bcum), vc)
        return state, o_inter + o_intra

    state0 = jnp.zeros((B, H, DK, DV), f32)
    _, o = lax.scan(step, state0, (to_chunks(q), to_chunks(k), to_chunks(v), to_chunks(g)))
    o = o.transpose(1, 0, 3, 2, 4).reshape(B, S, H, DV)
    o = rmsnorm(o, norm_g) * jax.nn.silu(r.astype(f32)).reshape(B, S, H, DV)
    return o.reshape(B, S, GLA_W).astype(r.dtype)


def spatial_gating(uv, ln_g, ln_b, w_s, b_s):
    B, S, _ = uv.shape
    nch = S // SG_CHUNK
    u, v = jnp.split(jax.nn.gelu(uv), 2, axis=-1)
    v = layernorm(v, ln_g, ln_b).reshape(B, nch, SG_CHUNK, SG_GROUPS, SG_CH)
    w = jnp.tril(w_s)
    s = jnp.einsum('gij,bnjgc->bnigc', w, v) + b_s.T[None, None, :, :, None]
    out = u.reshape(B, nch, SG_CHUNK, SG_GROUPS, SG_CH) * s
    return out.reshape(B, S, SG_W)


def split_columns(z):
    offs = np.cumsum(IN_SIZES)[:-1]
    return jnp.split(z, [int(o) for o in offs], axis=-1)


def setup_inputs(seed: int = 0) -> dict:
    key = jax.random.key(seed)
    ks = jax.random.split(key, 32)
    L = DEPTH
    res = (2.0 * DEPTH) ** -0.5

    def nrm(k, shape, scale):
        return jax.random.normal(k, shape, jnp.float32) * scale

    def gain(k, shape):
        return 1.0 + nrm(k, shape, 0.02)

    return {
        "x": nrm(ks[0], (BATCH, SEQ, D_MODEL), 1.0),
        "attn_norm": gain(ks[1], (L, D_MODEL)),
        "w_in": nrm(ks[2], (L, D_MODEL, D_IN), D_MODEL ** -0.5),
        "cmp_pos_k": nrm(ks[3], (L, CMP_LEN, HEAD_DIM), 0.1),
        "cmp_w1_k": nrm(ks[4], (L, CMP_LEN * HEAD_DIM, CMP_HIDDEN), (CMP_LEN * HEAD_DIM) ** -0.5),
        "cmp_w2_k": nrm(ks[5], (L, CMP_HIDDEN, HEAD_DIM), CMP_HIDDEN ** -0.5),
        "cmp_pos_v": nrm(ks[6], (L, CMP_LEN, HEAD_DIM), 0.1),
        "cmp_w1_v": nrm(ks[7], (L, CMP_LEN * HEAD_DIM, CMP_HIDDEN), (CMP_LEN * HEAD_DIM) ** -0.5),
        "cmp_w2_v": nrm(ks[8], (L, CMP_HIDDEN, HEAD_DIM), CMP_HIDDEN ** -0.5),
        "gla_w_up": nrm(ks[9], (L, GLA_RANK, GLA_HEADS * GLA_DK), GLA_RANK ** -0.5),
        "gla_b_up": nrm(ks[10], (L, GLA_HEADS * GLA_DK), 0.1),
        "gla_norm": gain(ks[11], (L, GLA_DV)),
        "sg_ln_g": gain(ks[12], (L, SG_W)),
        "sg_ln_b": nrm(ks[13], (L, SG_W), 0.02),
        "sg_w": nrm(ks[14], (L, SG_GROUPS, SG_CHUNK, SG_CHUNK), SG_CHUNK ** -0.5),
        "sg_b": 1.0 + nrm(ks[15], (L, SG_GROUPS, SG_CHUNK), 0.1),
        "w_out": nrm(ks[16], (L, D_MIX, D_MODEL), D_MIX ** -0.5 * res),
        "ffn_norm": gain(ks[17], (L, D_MODEL)),
        "w_gate": nrm(ks[18], (L, D_MODEL, D_FF), D_MODEL ** -0.5),
        "w_up": nrm(ks[19], (L, D_MODEL, D_FF), D_MODEL ** -0.5),
        "w_down": nrm(ks[20], (L, D_FF, D_MODEL), D_FF ** -0.5 * res),
        "final_norm": gain(ks[21], (D_MODEL,)),
    }


def reference(x, attn_norm, w_in, cmp_pos_k, cmp_w1_k, cmp_w2_k, cmp_pos_v, cmp_w1_v, cmp_w2_v,
              gla_w_up, gla_b_up, gla_norm, sg_ln_g, sg_ln_b, sg_w, sg_b, w_out,
              ffn_norm, w_gate, w_up, w_down, final_norm):
    for l in range(DEPTH):
        h = rmsnorm(x, attn_norm[l])
        (q, kc, vc, ksl, vsl, kwn, vwn, gts, gq, gk, gv, ga, gr, uv) = split_columns(h @ w_in[l])
        o_nsa = nsa_mixer(q, kc, vc, ksl, vsl, kwn, vwn, gts,
                          cmp_pos_k[l], cmp_w1_k[l], cmp_w2_k[l], cmp_pos_v[l], cmp_w1_v[l], cmp_w2_v[l])
        o_gla = gla_mixer(gq, gk, gv, ga, gr, gla_w_up[l], gla_b_up[l], gla_norm[l])
        o_sg = spatial_gating(uv, sg_ln_g[l], sg_ln_b[l], sg_w[l], sg_b[l])
        x = x + jnp.concatenate([o_nsa, o_gla.astype(x.dtype), o_sg], axis=-1) @ w_out[l]
        h = rmsnorm(x, ffn_norm[l])
        x = x + (jax.nn.silu(h @ w_gate[l]) * (h @ w_up[l])) @ w_down[l]
    return rmsnorm(x, final_norm)
```

```python
import numpy as np
import ml_dtypes
import concourse.bass as bass
import concourse.mybir as mybir
from concourse.bass_utils import run_bass_kernel_spmd

F32 = mybir.dt.float32
BF16 = mybir.dt.bfloat16
AF = mybir.ActivationFunctionType
ALU = mybir.AluOpType
AX = mybir.AxisListType

S = 4096
D = 1024
NT = S // 128
DIN = 2600
DFF = 2816
EPS = 1e-6
BIG = 30000.0
DEPTH = 4

_ISZ = {}


def isz(dt):
    k = str(dt)
    if k not in _ISZ:
        _ISZ[k] = mybir.dt.size(dt)
    return _ISZ[k]


class Prog:
    ENGS = ["pe", "act", "dve", "pool", "sp"]
    NDSEM = 8

    def __init__(self, nc):
        self.nc = nc
        self.instrs = {e: [] for e in self.ENGS}
        self.bpp = {}
        self.recs = {}
        self.clock = {e: {x: -1 for x in self.ENGS} for e in self.ENGS}
        self.iclock = {e: [] for e in self.ENGS}
        self.dma_cnt = {e: 0 for e in self.ENGS}
        self.dma_known = {e: set() for e in self.ENGS}
        self.marked = {e: set() for e in self.ENGS}
        self.ctx = []

    def sbuf(self, name, shape, dt):
        t = self.nc.sbuf_tensor(name, list(shape), dt)
        h = t.__enter__()
        self.ctx.append(t)
        n = 1
        for s in shape[1:]:
            n *= s
        self.bpp[name] = n * isz(dt)
        return h

    def psum(self, name, shape, dt):
        t = self.nc.psum_tensor(name, list(shape), dt)
        h = t.__enter__()
        self.ctx.append(t)
        n = 1
        for s in shape[1:]:
            n *= s
        self.bpp[name] = n * isz(dt)
        return h

    def dram(self, name, shape, dt, kind="Internal"):
        t = self.nc.dram_tensor(name, list(shape), dt, kind=kind)
        self.bpp[name] = None
        return t.ap()

    def region(self, a):
        name = a.tensor.name
        sz = isz(a.dtype)
        bpp = self.bpp.get(name, None)
        ap = a.ap
        if bpp is None:
            lo = a.offset
            ext = 0
            for st, cn in ap:
                ext += (cn - 1) * abs(st)
            return (name, 0, 1, lo * sz, (lo + ext + 1) * sz)
        off = a.offset * sz
        p0 = off // bpp
        lo = off % bpp
        if name.startswith("pb"):
            q0 = (p0 // 32) * 32
            q1 = ((p0 + ap[0][1] + 31) // 32) * 32
            return (name, q0, q1, 0, bpp)
        ext = 0
        for st, cn in ap[1:]:
            ext += (cn - 1) * abs(st)
        return (name, p0, p0 + ap[0][1], lo, lo + (ext + 1) * sz)

    def op(self, eng, fn, reads=(), writes=(), dma=False):
        import os
        lim = int(os.environ.get("KMAXOPS", "0"))
        self.nops = getattr(self, "nops", 0) + 1
        if lim and self.nops > lim:
            return None
        if os.environ.get("KLOG"):
            import sys as _s
            f = _s._getframe(1)
            while f is not None and f.f_code.co_name not in ("layer", "build_layer"):
                f = f.f_back
            print("OP", self.nops, eng, f.f_lineno if f else -1)
        idx = len(self.instrs[eng])
        waits = {}
        myclk = self.clock[eng]

        def need(tok):
            if tok[0] == "dma":
                if tok in self.dma_known[eng]:
                    return
                self.dma_known[eng].add(tok)
                waits[tok] = True
            else:
                e2, i2 = tok
                if myclk[e2] >= i2:
                    return
                waits[tok] = True

        if dma:
            q = eng
            di = self.dma_cnt[q]
            self.dma_cnt[q] += 1
            mytok = ("dma", q, di)
            if di >= self.NDSEM:
                need(("dma", q, di - self.NDSEM))
        else:
            mytok = (eng, idx)

        for (aps, kind) in ((reads, "R"), (writes, "W")):
            for a in aps:
                name, p0, p1, lo, hi = self.region(a)
                lst = self.recs.get(name, [])
                keep = []
                for r in lst:
                    (rp0, rp1, rlo, rhi, rkind, rtok) = r
                    ov = not (rp1 <= p0 or p1 <= rp0 or rhi <= lo or hi <= rlo)
                    rr = name.startswith("pb") and rtok[0] != eng
                    if ov and (rkind == "W" or kind == "W" or rr) and rtok != mytok:
                        if rtok[0] != "dma" and rtok[0] == eng and not dma:
                            if eng != "pe":
                                need(rtok)
                        else:
                            need(rtok)
                    cov = rp0 >= p0 and rp1 <= p1 and rlo >= lo and rhi <= hi
                    if kind == "W" and cov:
                        continue
                    if kind == "R" and rkind == "R" and cov and rtok[0] == mytok[0] and rtok[0] != "dma":
                        continue
                    keep.append(r)
                keep.append((p0, p1, lo, hi, kind, mytok))
                self.recs[name] = keep

        red = {}
        dmaw = []
        for tok in waits:
            if tok[0] == "dma":
                dmaw.append(tok)
            else:
                red[tok[0]] = max(red.get(tok[0], -1), tok[1])
        for e2, i2 in red.items():
            self.marked[e2].add(i2)
            oc = self.iclock[e2][i2]
            for x in self.ENGS:
                if oc[x] > myclk[x]:
                    myclk[x] = oc[x]
            if i2 > myclk[e2]:
                myclk[e2] = i2
        snap = dict(myclk)
        if not dma:
            snap[eng] = max(snap[eng], idx - 1)
        self.iclock[eng].append(snap)
        self.instrs[eng].append(dict(fn=fn, waits=red, dmaw=dmaw, dma=(mytok if dma else None)))
        return mytok

    def emit(self, final_waits=()):
        nc = self.nc
        ENGH = {"pe": "tensor", "act": "scalar", "dve": "vector", "pool": "gpsimd", "sp": "sync"}
        rank = {}
        for e in self.ENGS:
            s = sorted(self.marked[e])
            rank[e] = {i: k + 1 for k, i in enumerate(s)}
        sems = {}
        semctx = []
        for e in self.ENGS:
            c = nc.semaphore("s_" + e)
            sems[e] = c.__enter__()
            semctx.append(c)
        dsems = {}
        for q in self.ENGS:
            if self.dma_cnt[q] > 0:
                lst = []
                for k in range(self.NDSEM):
                    c = nc.semaphore("d_%s_%d" % (q, k))
                    lst.append(c.__enter__())
                    semctx.append(c)
                dsems[q] = lst
        blk = nc.Block()
        block = blk.__enter__()
        prog = self

        def mk(e):
            def body(eng):
                for idx, ins in enumerate(prog.instrs[e]):
                    for e2, i2 in ins["waits"].items():
                        eng.wait_ge(sems[e2], rank[e2][i2])
                    for (_, q, di) in ins["dmaw"]:
                        eng.wait_ge(dsems[q][di % prog.NDSEM], 16 * (di // prog.NDSEM + 1))
                    r = ins["fn"](eng)
                    if ins["dma"] is not None:
                        (_, q, di) = ins["dma"]
                        r.then_inc(dsems[q][di % prog.NDSEM], 16)
                    elif idx in rank[e]:
                        r.then_inc(sems[e], 1)
                if e == "sp":
                    for tok in final_waits:
                        if tok is None:
                            continue
                        (_, q, di) = tok
                        eng.wait_ge(dsems[q][di % prog.NDSEM], 16 * (di // prog.NDSEM + 1))
            return body

        for e in self.ENGS:
            if len(self.instrs[e]) == 0 and not (e == "sp" and final_waits):
                continue
            getattr(block, ENGH[e])(mk(e))
        blk.__exit__(None, None, None)
        for c in reversed(semctx):
            c.__exit__(None, None, None)
        for c in reversed(self.ctx):
            c.__exit__(None, None, None)


CB_ID = 0
CB_DIAG = 128
CB_WIN = 640
CB_BD = 1152
CB_ONES = 1280
CB_EEXP = 1408
CB_CMASK = 5504
CB_VCA = 13696
CB_HM = 13696 + 516
CB_BD256 = CB_HM + 64
NCB = CB_BD256 + 256
CF_ID = 0
CF_BDM = 128
CF_TRIL = 256
CF_RTAB = 384
CF_COS = 576
CF_SIN = 1600
CF_CCOS = 2624
CF_CSIN = 2688
CF_CM = 2752
NCF = 2816


def _consts():
    p = np.arange(128)[:, None]
    f = np.arange(128)[None, :]
    cb = np.zeros((128, NCB), np.float32)
    cb[:, CB_ID:CB_ID + 128] = (p == f)
    diag = np.where(p <= f, 0.0, -BIG)
    win = np.where(p > f, 0.0, -BIG)
    cb[:, CB_DIAG:CB_DIAG + 512] = np.tile(diag, (1, 4))
    cb[:, CB_WIN:CB_WIN + 512] = np.tile(win, (1, 4))
    bd = ((p // 64 == f // 64) & (p <= f)).astype(np.float32)
    cb[:, CB_BD:CB_BD + 128] = bd
    cb[:, CB_ONES:CB_ONES + 128] = 1.0
    m = np.arange(4096)[None, :]
    cb[:, CB_EEXP:CB_EEXP + 4096] = np.where((m // 64) == p, BIG, 0.0)
    for ch in range(2):
        c = ch * 128 + p
        cb[:, CB_CMASK + ch * 4096:CB_CMASK + (ch + 1) * 4096] = np.where(16 * c + 31 <= m, 0.0, -BIG)
    ncmp = 255
    cs = np.arange(ncmp) * 16
    ce = cs + 32
    bs = np.arange(64) * 64
    be = bs + 64
    ov = np.clip(np.minimum(ce[:, None], be[None, :]) - np.maximum(cs[:, None], bs[None, :]), 0, None) / 32.0
    vca = np.zeros((128, 2, 2, 129), np.float32)
    for ch in range(2):
        for pp in range(128):
            c = ch * 128 + pp
            if c < ncmp:
                vca[pp, ch, :, 64] = 1.0
                vca[pp, ch, :, 65:129] = ov[c][None, :]
    cb[:, CB_VCA:CB_VCA + 516] = vca.reshape(128, 516)
    for h in range(4):
        cb[:, CB_HM + h] = (np.arange(128) // 32 == h)
    cb[:, CB_BD256:CB_BD256 + 256] = ((np.arange(128)[:, None] // 32) == (np.arange(256)[None, :] // 64))

    cf = np.zeros((128, NCF), np.float32)
    cf[:, CF_ID:CF_ID + 128] = (p == f)
    cf[:, CF_BDM:CF_BDM + 128] = -bd / 16.0
    cf[:, CF_TRIL:CF_TRIL + 128] = (f <= p)
    j = np.arange(192)[None, :]
    mm = j - 62
    tbrel = (p >= 64).astype(np.int64)
    r = np.zeros((128, 192), np.float32)
    r[(mm == tbrel) | (mm == tbrel - 1)] = 1e4
    r[mm > tbrel] = -1e30
    cf[:, CF_RTAB:CF_RTAB + 192] = r
    half = 32
    inv = (1.0 / (np.float32(10000.0) ** (np.arange(half, dtype=np.float32) / np.float32(half)))).astype(np.float32)
    pos = (np.arange(32)[None, :] * 128 + np.arange(128)[:, None]).astype(np.float32)
    ang = (pos[:, :, None] * inv[None, None, :]).astype(np.float32)
    cf[:, CF_COS:CF_COS + 1024] = np.cos(ang).astype(np.float32).reshape(128, 1024)
    cf[:, CF_SIN:CF_SIN + 1024] = np.sin(ang).astype(np.float32).reshape(128, 1024)
    cpos = ((np.arange(2)[None, :] * 128 + np.arange(128)[:, None]) * 16 + 31).astype(np.float32)
    cang = (cpos[:, :, None] * inv[None, None, :]).astype(np.float32)
    cf[:, CF_CCOS:CF_CCOS + 64] = np.cos(cang).astype(np.float32).reshape(128, 64)
    cf[:, CF_CSIN:CF_CSIN + 64] = np.sin(cang).astype(np.float32).reshape(128, 64)
    for c in range(2):
        cf[:, CF_CM + c] = (np.arange(128) // 64 == c)
    return cb.astype(ml_dtypes.bfloat16), cf


WNAMES = [("attn_norm", [D]), ("w_in", [D, DIN]), ("cmp_pos_k", [32, 64]), ("cmp_w1_k", [2048, 256]),
          ("cmp_w2_k", [256, 64]), ("cmp_pos_v", [32, 64]), ("cmp_w1_v", [2048, 256]), ("cmp_w2_v", [256, 64]),
          ("gla_w_up", [16, 128]), ("gla_b_up", [128]), ("gla_norm", [64]), ("sg_ln_g", [256]), ("sg_ln_b", [256]),
          ("sg_w", [4, 128, 128]), ("sg_b", [4, 128]), ("w_out", [D, D]), ("ffn_norm", [D]),
          ("w_gate", [D, DFF]), ("w_up", [D, DFF]), ("w_down", [DFF, D])]


def build_layer(final=False, dbg=False, stop_after="C"):
    nc = bass.Bass("TRN2", target_bir_lowering=False)
    P = Prog(nc)
    x_in = P.dram("x", [S, D], F32, kind="ExternalInput")
    cb_d = P.dram("cb", [128, NCB], BF16, kind="ExternalInput")
    cf_d = P.dram("cf", [128, NCF], F32, kind="ExternalInput")
    Wd = {n: P.dram(n, s, F32, kind="ExternalInput") for n, s in WNAMES}
    fnorm_d = P.dram("final_norm", [D], F32, kind="ExternalInput")
    y_out = P.dram("y", [S, D], F32, kind="ExternalOutput")
    sk = "ExternalOutput" if dbg else "Internal"
    qTs = P.dram("qTs", [NT, 128, 512], BF16, kind=sk)
    mix = P.dram("mixs", [S, D], BF16, kind=sk)
    if dbg:
        d_kcT = P.dram("d_kcT", [128, 256], BF16, kind="ExternalOutput")
        d_vca = P.dram("d_vca", [128, 516], BF16, kind="ExternalOutput")
        d_kslT = P.dram("d_kslT", [128, 4096], BF16, kind="ExternalOutput")
        d_vsl = P.dram("d_vsl", [128, 32 * 2 * 65], BF16, kind="ExternalOutput")
        d_gates = P.dram("d_gates", [128, 32 * 24], F32, kind="ExternalOutput")
    dtoks = []

    ARENA = 104448
    arena = P.sbuf("arena", [128, ARENA], BF16)
    psb = [P.psum("pb%d" % i, [128, 512], F32) for i in range(8)]

    class Carver:
        def __init__(self, base, limit):
            self.off = base
            self.limit = limit

        def take(self, nbytes):
            nbytes = (nbytes + 63) // 64 * 64
            o = self.off
            self.off += nbytes
            assert self.off <= self.limit, (self.off, self.limit)
            return o

    def view(off_bytes, shape, dt, parts=128):
        n = 1
        for s in shape:
            n *= s
        if dt == BF16:
            a = arena[0:parts, off_bytes // 2: off_bytes // 2 + n]
        else:
            a = arena[0:parts, off_bytes // 2: off_bytes // 2 + n * 2].bitcast(F32)
        if len(shape) == 1:
            return a
        names = " ".join("d%d" % i for i in range(len(shape)))
        kw = {"d%d" % i: shape[i] for i in range(len(shape))}
        return a.rearrange("p (%s) -> p %s" % (names, names), **kw)

    def alloc(cv, shape, dt, parts=128):
        n = 1
        for s in shape:
            n *= s
        return view(cv.take(n * isz(dt)), shape, dt, parts)

    TOT = ARENA * 2
    cvK = Carver(0, 8 * 1024)
    identb = alloc(cvK, [128], BF16)
    identf = alloc(cvK, [128], F32)
    diagm = alloc(cvK, [512], BF16)
    winm = alloc(cvK, [512], BF16)
    bdmask = alloc(cvK, [128], BF16)
    onesb = alloc(cvK, [128], BF16)
    bdmf = alloc(cvK, [128], F32)
    trilf = alloc(cvK, [128], F32)
    rtab = alloc(cvK, [192], F32)
    ptT = alloc(cvK, [128], F32)
    gng = alloc(cvK, [64], F32)
    bupb = alloc(cvK, [128], BF16)
    wupb = alloc(cvK, [128], BF16)
    hm4 = alloc(cvK, [4], BF16)
    bd256 = alloc(cvK, [256], BF16)
    cm2 = alloc(cvK, [2], F32)
    KEND = cvK.off

    def dma(q, out, in_):
        return P.op(q, lambda e: e.dma_start(out=out, in_=in_), reads=[in_], writes=[out], dma=True)

    def act(out, in_, func, scale=None, bias=None, accum=None):
        kw = {}
        rd = [in_]
        wr = [out]
        if scale is not None:
            kw["scale"] = scale
            if not isinstance(scale, (int, float)):
                rd.append(scale)
        if bias is not None:
            kw["bias"] = bias
            if not isinstance(bias, (int, float)):
                rd.append(bias)
        if accum is not None:
            kw["accum_out"] = accum
            wr.append(accum)
        return P.op("act", lambda e: e.activation(out=out, in_=in_, func=func, **kw), reads=rd, writes=wr)

    def mm(out, lhsT, rhs, start, stop, tp=None, sgc=False):
        kw = {}
        if tp is not None:
            kw["tile_position"] = tp
        if sgc:
            kw["skip_group_check"] = True
        return P.op("pe", lambda e: e.matmul(out, lhsT=lhsT, rhs=rhs, start=start, stop=stop, **kw),
                    reads=[lhsT, rhs], writes=[out])

    def trp(out, in_, ident):
        return P.op("pe", lambda e: e.transpose(out, in_, ident), reads=[in_, ident], writes=[out])

    def tt(eng, out, in0, in1, op):
        return P.op(eng, lambda e: e.tensor_tensor(out=out, in0=in0, in1=in1, op=op), reads=[in0, in1], writes=[out])

    def ts(eng, out, in0, s1, s2, op0, op1=None):
        rd = [in0]
        if not isinstance(s1, (int, float)):
            rd.append(s1)
        if s2 is not None and not isinstance(s2, (int, float)):
            rd.append(s2)
        if op1 is None:
            return P.op(eng, lambda e: e.tensor_scalar(out=out, in0=in0, scalar1=s1, scalar2=None, op0=op0),
                        reads=rd, writes=[out])
        return P.op(eng, lambda e: e.tensor_scalar(out=out, in0=in0, scalar1=s1, scalar2=s2, op0=op0, op1=op1),
                    reads=rd, writes=[out])

    def stt(out, in0, sc, in1, op0, op1):
        rd = [in0, in1]
        if not isinstance(sc, (int, float)):
            rd.append(sc)
        return P.op("dve", lambda e: e.scalar_tensor_tensor(out=out, in0=in0, scalar=sc, in1=in1, op0=op0, op1=op1),
                    reads=rd, writes=[out])

    def copy(eng, out, in_):
        if eng == "act":
            return act(out, in_, AF.Copy)
        return P.op(eng, lambda e: e.tensor_copy(out=out, in_=in_), reads=[in_], writes=[out])

    def memset(eng, out, val):
        return P.op(eng, lambda e: e.memset(out, val), reads=[], writes=[out])

    def recip(out, in_):
        return P.op("dve", lambda e: e.reciprocal(out=out, in_=in_), reads=[in_], writes=[out])

    def bank_bf(i):
        return psb[i][:].bitcast(BF16)

    dma("sp", identb, cb_d[:, CB_ID:CB_ID + 128])
    dma("sp", diagm, cb_d[:, CB_DIAG:CB_DIAG + 512])
    dma("sp", winm, cb_d[:, CB_WIN:CB_WIN + 512])
    dma("sp", bdmask, cb_d[:, CB_BD:CB_BD + 128])
    dma("sp", onesb, cb_d[:, CB_ONES:CB_ONES + 128])
    dma("sp", identf, cf_d[:, CF_ID:CF_ID + 128])
    dma("sp", bdmf, cf_d[:, CF_BDM:CF_BDM + 128])
    dma("sp", trilf, cf_d[:, CF_TRIL:CF_TRIL + 128])
    dma("sp", rtab, cf_d[:, CF_RTAB:CF_RTAB + 192])
    dma("sp", hm4, cb_d[:, CB_HM:CB_HM + 4])
    dma("sp", bd256, cb_d[:, CB_BD256:CB_BD256 + 256])
    dma("sp", cm2, cf_d[:, CF_CM:CF_CM + 2])

    x_src = x_in
    fins = []

    def layer(x_src, x_dst, is_final):
        cv = Carver(KEND, TOT)
        cmask = alloc(cv, [2, 4096], BF16)
        eexp = alloc(cv, [4096], BF16)
        cosT = alloc(cv, [32, 32], F32)
        sinT = alloc(cv, [32, 32], F32)
        ccos = alloc(cv, [2, 32], F32)
        csin = alloc(cv, [2, 32], F32)
        gates = alloc(cv, [32, 24], F32)
        kslT = alloc(cv, [4096], BF16)
        kwnT = alloc(cv, [4096], BF16)
        vsl = alloc(cv, [32, 2, 65], BF16)
        vwn = alloc(cv, [32, 2, 65], BF16)
        kcrT = alloc(cv, [4096], BF16)
        vcrT = alloc(cv, [4096], BF16)
        kcT = alloc(cv, [256], BF16)
        vca = alloc(cv, [2, 2, 129], BF16)
        lng = alloc(cv, [256], F32)
        lnb = alloc(cv, [256], F32)
        sgWT = alloc(cv, [4, 128], BF16)
        st_f = alloc(cv, [256], F32)
        st_b = alloc(cv, [2, 256], BF16)
        w_off = cv.off
        win_b = alloc(cv, [8, DIN], BF16)
        wk_off = cv.off

        dma("sp", cmask, cb_d[:, CB_CMASK:CB_CMASK + 8192].rearrange("p (a b) -> p a b", a=2))
        dma("sp", eexp, cb_d[:, CB_EEXP:CB_EEXP + 4096])
        dma("sp", cosT, cf_d[:, CF_COS:CF_COS + 1024].rearrange("p (a b) -> p a b", a=32))
        dma("sp", sinT, cf_d[:, CF_SIN:CF_SIN + 1024].rearrange("p (a b) -> p a b", a=32))
        dma("sp", ccos, cf_d[:, CF_CCOS:CF_CCOS + 64].rearrange("p (a b) -> p a b", a=2))
        dma("sp", csin, cf_d[:, CF_CSIN:CF_CSIN + 64].rearrange("p (a b) -> p a b", a=2))
        dma("sp", vca, cb_d[:, CB_VCA:CB_VCA + 516].rearrange("p (a b c) -> p a b c", a=2, b=2))
        dma("sp", gng, Wd["gla_norm"].partition_broadcast(128))
        dma("sp", lng, Wd["sg_ln_g"].partition_broadcast(128))
        dma("sp", lnb, Wd["sg_ln_b"].partition_broadcast(128))

        cw = Carver(wk_off, TOT)
        stg = [alloc(cw, [DFF], F32), alloc(cw, [DFF], F32)]
        stg_i = [0]

        def stage():
            s = stg[stg_i[0] % 2]
            stg_i[0] += 1
            return s

        pt = stage()[:, 0:128]
        memset("pool", pt, 0.0)
        dma("sp", pt[0:8, :], Wd["attn_norm"].rearrange("(k p) -> k p", p=128))
        dma("sp", pt[8:16, :], Wd["ffn_norm"].rearrange("(k p) -> k p", p=128))
        dma("sp", pt[16:20, :], Wd["sg_b"])
        dma("sp", pt[32:64, 0:64], Wd["cmp_pos_k"])
        dma("sp", pt[64:96, 0:64], Wd["cmp_pos_v"])
        trp(psb[7][:, 0:128], pt, identf)
        copy("dve", ptT, psb[7][:, 0:128])
        s2 = stage()
        dma("sp", s2[0:16, 0:128], Wd["gla_w_up"])
        dma("sp", s2[0:1, 128:256], Wd["gla_b_up"].rearrange("(o n) -> o n", o=1))
        copy("pool", wupb[0:16, :], s2[0:16, 0:128])
        copy("pool", bupb[0:1, :], s2[0:1, 128:256])
        s3 = stage()
        s3v = s3[:, 0:512].rearrange("p (g j) -> p g j", g=4)
        dma("sp", s3v, Wd["sg_w"].rearrange("g i j -> i g j"))
        sgm = alloc(cw, [4, 128], BF16)
        tt("dve", sgm, s3v, trilf.unsqueeze(1).broadcast_to([128, 4, 128]), ALU.mult)
        b7b = bank_bf(7)
        for g in range(4):
            trp(b7b[:, 256 + g * 128: 256 + (g + 1) * 128], sgm[:, g, :], identb)
        copy("dve", sgWT, b7b[:, 256:768].rearrange("p (g i) -> p g i", g=4))
        for k in range(8):
            s = stage()
            dma("sp", s[:, 0:DIN], Wd["w_in"][k * 128:(k + 1) * 128, :])
            ts("pool", win_b[:, k, :], s[:, 0:DIN], ptT[:, k:k + 1], 1.0, ALU.mult, ALU.mult)
        memset("pool", vsl, 1.0)
        memset("pool", vwn, 1.0)
        memset("pool", st_f, 0.0)
        memset("pool", st_b, 0.0)
        if NT != 32:
            memset("pool", kcrT, 0.0)
            memset("pool", vcrT, 0.0)

        xbuf = [alloc(cw, [D], F32), alloc(cw, [D], F32)]
        hbf = alloc(cw, [D], BF16)
        junk = hbf
        hT = [alloc(cw, [8, 128], BF16), alloc(cw, [8, 128], BF16)]
        ssq = alloc(cw, [4], F32)
        ri = alloc(cw, [12, 64], F32)
        ro = alloc(cw, [768], BF16)
        rt1 = alloc(cw, [12, 32], F32)
        rt2 = alloc(cw, [12, 32], F32)
        qTst = [alloc(cw, [512], BF16), alloc(cw, [512], BF16)]
        rawb = alloc(cw, [256], BF16)
        alT = alloc(cw, [128], BF16)
        e1 = alloc(cw, [128], F32)
        spl = alloc(cw, [128], F32)
        epn = alloc(cw, [2, 128], F32)
        bcs = alloc(cw, [128], F32)
        qkg = alloc(cw, [2, 128], BF16)
        kgT = alloc(cw, [128], BF16)
        qTm = alloc(cw, [4, 128], BF16)
        qTc = alloc(cw, [2, 128], BF16)
        kgc = alloc(cw, [2, 128], BF16)
        tS2 = alloc(cw, [256], F32)
        memset("pool", qTc, 0.0)
        vg = alloc(cw, [256], BF16)
        attm = alloc(cw, [4, 128], BF16)
        ebc = alloc(cw, [2], F32)
        tS = alloc(cw, [256], F32)
        og = alloc(cw, [256], F32)
        og2 = alloc(cw, [256], F32)
        gms = alloc(cw, [8], F32)
        sir = alloc(cw, [256], F32)
        mixt = [alloc(cw, [512], BF16), alloc(cw, [512], BF16)]
        guv = alloc(cw, [512], F32)
        bns = alloc(cw, [8], F32)
        bna = alloc(cw, [4], F32)
        vn = alloc(cw, [256], F32)
        vnb = alloc(cw, [256], BF16)

        for t in range(NT):
            xs = xbuf[t % 2]
            hTt = hT[t % 2]
            dma("sp", xs, x_src[t * 128:(t + 1) * 128, :])
            act(junk, xs, AF.Square, accum=ssq[:, 0:1])
            act(ssq[:, 1:2], ssq[:, 0:1], AF.Sqrt, scale=1.0 / D, bias=EPS)
            recip(ssq[:, 2:3], ssq[:, 1:2])
            ts("dve", hbf, xs, ssq[:, 2:3], None, ALU.mult)
            b0 = bank_bf(0)
            for k in range(8):
                trp(b0[:, k * 128:(k + 1) * 128], hbf[:, k * 128:(k + 1) * 128], identb)
            copy("act", hTt, b0.rearrange("p (k t) -> p k t", k=8))
            chunks = [(0, 512, 1), (512, 1024, 2), (1024, 1432, 3), (1432, 1832, 4), (1832, 2088, 5), (2088, 2600, 6)]
            for (c0, c1, bk) in chunks:
                for k in range(8):
                    mm(psb[bk][:, 0:c1 - c0], hTt[:, k, :], win_b[:, k, c0:c1], k == 0, k == 7)
            for k in range(8):
                mm(psb[7][0:16, 256:384], win_b[:, k, 1816:1832], hTt[:, k, :], k == 0, k == 7)
            riv = ri.rearrange("p (w r) d -> p w r d", w=2)
            act(riv[:, :, 0:4, :], psb[1][:, 0:512].rearrange("p (w r d) -> p w r d", w=2, r=4), AF.Copy, scale=0.125)
            copy("dve", riv[:, :, 4, :], psb[2][:, 256:384].rearrange("p (w d) -> p w d", w=2))
            copy("dve", riv[:, :, 5, :], psb[3][:, 0:128].rearrange("p (w d) -> p w d", w=2))
            copy("act", rawb, psb[2][:, 0:256])
            copy("dve", vsl[:, t, :, 0:64], psb[2][:, 384:512].rearrange("p (w d) -> p w d", w=2))
            copy("dve", vwn[:, t, :, 0:64], psb[3][:, 128:256].rearrange("p (w d) -> p w d", w=2))
            act(gates[:, t, :], psb[3][:, 256:280], AF.Sigmoid)
            act(sir, psb[5][:, 0:256], AF.Silu)
            r4 = ri.rearrange("p (w r) (two d) -> p w r two d", w=2, two=2)
            x1 = r4[:, :, :, 0, :]
            x2 = r4[:, :, :, 1, :]
            cs_ = cosT[:, t, :].unsqueeze(1).unsqueeze(1).broadcast_to([128, 2, 6, 32])
            sn_ = sinT[:, t, :].unsqueeze(1).unsqueeze(1).broadcast_to([128, 2, 6, 32])
            t1 = rt1.rearrange("p (w r) d -> p w r d", w=2)
            t2 = rt2.rearrange("p (w r) d -> p w r d", w=2)
            rov = ro.rearrange("p (r w two d) -> p w r two d", r=6, w=2, two=2)
            tt("dve", t1, x1, cs_, ALU.mult)
            tt("dve", t2, x2, sn_, ALU.mult)
            tt("dve", rov[:, :, :, 0, :], t1, t2, ALU.subtract)
            tt("dve", t1, x2, cs_, ALU.mult)
            tt("dve", t2, x1, sn_, ALU.mult)
            tt("dve", rov[:, :, :, 1, :], t1, t2, ALU.add)
            b1 = bank_bf(1)
            for r in range(6):
                trp(b1[:, r * 128:(r + 1) * 128], ro[:, r * 128:(r + 1) * 128], identb)
            qst = qTst[t % 2]
            copy("act", qst, b1[:, 0:512])
            dma("pool", qTs[t], qst)
            copy("dve", kslT[:, t * 128:(t + 1) * 128], b1[:, 512:640])
            copy("dve", kwnT[:, t * 128:(t + 1) * 128], b1[:, 640:768])
            b2 = bank_bf(2)
            trp(b2[:, 0:128], rawb[:, 0:128], identb)
            trp(b2[:, 128:256], rawb[:, 128:256], identb)
            copy("act", kcrT[:, t * 128:(t + 1) * 128], b2[:, 0:128])
            copy("act", vcrT[:, t * 128:(t + 1) * 128], b2[:, 128:256])
            copy("dve", alT[0:16, :], psb[7][0:16, 256:384])
            mm(psb[7][:, 0:128], alT[0:16, :], wupb[0:16, :], True, False)
            mm(psb[7][:, 0:128], onesb[0:1, :], bupb[0:1, :], False, True)
            act(e1, psb[7][:, 0:128], AF.Exp, scale=-1.0)
            act(spl, e1, AF.Ln, bias=1.0)
            mm(psb[7][:, 128:256], bdmf, spl, True, True)
            act(epn[:, 0, :], psb[7][:, 128:256], AF.Exp)
            act(epn[:, 1, :], psb[7][:, 128:256], AF.Exp, scale=-1.0)
            copy("dve", bcs, psb[7][:, 128:256])
            stt(qkg[:, 0, :], psb[3][:, 280:408], 32.0 ** -0.5, epn[:, 0, :], ALU.mult, ALU.mult)
            tt("dve", qkg[:, 1, :], psb[4][:, 0:128], epn[:, 1, :], ALU.mult)
            copy("act", vg, psb[4][:, 128:384])
            trp(b2[:, 256:384], qkg[:, 0, :], identb)
            trp(b2[:, 384:512], qkg[:, 1, :], identb)
            copy("act", kgT, b2[:, 384:512])
            tt("dve", qTm, b2[:, 256:384].unsqueeze(1).broadcast_to([128, 4, 128]),
               hm4.unsqueeze(2).broadcast_to([128, 4, 128]), ALU.mult)
            copy("dve", qTc[:, 0, 0:64], b2[:, 256:320])
            copy("dve", qTc[:, 1, 64:128], b2[:, 320:384])
            tt("dve", kgc, qkg[:, 1, :].unsqueeze(1).broadcast_to([128, 2, 128]),
               cm2.unsqueeze(2).broadcast_to([128, 2, 128]), ALU.mult)
            trp(psb[7][:, 384:512], bcs, identf)
            act(ebc[:, 0:1], psb[7][:, 447:448], AF.Exp)
            act(ebc[:, 1:2], psb[7][:, 511:512], AF.Exp)
            mm(psb[3][:, 0:512], kgT, qTm.rearrange("p h i -> p (h i)"), True, True)
            tt("dve", attm, psb[3][:, 0:512].rearrange("p (h i) -> p h i", h=4),
               bdmask.unsqueeze(1).broadcast_to([128, 4, 128]), ALU.mult)
            first = True
            for h in range(4):
                mm(psb[4][:, h * 64:(h + 1) * 64], attm[:, h, :], vg[:, h * 64:(h + 1) * 64], first, False, sgc=True)
                first = False
            mm(psb[4][:, 0:256], qTc[:, 0, :], st_b[:, 0, :], False, False, sgc=True)
            mm(psb[5][:, 256:512], kgc[:, 0, :], vg, True, True)
            tt("dve", tS, psb[5][:, 256:512], bd256, ALU.mult)
            tt("dve", tS2, tS, st_f, ALU.add)
            ts("dve", st_f, tS2, ebc[:, 0:1], None, ALU.mult)
            copy("dve", st_b[:, 1, :], st_f)
            mm(psb[4][:, 0:256], qTc[:, 1, :], st_b[:, 1, :], False, True, sgc=True)
            mm(psb[5][:, 256:512], kgc[:, 1, :], vg, True, True)
            tt("dve", tS, psb[5][:, 256:512], bd256, ALU.mult)
            tt("dve", tS2, tS, st_f, ALU.add)
            ts("dve", st_f, tS2, ebc[:, 1:2], None, ALU.mult)
            copy("dve", st_b[:, 0, :], st_f)
            copy("act", og, psb[4][:, 0:256])
            tt("dve", og2, og, og, ALU.mult)
            P.op("dve", lambda e: e.tensor_reduce(out=gms[:, 0:4], in_=og2.rearrange("p (h d) -> p h d", h=4),
                                                  axis=AX.X, op=ALU.add),
                 reads=[og2], writes=[gms[:, 0:4]])
            act(gms[:, 4:8], gms[:, 0:4], AF.Sqrt, scale=1.0 / 64, bias=EPS)
            recip(gms[:, 0:4], gms[:, 4:8])
            tt("dve", og2.rearrange("p (h d) -> p h d", h=4), og.rearrange("p (h d) -> p h d", h=4),
               gms[:, 0:4].unsqueeze(2).broadcast_to([128, 4, 64]), ALU.mult)
            tt("dve", og.rearrange("p (h d) -> p h d", h=4), og2.rearrange("p (h d) -> p h d", h=4),
               gng.unsqueeze(1).broadcast_to([128, 4, 64]), ALU.mult)
            mxt = mixt[t % 2]
            tt("dve", mxt[:, 0:256], og, sir, ALU.mult)
            act(guv, psb[6][:, 0:512], AF.Gelu_apprx_tanh)
            P.op("dve", lambda e: e.bn_stats(out=bns[:, 0:6], in_=guv[:, 256:512]), reads=[guv[:, 256:512]], writes=[bns[:, 0:6]])
            P.op("dve", lambda e: e.bn_aggr(out=bna[:, 0:2], in_=bns[:, 0:6]), reads=[bns[:, 0:6]], writes=[bna[:, 0:2]])
            act(bna[:, 2:3], bna[:, 1:2], AF.Sqrt, bias=EPS)
            recip(bna[:, 3:4], bna[:, 2:3])
            ts("dve", vn, guv[:, 256:512], bna[:, 0:1], bna[:, 3:4], ALU.subtract, ALU.mult)
            tt("dve", vn, vn, lng, ALU.mult)
            tt("dve", vnb, vn, lnb, ALU.add)
            for g in range(4):
                mm(psb[6][:, g * 64:(g + 1) * 64], sgWT[:, g, :], vnb[:, g * 64:(g + 1) * 64], True, True)
            for g in range(4):
                stt(mxt[:, 256 + g * 64: 256 + (g + 1) * 64], psb[6][:, g * 64:(g + 1) * 64], ptT[:, 16 + g:17 + g],
                    guv[:, g * 64:(g + 1) * 64], ALU.add, ALU.mult)
            dtoks.append(dma("pool", mix[t * 128:(t + 1) * 128, 512:1024], mxt))
        if dbg and NT == 32:
            dtoks.append(dma("pool", d_kslT[:, :], kslT))
            dtoks.append(dma("pool", d_vsl[:, :], vsl.rearrange("p a b c -> p (a b c)")))
            dtoks.append(dma("pool", d_gates[:, :], gates.rearrange("p a b -> p (a b)")))
        if stop_after == "A":
            return dtoks

        cc = Carver(w_off, TOT)
        w1d = [alloc(cc, [32, 256], BF16), alloc(cc, [32, 256], BF16)]
        w2b = alloc(cc, [2, 2, 64], BF16)
        posTb = alloc(cc, [64], BF16, parts=64)
        hbias = alloc(cc, [2, 2], F32)
        hidT = alloc(cc, [2, 255], BF16)
        kct = alloc(cc, [2, 128], F32)
        kcr = alloc(cc, [2, 128], BF16)
        cst = [alloc(cc, [8, 256], F32), alloc(cc, [8, 256], F32)]
        ci = 0
        for kv, nm in enumerate(["cmp_w1_k", "cmp_w1_v"]):
            src = Wd[nm].rearrange("(l d) h -> d l h", d=64)
            for part in range(2):
                for l0 in range(0, 32, 8):
                    s = cst[ci % 2]
                    ci += 1
                    dma("sp", s[64 * part:64 * part + 64], src[:, l0:l0 + 8, :])
                    copy("pool", w1d[kv][64 * part:64 * part + 64, l0:l0 + 8, :], s[64 * part:64 * part + 64])
        for kv, nm in enumerate(["cmp_w2_k", "cmp_w2_v"]):
            s = cst[ci % 2]
            ci += 1
            sv = s[:, 0, 0:128].rearrange("p (c d) -> p c d", c=2)
            dma("sp", sv, Wd[nm].rearrange("(c p) d -> p c d", p=128))
            copy("pool", w2b[:, kv, :, :], sv)
        copy("dve", posTb[0:64, :], ptT[0:64, 32:96])
        for kv in range(2):
            rawT = kcrT if kv == 0 else vcrT
            for c in range(2):
                for l in range(32):
                    mm(psb[7][:, 2 * kv + c: 2 * kv + c + 1], w1d[kv][0:64, l, c * 128:(c + 1) * 128],
                       posTb[0:64, kv * 32 + l: kv * 32 + l + 1], l == 0, l == 31)
            copy("dve", hbias[:, kv, :], psb[7][:, 2 * kv: 2 * kv + 2])
            for g in range(2):
                for c in range(2):
                    for l in range(32):
                        rv = rawT[64 * g:64 * g + 64, :].rearrange("p (c s) -> p s c", s=16)
                        rhs = rv[:, l, 0:255] if l < 16 else rv[:, l - 16, 1:256]
                        mm(psb[c + 2 * g][:, 0:255], w1d[kv][64 * g:64 * g + 64, l, c * 128:(c + 1) * 128], rhs, l == 0, l == 31)
                    act(hidT[:, c, :], psb[c + 2 * g][:, 0:255], AF.Gelu_apprx_tanh, bias=hbias[:, kv, c:c + 1])
                for (c0, cn, ch) in [(0, 128, 0), (128, 127, 1)]:
                    for c in range(2):
                        mm(psb[4 + ch][0:cn, g * 64:(g + 1) * 64], hidT[:, c, c0:c0 + cn], w2b[:, kv, c, :], c == 0, c == 1)
                    if kv == 0:
                        copy("dve", kct[0:cn, ch, g * 64:(g + 1) * 64], psb[4 + ch][0:cn, g * 64:(g + 1) * 64])
                    else:
                        copy("dve", vca[0:cn, ch, g, 0:64], psb[4 + ch][0:cn, g * 64:(g + 1) * 64])
        k5 = kct.rearrange("p c (g two d) -> p c g two d", g=2, two=2)
        o5 = kcr.rearrange("p c (g two d) -> p c g two d", g=2, two=2)
        ta = alloc(cc, [2, 2, 32], F32)
        tb = alloc(cc, [2, 2, 32], F32)
        for (cn, ch) in [(128, 0), (127, 1)]:
            a1 = k5[0:cn, ch, :, 0, :]
            a2 = k5[0:cn, ch, :, 1, :]
            cb_ = ccos[0:cn, ch, :].unsqueeze(1).broadcast_to([cn, 2, 32])
            sb_ = csin[0:cn, ch, :].unsqueeze(1).broadcast_to([cn, 2, 32])
            tt("dve", ta[0:cn, ch], a1, cb_, ALU.mult)
            tt("dve", tb[0:cn, ch], a2, sb_, ALU.mult)
            tt("dve", o5[0:cn, ch, :, 0, :], ta[0:cn, ch], tb[0:cn, ch], ALU.subtract)
            tt("dve", ta[0:cn, ch], a2, cb_, ALU.mult)
            tt("dve", tb[0:cn, ch], a1, sb_, ALU.mult)
            tt("dve", o5[0:cn, ch, :, 1, :], ta[0:cn, ch], tb[0:cn, ch], ALU.add)
            b6_ = bank_bf(6)
            trp(b6_[:, ch * 128: ch * 128 + cn], kcr[0:cn, ch, :], identb[0:cn, 0:cn])
            copy("dve", kcT[:, ch * 128: ch * 128 + cn], b6_[:, ch * 128: ch * 128 + cn])

        if dbg:
            dtoks.append(dma("pool", d_kcT[:, :], kcT))
            dtoks.append(dma("pool", d_vca[:, :], vca.rearrange("p a b c -> p (a b c)")))
        if stop_after == "K":
            return dtoks
        cb2 = Carver(cc.off, TOT)
        qt = [alloc(cb2, [2, 512], BF16), alloc(cb2, [2, 512], BF16)]
        memset("pool", qt[0], 0.0)
        memset("pool", qt[1], 0.0)
        ET = [alloc(cb2, [512], BF16) for _ in range(3)]
        et_i = [0]
        sc = alloc(cb2, [64], F32)
        sc2 = alloc(cb2, [64], F32)
        impa = alloc(cb2, [64], F32)
        m8 = alloc(cb2, [16], F32)
        mb = alloc(cb2, [128], BF16)
        memset("pool", mb, 0.0)
        mbT = alloc(cb2, [128], BF16)
        dn = alloc(cb2, [3, 4], F32)
        sg_ = alloc(cb2, [3, 4], F32)
        oacc = [alloc(cb2, [512], F32), alloc(cb2, [512], F32)]
        otmp = alloc(cb2, [256], F32)
        onb = [alloc(cb2, [512], BF16), alloc(cb2, [512], BF16)]
        sbank = [0]

        def score_bank():
            b = sbank[0] % 2
            sbank[0] += 1
            return psb[b]

        def next_et():
            e = ET[et_i[0] % 3]
            et_i[0] += 1
            return e

        for t in range(NT):
            q_t = qt[t % 2]
            dma("sp", q_t[0:64, 0, :], qTs[t, 0:64, :])
            dma("sp", q_t[64:128, 1, :], qTs[t, 64:128, :])
            oa = oacc[t % 2]
            for w in range(2):
                qw = q_t[:, w, :]
                gsl = gates[:, t, :].rearrange("p (h b) -> p h b", b=3)[:, 4 * w:4 * w + 4, :]
                ncv = min(8 * t + 7, 255)
                chs = [(0, 128, 0)] + ([(128, 127, 1)] if ncv > 128 else [])
                ets = []
                for (c0, cn, ch) in chs:
                    pb = score_bank()
                    mm(pb[0:cn, :], kcT[:, c0:c0 + cn], qw, True, False)
                    mm(pb[0:cn, :].rearrange("p (h q) -> p h q", h=4), identb[:, 0:cn],
                       cmask[:, ch, t * 128:(t + 1) * 128].unsqueeze(1).broadcast_to([128, 4, 128]), False, True)
                    e = next_et()
                    act(e[0:cn, :], pb[0:cn, :], AF.Exp)
                    ets.append((e, cn, ch))
                for hp in range(2):
                    ob = psb[4 + hp][:, 0:258].rearrange("p (h c) -> p h c", h=2)
                    first = True
                    for i, (e, cn, ch) in enumerate(ets):
                        for hh in range(2):
                            h = hp * 2 + hh
                            last = (i == len(ets) - 1) and hh == 1
                            mm(ob[:, hh, :], e[0:cn, h * 128:(h + 1) * 128], vca[0:cn, ch, w, :], first, last, sgc=True)
                            first = False
                for hp in range(2):
                    ob = psb[4 + hp][:, 0:258].rearrange("p (h c) -> p h c", h=2)
                    ts("dve", dn[:, 0, 2 * hp:2 * hp + 2], ob[:, :, 64], 1e-30, None, ALU.max)
                recip(dn[:, 0, :], dn[:, 0, :])
                use_sel = t >= 8
                if use_sel:
                    for h in range(4):
                        ob = psb[4 + h // 2][:, 0:258].rearrange("p (h c) -> p h c", h=2)
                        if h == 0:
                            ts("dve", impa, ob[:, 0, 65:129], dn[:, 0, 0:1], None, ALU.mult)
                        else:
                            stt(impa, ob[:, h % 2, 65:129], dn[:, 0, h:h + 1], impa, ALU.mult, ALU.add)
                    tt("dve", sc, impa, rtab[:, 62 - 2 * t: 62 - 2 * t + 64], ALU.add)
                    ts("dve", sc[:, 0:1], sc[:, 0:1], 1e4, None, ALU.add)
                    P.op("dve", lambda e: e.max(out=m8[:, 0:8], in_=sc), reads=[sc], writes=[m8[:, 0:8]])
                    P.op("dve", lambda e: e.match_replace(out=sc2, in_to_replace=m8[:, 0:8], in_values=sc, imm_value=-1e30),
                         reads=[sc, m8[:, 0:8]], writes=[sc2])
                    P.op("dve", lambda e: e.max(out=m8[:, 8:16], in_=sc2), reads=[sc2], writes=[m8[:, 8:16]])
                    ts("dve", mb[:, 0:64], sc, m8[:, 15:16], 1.0, ALU.is_ge, ALU.subtract)
                    b6 = bank_bf(6)
                    trp(b6[:, 0:128], mb, identb)
                    copy("act", mbT, b6[:, 0:128])
                tt("dve", sg_[:, 0, :], dn[:, 0, :], gsl[:, :, 0], ALU.mult)
                oav = oa[:, w * 256:(w + 1) * 256].rearrange("p (h d) -> p h d", h=4)
                for hp in range(2):
                    ob = psb[4 + hp][:, 0:258].rearrange("p (h c) -> p h c", h=2)
                    tt("dve", oav[:, 2 * hp:2 * hp + 2, :], ob[:, :, 0:64],
                       sg_[:, 0, 2 * hp:2 * hp + 2].unsqueeze(2).broadcast_to([128, 2, 64]), ALU.mult)
                for br in (1, 2):
                    kT = kslT if br == 1 else kwnT
                    vv = vsl if br == 1 else vwn
                    js = list(range(0, t + 1)) if br == 1 else list(range(max(0, t - 4), t + 1))
                    ob = psb[1 + br][:, 0:260].rearrange("p (h c) -> p h c", h=4)
                    first = True
                    for ji, j in enumerate(js):
                        pb = score_bank()
                        extra = []
                        if br == 1 and use_sel:
                            extra.append(("sel", None))
                        if j == t:
                            extra.append(("m", diagm))
                        if br == 2 and j == t - 4:
                            extra.append(("m", winm))
                        mm(pb[:, :], kT[:, j * 128:(j + 1) * 128], qw, True, len(extra) == 0)
                        for xi, (kind, mk_) in enumerate(extra):
                            lastx = xi == len(extra) - 1
                            if kind == "sel":
                                mm(pb[:, :].rearrange("p (h q) -> p h q", h=4), eexp[:, j * 128:(j + 1) * 128],
                                   mbT.unsqueeze(1).broadcast_to([128, 4, 128]), False, lastx)
                            else:
                                mm(pb[:, :], identb, mk_, False, lastx)
                        e = next_et()
                        act(e, pb[:, :], AF.Exp)
                        for h in range(4):
                            mm(ob[:, h, :], e[:, h * 128:(h + 1) * 128], vv[:, j, w, :], first,
                               (ji == len(js) - 1) and h == 3, sgc=True)
                            first = False
                    ts("dve", dn[:, br, :], ob[:, :, 64], 1e-30, None, ALU.max)
                    recip(dn[:, br, :], dn[:, br, :])
                    tt("dve", sg_[:, br, :], dn[:, br, :], gsl[:, :, br], ALU.mult)
                    ov_ = otmp.rearrange("p (h d) -> p h d", h=4)
                    tt("dve", ov_, ob[:, :, 0:64], sg_[:, br, :].unsqueeze(2).broadcast_to([128, 4, 64]), ALU.mult)
                    tt("dve", oav, oav, ov_, ALU.add)
            ob_ = onb[t % 2]
            copy("act", ob_, oa)
            dtoks.append(dma("pool", mix[t * 128:(t + 1) * 128, 0:512], ob_))
        if stop_after == "B":
            return dtoks

        c3 = Carver(KEND, TOT)
        wg_b = alloc(c3, [8, DFF], BF16)
        wu_b = alloc(c3, [8, DFF], BF16)
        wd_b = alloc(c3, [22, D], BF16)
        wo_b = alloc(c3, [8, D], BF16)
        HW = DFF // 2
        stg3 = [alloc(c3, [HW], F32), alloc(c3, [HW], F32)]
        fng = alloc(c3, [D], F32) if is_final else None
        si = 0
        for k in range(8):
            s = stg3[si % 2]; si += 1
            dma("sp", s[:, 0:D], Wd["w_out"][k * 128:(k + 1) * 128, :])
            copy("pool", wo_b[:, k, :], s[:, 0:D])
        for (wsrc, wdst) in (("w_gate", wg_b), ("w_up", wu_b)):
            for k in range(8):
                for hh in range(2):
                    s = stg3[si % 2]; si += 1
                    dma("sp", s, Wd[wsrc][k * 128:(k + 1) * 128, hh * HW:(hh + 1) * HW])
                    ts("pool", wdst[:, k, hh * HW:(hh + 1) * HW], s, ptT[:, 8 + k:9 + k], 1.0, ALU.mult, ALU.mult)
        for c2 in range(22):
            s = stg3[si % 2]; si += 1
            dma("sp", s[:, 0:D], Wd["w_down"][c2 * 128:(c2 + 1) * 128, :])
            copy("pool", wd_b[:, c2, :], s[:, 0:D])
        if is_final:
            dma("sp", fng, fnorm_d.partition_broadcast(128))
        mxb = [alloc(c3, [D], BF16), alloc(c3, [D], BF16)]
        xb3 = [alloc(c3, [D], F32), alloc(c3, [D], F32)]
        x1b = alloc(c3, [D], F32)
        h3T = alloc(c3, [8, 128], BF16)
        mxT = h3T
        ss3 = alloc(c3, [4], F32)
        sil = [alloc(c3, [512], F32), alloc(c3, [512], F32)]
        aT_in = alloc(c3, [DFF], BF16)
        junk3 = aT_in[:, 0:D]
        aT = alloc(c3, [22, 128], BF16)
        h3 = aT[:, 0:8, :].rearrange("p c t -> p (c t)")

        toks = []
        for t in range(NT):
            mx = mxb[t % 2]
            xs = xb3[t % 2]
            dma("sp", mx, mix[t * 128:(t + 1) * 128, :])
            dma("sp", xs, x_src[t * 128:(t + 1) * 128, :])
            b0 = bank_bf(0)
            for k in range(8):
                trp(b0[:, k * 128:(k + 1) * 128], mx[:, k * 128:(k + 1) * 128], identb)
            copy("act", mxT, b0.rearrange("p (k t) -> p k t", k=8))
            for hf in range(2):
                for k in range(8):
                    mm(psb[6 + hf][:, :], mxT[:, k, :], wo_b[:, k, hf * 512:(hf + 1) * 512], k == 0, k == 7)
            for hf in range(2):
                tt("dve", x1b[:, hf * 512:(hf + 1) * 512], psb[6 + hf][:, :], xs[:, hf * 512:(hf + 1) * 512], ALU.add)
            act(junk3, x1b, AF.Square, accum=ss3[:, 0:1])
            act(ss3[:, 1:2], ss3[:, 0:1], AF.Sqrt, scale=1.0 / D, bias=EPS)
            recip(ss3[:, 2:3], ss3[:, 1:2])
            ts("dve", h3, x1b, ss3[:, 2:3], None, ALU.mult)
            b1 = bank_bf(1)
            for k in range(8):
                trp(b1[:, k * 128:(k + 1) * 128], h3[:, k * 128:(k + 1) * 128], identb)
            copy("act", h3T, b1.rearrange("p (k t) -> p k t", k=8))
            for n in range(6):
                n0 = n * 512
                nn = min(512, DFF - n0)
                pg = psb[2 + n % 2]
                pu = psb[4 + n % 2]
                for k in range(8):
                    mm(pg[:, 0:nn], h3T[:, k, :], wg_b[:, k, n0:n0 + nn], k == 0, k == 7)
                for k in range(8):
                    mm(pu[:, 0:nn], h3T[:, k, :], wu_b[:, k, n0:n0 + nn], k == 0, k == 7)
                sl = sil[n % 2]
                act(sl[:, 0:nn], pg[:, 0:nn], AF.Silu)
                tt("dve", aT_in[:, n0:n0 + nn], sl[:, 0:nn], pu[:, 0:nn], ALU.mult)
            for r in range(3):
                bb = bank_bf(r % 2)
                c_lo = r * 8
                c_hi = min(22, c_lo + 8)
                for c in range(c_lo, c_hi):
                    trp(bb[:, (c - c_lo) * 128:(c - c_lo + 1) * 128], aT_in[:, c * 128:(c + 1) * 128], identb)
                copy("act" if r != 1 else "dve", aT[:, c_lo:c_hi, :],
                     bb[:, 0:(c_hi - c_lo) * 128].rearrange("p (c t) -> p c t", c=c_hi - c_lo))
            for hf in range(2):
                for c in range(22):
                    mm(psb[6 + hf][:, :], aT[:, c, :], wd_b[:, c, hf * 512:(hf + 1) * 512], c == 0, c == 21)
            x2 = xs
            for hf in range(2):
                tt("dve", x2[:, hf * 512:(hf + 1) * 512], psb[6 + hf][:, :], x1b[:, hf * 512:(hf + 1) * 512], ALU.add)
            if is_final:
                act(junk3, x2, AF.Square, accum=ss3[:, 0:1])
                act(ss3[:, 1:2], ss3[:, 0:1], AF.Sqrt, scale=1.0 / D, bias=EPS)
                recip(ss3[:, 2:3], ss3[:, 1:2])
                stt(x2, x2, ss3[:, 2:3], fng, ALU.mult, ALU.mult)
            toks.append(dma("pool", x_dst[t * 128:(t + 1) * 128, :], x2))
        return toks + (dtoks if dbg else [])

    toks = layer(x_src, y_out, final)
    P.emit(final_waits=toks)
    nc._prog_stats = {e: len(P.instrs[e]) for e in P.ENGS}
    return nc


_CACHE = {}


def _get_prog(final):
    if final not in _CACHE:
        _CACHE[final] = build_layer(final=final)
    return _CACHE[final]


def kernel(**inputs):
    x = np.ascontiguousarray(np.asarray(inputs["x"], dtype=np.float32))
    B = x.shape[0]
    cb, cf = _consts()
    cur = [x[b] for b in range(B)]
    for l in range(DEPTH):
        final = (l == DEPTH - 1)
        nc = _get_prog(final)
        in_maps = []
        for b in range(B):
            m = {"x": cur[b], "cb": cb, "cf": cf,
                 "final_norm": np.ascontiguousarray(np.asarray(inputs["final_norm"], dtype=np.float32))}
            for n, _ in WNAMES:
                m[n] = np.ascontiguousarray(np.asarray(inputs[n][l], dtype=np.float32))
            in_maps.append(m)
        res = run_bass_kernel_spmd(nc, in_maps, core_ids=list(range(B)))
        cur = [np.asarray(res.results[b]["y"], dtype=np.float32) for b in range(B)]
    return np.stack(cur, axis=0)
```

```python
import numpy as np
import ml_dtypes
import concourse.bass as bass
import concourse.mybir as mybir
from concourse.bass_utils import run_bass_kernel_spmd

F32 = mybir.dt.float32
BF16 = mybir.dt.bfloat16
AF = mybir.ActivationFunctionType
ALU = mybir.AluOpType
AX = mybir.AxisListType

S = 4096
D = 1024
NT = S // 128
DIN = 2600
DFF = 2816
EPS = 1e-6
BIG = 30000.0
DEPTH = 4

_ISZ = {}


def isz(dt):
    k = str(dt)
    if k not in _ISZ:
        _ISZ[k] = mybir.dt.size(dt)
    return _ISZ[k]


class Prog:
    ENGS = ["pe", "act", "dve", "pool", "sp"]
    NDSEM = 8

    def __init__(self, nc):
        self.nc = nc
        self.instrs = {e: [] for e in self.ENGS}
        self.bpp = {}
        self.recs = {}
        self.clock = {e: {x: -1 for x in self.ENGS} for e in self.ENGS}
        self.iclock = {e: [] for e in self.ENGS}
        self.dma_cnt = {e: 0 for e in self.ENGS}
        self.dma_known = {e: set() for e in self.ENGS}
        self.marked = {e: set() for e in self.ENGS}
        self.ctx = []
        self.group = 0
        self.igroup = {e: [] for e in self.ENGS}

    def sbuf(self, name, shape, dt):
        t = self.nc.sbuf_tensor(name, list(shape), dt)
        h = t.__enter__()
        self.ctx.append(t)
        n = 1
        for s in shape[1:]:
            n *= s
        self.bpp[name] = n * isz(dt)
        return h

    def psum(self, name, shape, dt):
        t = self.nc.psum_tensor(name, list(shape), dt)
        h = t.__enter__()
        self.ctx.append(t)
        n = 1
        for s in shape[1:]:
            n *= s
        self.bpp[name] = n * isz(dt)
        return h

    def dram(self, name, shape, dt, kind="Internal"):
        t = self.nc.dram_tensor(name, list(shape), dt, kind=kind)
        self.bpp[name] = None
        return t.ap()

    def region(self, a):
        name = a.tensor.name
        sz = isz(a.dtype)
        bpp = self.bpp.get(name, None)
        ap = a.ap
        if bpp is None:
            lo = a.offset
            ext = 0
            for st, cn in ap:
                ext += (cn - 1) * abs(st)
            return (name, 0, 1, lo * sz, (lo + ext + 1) * sz)
        off = a.offset * sz
        p0 = off // bpp
        lo = off % bpp
        if name.startswith("pb"):
            q0 = (p0 // 32) * 32
            q1 = ((p0 + ap[0][1] + 31) // 32) * 32
            return (name, q0, q1, 0, bpp)
        ext = 0
        for st, cn in ap[1:]:
            ext += (cn - 1) * abs(st)
        return (name, p0, p0 + ap[0][1], lo, lo + (ext + 1) * sz)

    def op(self, eng, fn, reads=(), writes=(), dma=False):
        import os
        lim = int(os.environ.get("KMAXOPS", "0"))
        self.nops = getattr(self, "nops", 0) + 1
        if lim and self.nops > lim:
            return None
        if os.environ.get("KLOG"):
            import sys as _s
            f = _s._getframe(1)
            while f is not None and f.f_code.co_name not in ("layer", "build_layer"):
                f = f.f_back
            print("OP", self.nops, eng, f.f_lineno if f else -1)
        idx = len(self.instrs[eng])
        waits = {}
        myclk = self.clock[eng]

        def need(tok):
            if tok[0] == "dma":
                if tok in self.dma_known[eng]:
                    return
                self.dma_known[eng].add(tok)
                waits[tok] = True
            else:
                e2, i2 = tok
                if myclk[e2] >= i2:
                    return
                waits[tok] = True

        if dma:
            q = eng
            di = self.dma_cnt[q]
            self.dma_cnt[q] += 1
            mytok = ("dma", q, di)
            if di >= self.NDSEM:
                need(("dma", q, di - self.NDSEM))
        else:
            mytok = (eng, idx)

        for (aps, kind) in ((reads, "R"), (writes, "W")):
            for a in aps:
                name, p0, p1, lo, hi = self.region(a)
                lst = self.recs.get(name, [])
                keep = []
                for r in lst:
                    (rp0, rp1, rlo, rhi, rkind, rtok) = r
                    ov = not (rp1 <= p0 or p1 <= rp0 or rhi <= lo or hi <= rlo)
                    rr = name.startswith("pb") and rtok[0] != eng
                    if ov and (rkind == "W" or kind == "W" or rr) and rtok != mytok:
                        if rtok[0] != "dma" and rtok[0] == eng and not dma:
                            if eng != "pe":
                                need(rtok)
                        else:
                            need(rtok)
                    cov = rp0 >= p0 and rp1 <= p1 and rlo >= lo and rhi <= hi
                    if kind == "W" and cov:
                        continue
                    if kind == "R" and rkind == "R" and cov and rtok[0] == mytok[0] and rtok[0] != "dma":
                        continue
                    keep.append(r)
                keep.append((p0, p1, lo, hi, kind, mytok))
                self.recs[name] = keep

        red = {}
        dmaw = []
        for tok in waits:
            if tok[0] == "dma":
                dmaw.append(tok)
            else:
                red[tok[0]] = max(red.get(tok[0], -1), tok[1])
        for e2, i2 in red.items():
            self.marked[e2].add(i2)
            oc = self.iclock[e2][i2]
            for x in self.ENGS:
                if oc[x] > myclk[x]:
                    myclk[x] = oc[x]
            if i2 > myclk[e2]:
                myclk[e2] = i2
        snap = dict(myclk)
        if not dma:
            snap[eng] = max(snap[eng], idx - 1)
        self.iclock[eng].append(snap)
        self.igroup[eng].append(self.group)
        self.instrs[eng].append(dict(fn=fn, waits=red, dmaw=dmaw, dma=(mytok if dma else None)))
        return mytok

    def emit(self, final_waits=()):
        nc = self.nc
        ENGH = {"pe": "tensor", "act": "scalar", "dve": "vector", "pool": "gpsimd", "sp": "sync"}
        rank = {}
        ngroups = self.group + 1
        sems = {}
        semctx = []
        for e in self.ENGS:
            cnt = [0] * ngroups
            rank[e] = {}
            for i in sorted(self.marked[e]):
                g = self.igroup[e][i]
                cnt[g] += 1
                rank[e][i] = (g, cnt[g])
            sems[e] = []
            for g in range(ngroups):
                c = nc.semaphore("s_%s_%d" % (e, g))
                sems[e].append(c.__enter__())
                semctx.append(c)
        dsems = {}
        for q in self.ENGS:
            if self.dma_cnt[q] > 0:
                lst = []
                for k in range(self.NDSEM):
                    c = nc.semaphore("d_%s_%d" % (q, k))
                    lst.append(c.__enter__())
                    semctx.append(c)
                dsems[q] = lst
        blk = nc.Block()
        block = blk.__enter__()
        prog = self

        def mk(e):
            def body(eng):
                for idx, ins in enumerate(prog.instrs[e]):
                    for e2, i2 in ins["waits"].items():
                        g2, v2 = rank[e2][i2]
                        eng.wait_ge(sems[e2][g2], v2)
                    for (_, q, di) in ins["dmaw"]:
                        eng.wait_ge(dsems[q][di % prog.NDSEM], 16 * (di // prog.NDSEM + 1))
                    r = ins["fn"](eng)
                    if ins["dma"] is not None:
                        (_, q, di) = ins["dma"]
                        r.then_inc(dsems[q][di % prog.NDSEM], 16)
                    elif idx in rank[e]:
                        r.then_inc(sems[e][rank[e][idx][0]], 1)
                if e == "sp":
                    for tok in final_waits:
                        if tok is None:
                            continue
                        (_, q, di) = tok
                        eng.wait_ge(dsems[q][di % prog.NDSEM], 16 * (di // prog.NDSEM + 1))
            return body

        for e in self.ENGS:
            if len(self.instrs[e]) == 0 and not (e == "sp" and final_waits):
                continue
            getattr(block, ENGH[e])(mk(e))
        blk.__exit__(None, None, None)
        for c in reversed(semctx):
            c.__exit__(None, None, None)
        for c in reversed(self.ctx):
            c.__exit__(None, None, None)


CB_ID = 0
CB_DIAG = 128
CB_WIN = 640
CB_BD = 1152
CB_ONES = 1280
CB_EEXP = 1408
CB_CMASK = 5504
CB_VCA = 13696
CB_HM = 13696 + 516
CB_BD256 = CB_HM + 64
NCB = CB_BD256 + 256
CF_ID = 0
CF_BDM = 128
CF_TRIL = 256
CF_RTAB = 384
CF_COS = 576
CF_SIN = 1600
CF_CCOS = 2624
CF_CSIN = 2688
CF_CM = 2752
NCF = 2816


def _consts():
    p = np.arange(128)[:, None]
    f = np.arange(128)[None, :]
    cb = np.zeros((128, NCB), np.float32)
    cb[:, CB_ID:CB_ID + 128] = (p == f)
    diag = np.where(p <= f, 0.0, -BIG)
    win = np.where(p > f, 0.0, -BIG)
    cb[:, CB_DIAG:CB_DIAG + 512] = np.tile(diag, (1, 4))
    cb[:, CB_WIN:CB_WIN + 512] = np.tile(win, (1, 4))
    bd = ((p // 64 == f // 64) & (p <= f)).astype(np.float32)
    cb[:, CB_BD:CB_BD + 128] = bd
    cb[:, CB_ONES:CB_ONES + 128] = 1.0
    m = np.arange(4096)[None, :]
    cb[:, CB_EEXP:CB_EEXP + 4096] = np.where((m // 64) == p, BIG, 0.0)
    for ch in range(2):
        c = ch * 128 + p
        cb[:, CB_CMASK + ch * 4096:CB_CMASK + (ch + 1) * 4096] = np.where(16 * c + 31 <= m, 0.0, -BIG)
    ncmp = 255
    cs = np.arange(ncmp) * 16
    ce = cs + 32
    bs = np.arange(64) * 64
    be = bs + 64
    ov = np.clip(np.minimum(ce[:, None], be[None, :]) - np.maximum(cs[:, None], bs[None, :]), 0, None) / 32.0
    vca = np.zeros((128, 2, 2, 129), np.float32)
    for ch in range(2):
        for pp in range(128):
            c = ch * 128 + pp
            if c < ncmp:
                vca[pp, ch, :, 64] = 1.0
                vca[pp, ch, :, 65:129] = ov[c][None, :]
    cb[:, CB_VCA:CB_VCA + 516] = vca.reshape(128, 516)
    for h in range(4):
        cb[:, CB_HM + h] = (np.arange(128) // 32 == h)
    cb[:, CB_BD256:CB_BD256 + 256] = ((np.arange(128)[:, None] // 32) == (np.arange(256)[None, :] // 64))

    cf = np.zeros((128, NCF), np.float32)
    cf[:, CF_ID:CF_ID + 128] = (p == f)
    cf[:, CF_BDM:CF_BDM + 128] = -bd / 16.0
    cf[:, CF_TRIL:CF_TRIL + 128] = (f <= p)
    j = np.arange(192)[None, :]
    mm = j - 62
    tbrel = (p >= 64).astype(np.int64)
    r = np.zeros((128, 192), np.float32)
    r[(mm == tbrel) | (mm == tbrel - 1)] = 1e4
    r[mm > tbrel] = -1e30
    cf[:, CF_RTAB:CF_RTAB + 192] = r
    half = 32
    inv = (1.0 / (np.float32(10000.0) ** (np.arange(half, dtype=np.float32) / np.float32(half)))).astype(np.float32)
    pos = (np.arange(32)[None, :] * 128 + np.arange(128)[:, None]).astype(np.float32)
    ang = (pos[:, :, None] * inv[None, None, :]).astype(np.float32)
    cf[:, CF_COS:CF_COS + 1024] = np.cos(ang).astype(np.float32).reshape(128, 1024)
    cf[:, CF_SIN:CF_SIN + 1024] = np.sin(ang).astype(np.float32).reshape(128, 1024)
    cpos = ((np.arange(2)[None, :] * 128 + np.arange(128)[:, None]) * 16 + 31).astype(np.float32)
    cang = (cpos[:, :, None] * inv[None, None, :]).astype(np.float32)
    cf[:, CF_CCOS:CF_CCOS + 64] = np.cos(cang).astype(np.float32).reshape(128, 64)
    cf[:, CF_CSIN:CF_CSIN + 64] = np.sin(cang).astype(np.float32).reshape(128, 64)
    for c in range(2):
        cf[:, CF_CM + c] = (np.arange(128) // 64 == c)
    return cb.astype(ml_dtypes.bfloat16), cf


WNAMES = [("attn_norm", [D]), ("w_in", [D, DIN]), ("cmp_pos_k", [32, 64]), ("cmp_w1_k", [2048, 256]),
          ("cmp_w2_k", [256, 64]), ("cmp_pos_v", [32, 64]), ("cmp_w1_v", [2048, 256]), ("cmp_w2_v", [256, 64]),
          ("gla_w_up", [16, 128]), ("gla_b_up", [128]), ("gla_norm", [64]), ("sg_ln_g", [256]), ("sg_ln_b", [256]),
          ("sg_w", [4, 128, 128]), ("sg_b", [4, 128]), ("w_out", [D, D]), ("ffn_norm", [D]),
          ("w_gate", [D, DFF]), ("w_up", [D, DFF]), ("w_down", [DFF, D])]


def build_layer(final=False, dbg=False, stop_after="C", nlayers=1, stacked=False):
    nc = bass.Bass("TRN2", target_bir_lowering=False)
    P = Prog(nc)
    x_in = P.dram("x", [S, D], F32, kind="ExternalInput")
    cb_d = P.dram("cb", [128, NCB], BF16, kind="ExternalInput")
    cf_d = P.dram("cf", [128, NCF], F32, kind="ExternalInput")
    if stacked:
        Wfull = {n: P.dram(n, [DEPTH] + s, F32, kind="ExternalInput") for n, s in WNAMES}
        Wd = {}
    else:
        Wd = {n: P.dram(n, s, F32, kind="ExternalInput") for n, s in WNAMES}
    fnorm_d = P.dram("final_norm", [D], F32, kind="ExternalInput")
    y_out = P.dram("y", [S, D], F32, kind="ExternalOutput")
    sk = "ExternalOutput" if dbg else "Internal"
    qTs = P.dram("qTs", [NT, 128, 512], BF16, kind=sk)
    mix = P.dram("mixs", [S, D], BF16, kind=sk)
    if dbg:
        d_kcT = P.dram("d_kcT", [128, 256], BF16, kind="ExternalOutput")
        d_vca = P.dram("d_vca", [128, 516], BF16, kind="ExternalOutput")
        d_kslT = P.dram("d_kslT", [128, 4096], BF16, kind="ExternalOutput")
        d_vsl = P.dram("d_vsl", [128, 32 * 2 * 65], BF16, kind="ExternalOutput")
        d_gates = P.dram("d_gates", [128, 32 * 24], F32, kind="ExternalOutput")
    dtoks = []

    ARENA = 104448
    arena = P.sbuf("arena", [128, ARENA], BF16)
    psb = [P.psum("pb%d" % i, [128, 512], F32) for i in range(8)]

    class Carver:
        def __init__(self, base, limit):
            self.off = base
            self.limit = limit

        def take(self, nbytes):
            nbytes = (nbytes + 63) // 64 * 64
            o = self.off
            self.off += nbytes
            assert self.off <= self.limit, (self.off, self.limit)
            return o

    def view(off_bytes, shape, dt, parts=128):
        n = 1
        for s in shape:
            n *= s
        if dt == BF16:
            a = arena[0:parts, off_bytes // 2: off_bytes // 2 + n]
        else:
            a = arena[0:parts, off_bytes // 2: off_bytes // 2 + n * 2].bitcast(F32)
        if len(shape) == 1:
            return a
        names = " ".join("d%d" % i for i in range(len(shape)))
        kw = {"d%d" % i: shape[i] for i in range(len(shape))}
        return a.rearrange("p (%s) -> p %s" % (names, names), **kw)

    def alloc(cv, shape, dt, parts=128):
        n = 1
        for s in shape:
            n *= s
        return view(cv.take(n * isz(dt)), shape, dt, parts)

    TOT = ARENA * 2
    cvK = Carver(0, 8 * 1024)
    identb = alloc(cvK, [128], BF16)
    identf = alloc(cvK, [128], F32)
    diagm = alloc(cvK, [512], BF16)
    winm = alloc(cvK, [512], BF16)
    bdmask = alloc(cvK, [128], BF16)
    onesb = alloc(cvK, [128], BF16)
    bdmf = alloc(cvK, [128], F32)
    trilf = alloc(cvK, [128], F32)
    rtab = alloc(cvK, [192], F32)
    ptT = alloc(cvK, [128], F32)
    gng = alloc(cvK, [64], F32)
    bupb = alloc(cvK, [128], BF16)
    wupb = alloc(cvK, [128], BF16)
    hm4 = alloc(cvK, [4], BF16)
    bd256 = alloc(cvK, [256], BF16)
    cm2 = alloc(cvK, [2], F32)
    KEND = cvK.off

    def dma(q, out, in_):
        return P.op(q, lambda e: e.dma_start(out=out, in_=in_), reads=[in_], writes=[out], dma=True)

    def act(out, in_, func, scale=None, bias=None, accum=None):
        kw = {}
        rd = [in_]
        wr = [out]
        if scale is not None:
            kw["scale"] = scale
            if not isinstance(scale, (int, float)):
                rd.append(scale)
        if bias is not None:
            kw["bias"] = bias
            if not isinstance(bias, (int, float)):
                rd.append(bias)
        if accum is not None:
            kw["accum_out"] = accum
            wr.append(accum)
        return P.op("act", lambda e: e.activation(out=out, in_=in_, func=func, **kw), reads=rd, writes=wr)

    def mm(out, lhsT, rhs, start, stop, tp=None, sgc=False):
        kw = {}
        if tp is not None:
            kw["tile_position"] = tp
        if sgc:
            kw["skip_group_check"] = True
        return P.op("pe", lambda e: e.matmul(out, lhsT=lhsT, rhs=rhs, start=start, stop=stop, **kw),
                    reads=[lhsT, rhs], writes=[out])

    def trp(out, in_, ident):
        return P.op("pe", lambda e: e.transpose(out, in_, ident), reads=[in_, ident], writes=[out])

    def tt(eng, out, in0, in1, op):
        return P.op(eng, lambda e: e.tensor_tensor(out=out, in0=in0, in1=in1, op=op), reads=[in0, in1], writes=[out])

    def ts(eng, out, in0, s1, s2, op0, op1=None):
        rd = [in0]
        if not isinstance(s1, (int, float)):
            rd.append(s1)
        if s2 is not None and not isinstance(s2, (int, float)):
            rd.append(s2)
        if op1 is None:
            return P.op(eng, lambda e: e.tensor_scalar(out=out, in0=in0, scalar1=s1, scalar2=None, op0=op0),
                        reads=rd, writes=[out])
        return P.op(eng, lambda e: e.tensor_scalar(out=out, in0=in0, scalar1=s1, scalar2=s2, op0=op0, op1=op1),
                    reads=rd, writes=[out])

    def stt(out, in0, sc, in1, op0, op1):
        rd = [in0, in1]
        if not isinstance(sc, (int, float)):
            rd.append(sc)
        return P.op("dve", lambda e: e.scalar_tensor_tensor(out=out, in0=in0, scalar=sc, in1=in1, op0=op0, op1=op1),
                    reads=rd, writes=[out])

    def copy(eng, out, in_):
        if eng == "act":
            return act(out, in_, AF.Copy)
        return P.op(eng, lambda e: e.tensor_copy(out=out, in_=in_), reads=[in_], writes=[out])

    def memset(eng, out, val):
        return P.op(eng, lambda e: e.memset(out, val), reads=[], writes=[out])

    def recip(out, in_):
        return P.op("dve", lambda e: e.reciprocal(out=out, in_=in_), reads=[in_], writes=[out])

    def bank_bf(i):
        return psb[i][:].bitcast(BF16)

    dma("sp", identb, cb_d[:, CB_ID:CB_ID + 128])
    dma("sp", diagm, cb_d[:, CB_DIAG:CB_DIAG + 512])
    dma("sp", winm, cb_d[:, CB_WIN:CB_WIN + 512])
    dma("sp", bdmask, cb_d[:, CB_BD:CB_BD + 128])
    dma("sp", onesb, cb_d[:, CB_ONES:CB_ONES + 128])
    dma("sp", identf, cf_d[:, CF_ID:CF_ID + 128])
    dma("sp", bdmf, cf_d[:, CF_BDM:CF_BDM + 128])
    dma("sp", trilf, cf_d[:, CF_TRIL:CF_TRIL + 128])
    dma("sp", rtab, cf_d[:, CF_RTAB:CF_RTAB + 192])
    dma("sp", hm4, cb_d[:, CB_HM:CB_HM + 4])
    dma("sp", bd256, cb_d[:, CB_BD256:CB_BD256 + 256])
    dma("sp", cm2, cf_d[:, CF_CM:CF_CM + 2])

    x_src = x_in
    fins = []

    def layer(x_src, x_dst, is_final):
        cv = Carver(KEND, TOT)
        cmask = alloc(cv, [2, 4096], BF16)
        eexp = alloc(cv, [4096], BF16)
        cosT = alloc(cv, [32, 32], F32)
        sinT = alloc(cv, [32, 32], F32)
        ccos = alloc(cv, [2, 32], F32)
        csin = alloc(cv, [2, 32], F32)
        gates = alloc(cv, [32, 24], F32)
        kslT = alloc(cv, [4096], BF16)
        kwnT = alloc(cv, [4096], BF16)
        vsl = alloc(cv, [32, 2, 65], BF16)
        vwn = alloc(cv, [32, 2, 65], BF16)
        kcrT = alloc(cv, [4096], BF16)
        vcrT = alloc(cv, [4096], BF16)
        kcT = alloc(cv, [256], BF16)
        vca = alloc(cv, [2, 2, 129], BF16)
        lng = alloc(cv, [256], F32)
        lnb = alloc(cv, [256], F32)
        sgWT = alloc(cv, [4, 128], BF16)
        st_f = alloc(cv, [256], F32)
        st_b = alloc(cv, [2, 256], BF16)
        w_off = cv.off
        win_b = alloc(cv, [8, DIN], BF16)
        wk_off = cv.off

        dma("sp", cmask, cb_d[:, CB_CMASK:CB_CMASK + 8192].rearrange("p (a b) -> p a b", a=2))
        dma("sp", eexp, cb_d[:, CB_EEXP:CB_EEXP + 4096])
        dma("sp", cosT, cf_d[:, CF_COS:CF_COS + 1024].rearrange("p (a b) -> p a b", a=32))
        dma("sp", sinT, cf_d[:, CF_SIN:CF_SIN + 1024].rearrange("p (a b) -> p a b", a=32))
        dma("sp", ccos, cf_d[:, CF_CCOS:CF_CCOS + 64].rearrange("p (a b) -> p a b", a=2))
        dma("sp", csin, cf_d[:, CF_CSIN:CF_CSIN + 64].rearrange("p (a b) -> p a b", a=2))
        dma("sp", vca, cb_d[:, CB_VCA:CB_VCA + 516].rearrange("p (a b c) -> p a b c", a=2, b=2))
        dma("sp", gng, Wd["gla_norm"].partition_broadcast(128))
        dma("sp", lng, Wd["sg_ln_g"].partition_broadcast(128))
        dma("sp", lnb, Wd["sg_ln_b"].partition_broadcast(128))

        cw = Carver(wk_off, TOT)
        stg = [alloc(cw, [DFF], F32), alloc(cw, [DFF], F32)]
        stg_i = [0]

        def stage():
            s = stg[stg_i[0] % 2]
            stg_i[0] += 1
            return s

        pt = stage()[:, 0:128]
        memset("pool", pt, 0.0)
        dma("sp", pt[0:8, :], Wd["attn_norm"].rearrange("(k p) -> k p", p=128))
        dma("sp", pt[8:16, :], Wd["ffn_norm"].rearrange("(k p) -> k p", p=128))
        dma("sp", pt[16:20, :], Wd["sg_b"])
        dma("sp", pt[32:64, 0:64], Wd["cmp_pos_k"])
        dma("sp", pt[64:96, 0:64], Wd["cmp_pos_v"])
        trp(psb[7][:, 0:128], pt, identf)
        copy("dve", ptT, psb[7][:, 0:128])
        s2 = stage()
        dma("sp", s2[0:16, 0:128], Wd["gla_w_up"])
        dma("sp", s2[0:1, 128:256], Wd["gla_b_up"].rearrange("(o n) -> o n", o=1))
        copy("pool", wupb[0:16, :], s2[0:16, 0:128])
        copy("pool", bupb[0:1, :], s2[0:1, 128:256])
        s3 = stage()
        s3v = s3[:, 0:512].rearrange("p (g j) -> p g j", g=4)
        dma("sp", s3v, Wd["sg_w"].rearrange("g i j -> i g j"))
        sgm = alloc(cw, [4, 128], BF16)
        tt("dve", sgm, s3v, trilf.unsqueeze(1).broadcast_to([128, 4, 128]), ALU.mult)
        b7b = bank_bf(7)
        for g in range(4):
            trp(b7b[:, 256 + g * 128: 256 + (g + 1) * 128], sgm[:, g, :], identb)
        copy("dve", sgWT, b7b[:, 256:768].rearrange("p (g i) -> p g i", g=4))
        for k in range(8):
            s = stage()
            dma("sp", s[:, 0:DIN], Wd["w_in"][k * 128:(k + 1) * 128, :])
            ts("pool", win_b[:, k, :], s[:, 0:DIN], ptT[:, k:k + 1], 1.0, ALU.mult, ALU.mult)
        memset("pool", vsl, 1.0)
        memset("pool", vwn, 1.0)
        memset("pool", st_f, 0.0)
        memset("pool", st_b, 0.0)
        if NT != 32:
            memset("pool", kcrT, 0.0)
            memset("pool", vcrT, 0.0)

        xbuf = [alloc(cw, [D], F32), alloc(cw, [D], F32)]
        hbf = alloc(cw, [D], BF16)
        junk = hbf
        hT = [alloc(cw, [8, 128], BF16), alloc(cw, [8, 128], BF16)]
        ssq = alloc(cw, [4], F32)
        ri = alloc(cw, [12, 64], F32)
        ro = alloc(cw, [768], BF16)
        rt1 = alloc(cw, [12, 32], F32)
        rt2 = alloc(cw, [12, 32], F32)
        qTst = [alloc(cw, [512], BF16), alloc(cw, [512], BF16)]
        rawb = alloc(cw, [256], BF16)
        alT = alloc(cw, [128], BF16)
        e1 = alloc(cw, [128], F32)
        spl = alloc(cw, [128], F32)
        epn = alloc(cw, [2, 128], F32)
        bcs = alloc(cw, [128], F32)
        qkg = alloc(cw, [2, 128], BF16)
        kgT = alloc(cw, [128], BF16)
        qTm = alloc(cw, [4, 128], BF16)
        qTc = alloc(cw, [2, 128], BF16)
        kgc = alloc(cw, [2, 128], BF16)
        tS2 = alloc(cw, [256], F32)
        memset("pool", qTc, 0.0)
        vg = alloc(cw, [256], BF16)
        attm = alloc(cw, [4, 128], BF16)
        ebc = alloc(cw, [2], F32)
        tS = alloc(cw, [256], F32)
        og = alloc(cw, [256], F32)
        og2 = alloc(cw, [256], F32)
        gms = alloc(cw, [8], F32)
        sir = alloc(cw, [256], F32)
        mixt = [alloc(cw, [512], BF16), alloc(cw, [512], BF16)]
        guv = alloc(cw, [512], F32)
        bns = alloc(cw, [8], F32)
        bna = alloc(cw, [4], F32)
        vn = alloc(cw, [256], F32)
        vnb = alloc(cw, [256], BF16)

        for t in range(NT):
            xs = xbuf[t % 2]
            hTt = hT[t % 2]
            dma("sp", xs, x_src[t * 128:(t + 1) * 128, :])
            act(junk, xs, AF.Square, accum=ssq[:, 0:1])
            act(ssq[:, 1:2], ssq[:, 0:1], AF.Sqrt, scale=1.0 / D, bias=EPS)
            recip(ssq[:, 2:3], ssq[:, 1:2])
            ts("dve", hbf, xs, ssq[:, 2:3], None, ALU.mult)
            b0 = bank_bf(0)
            for k in range(8):
                trp(b0[:, k * 128:(k + 1) * 128], hbf[:, k * 128:(k + 1) * 128], identb)
            copy("act", hTt, b0.rearrange("p (k t) -> p k t", k=8))
            chunks = [(0, 512, 1), (512, 1024, 2), (1024, 1432, 3), (1432, 1832, 4), (1832, 2088, 5), (2088, 2600, 6)]
            for (c0, c1, bk) in chunks:
                for k in range(8):
                    mm(psb[bk][:, 0:c1 - c0], hTt[:, k, :], win_b[:, k, c0:c1], k == 0, k == 7)
            for k in range(8):
                mm(psb[7][0:16, 256:384], win_b[:, k, 1816:1832], hTt[:, k, :], k == 0, k == 7)
            riv = ri.rearrange("p (w r) d -> p w r d", w=2)
            act(riv[:, :, 0:4, :], psb[1][:, 0:512].rearrange("p (w r d) -> p w r d", w=2, r=4), AF.Copy, scale=0.125)
            copy("dve", riv[:, :, 4, :], psb[2][:, 256:384].rearrange("p (w d) -> p w d", w=2))
            copy("dve", riv[:, :, 5, :], psb[3][:, 0:128].rearrange("p (w d) -> p w d", w=2))
            copy("act", rawb, psb[2][:, 0:256])
            copy("dve", vsl[:, t, :, 0:64], psb[2][:, 384:512].rearrange("p (w d) -> p w d", w=2))
            copy("dve", vwn[:, t, :, 0:64], psb[3][:, 128:256].rearrange("p (w d) -> p w d", w=2))
            act(gates[:, t, :], psb[3][:, 256:280], AF.Sigmoid)
            act(sir, psb[5][:, 0:256], AF.Silu)
            r4 = ri.rearrange("p (w r) (two d) -> p w r two d", w=2, two=2)
            x1 = r4[:, :, :, 0, :]
            x2 = r4[:, :, :, 1, :]
            cs_ = cosT[:, t, :].unsqueeze(1).unsqueeze(1).broadcast_to([128, 2, 6, 32])
            sn_ = sinT[:, t, :].unsqueeze(1).unsqueeze(1).broadcast_to([128, 2, 6, 32])
            t1 = rt1.rearrange("p (w r) d -> p w r d", w=2)
            t2 = rt2.rearrange("p (w r) d -> p w r d", w=2)
            rov = ro.rearrange("p (r w two d) -> p w r two d", r=6, w=2, two=2)
            tt("dve", t1, x1, cs_, ALU.mult)
            tt("dve", t2, x2, sn_, ALU.mult)
            tt("dve", rov[:, :, :, 0, :], t1, t2, ALU.subtract)
            tt("dve", t1, x2, cs_, ALU.mult)
            tt("dve", t2, x1, sn_, ALU.mult)
            tt("dve", rov[:, :, :, 1, :], t1, t2, ALU.add)
            b1 = bank_bf(1)
            for r in range(6):
                trp(b1[:, r * 128:(r + 1) * 128], ro[:, r * 128:(r + 1) * 128], identb)
            qst = qTst[t % 2]
            copy("act", qst, b1[:, 0:512])
            dma("pool", qTs[t], qst)
            copy("dve", kslT[:, t * 128:(t + 1) * 128], b1[:, 512:640])
            copy("dve", kwnT[:, t * 128:(t + 1) * 128], b1[:, 640:768])
            b2 = bank_bf(2)
            trp(b2[:, 0:128], rawb[:, 0:128], identb)
            trp(b2[:, 128:256], rawb[:, 128:256], identb)
            copy("act", kcrT[:, t * 128:(t + 1) * 128], b2[:, 0:128])
            copy("act", vcrT[:, t * 128:(t + 1) * 128], b2[:, 128:256])
            copy("dve", alT[0:16, :], psb[7][0:16, 256:384])
            mm(psb[7][:, 0:128], alT[0:16, :], wupb[0:16, :], True, False)
            mm(psb[7][:, 0:128], onesb[0:1, :], bupb[0:1, :], False, True)
            act(e1, psb[7][:, 0:128], AF.Exp, scale=-1.0)
            act(spl, e1, AF.Ln, bias=1.0)
            mm(psb[7][:, 128:256], bdmf, spl, True, True)
            act(epn[:, 0, :], psb[7][:, 128:256], AF.Exp)
            act(epn[:, 1, :], psb[7][:, 128:256], AF.Exp, scale=-1.0)
            copy("dve", bcs, psb[7][:, 128:256])
            stt(qkg[:, 0, :], psb[3][:, 280:408], 32.0 ** -0.5, epn[:, 0, :], ALU.mult, ALU.mult)
            tt("dve", qkg[:, 1, :], psb[4][:, 0:128], epn[:, 1, :], ALU.mult)
            copy("act", vg, psb[4][:, 128:384])
            trp(b2[:, 256:384], qkg[:, 0, :], identb)
            trp(b2[:, 384:512], qkg[:, 1, :], identb)
            copy("act", kgT, b2[:, 384:512])
            tt("dve", qTm, b2[:, 256:384].unsqueeze(1).broadcast_to([128, 4, 128]),
               hm4.unsqueeze(2).broadcast_to([128, 4, 128]), ALU.mult)
            copy("dve", qTc[:, 0, 0:64], b2[:, 256:320])
            copy("dve", qTc[:, 1, 64:128], b2[:, 320:384])
            tt("dve", kgc, qkg[:, 1, :].unsqueeze(1).broadcast_to([128, 2, 128]),
               cm2.unsqueeze(2).broadcast_to([128, 2, 128]), ALU.mult)
            trp(psb[7][:, 384:512], bcs, identf)
            act(ebc[:, 0:1], psb[7][:, 447:448], AF.Exp)
            act(ebc[:, 1:2], psb[7][:, 511:512], AF.Exp)
            mm(psb[3][:, 0:512], kgT, qTm.rearrange("p h i -> p (h i)"), True, True)
            tt("dve", attm, psb[3][:, 0:512].rearrange("p (h i) -> p h i", h=4),
               bdmask.unsqueeze(1).broadcast_to([128, 4, 128]), ALU.mult)
            first = True
            for h in range(4):
                mm(psb[4][:, h * 64:(h + 1) * 64], attm[:, h, :], vg[:, h * 64:(h + 1) * 64], first, False, sgc=True)
                first = False
            mm(psb[4][:, 0:256], qTc[:, 0, :], st_b[:, 0, :], False, False, sgc=True)
            mm(psb[5][:, 256:512], kgc[:, 0, :], vg, True, True)
            tt("dve", tS, psb[5][:, 256:512], bd256, ALU.mult)
            tt("dve", tS2, tS, st_f, ALU.add)
            ts("dve", st_f, tS2, ebc[:, 0:1], None, ALU.mult)
            copy("dve", st_b[:, 1, :], st_f)
            mm(psb[4][:, 0:256], qTc[:, 1, :], st_b[:, 1, :], False, True, sgc=True)
            mm(psb[5][:, 256:512], kgc[:, 1, :], vg, True, True)
            tt("dve", tS, psb[5][:, 256:512], bd256, ALU.mult)
            tt("dve", tS2, tS, st_f, ALU.add)
            ts("dve", st_f, tS2, ebc[:, 1:2], None, ALU.mult)
            copy("dve", st_b[:, 0, :], st_f)
            copy("act", og, psb[4][:, 0:256])
            tt("dve", og2, og, og, ALU.mult)
            P.op("dve", lambda e: e.tensor_reduce(out=gms[:, 0:4], in_=og2.rearrange("p (h d) -> p h d", h=4),
                                                  axis=AX.X, op=ALU.add),
                 reads=[og2], writes=[gms[:, 0:4]])
            act(gms[:, 4:8], gms[:, 0:4], AF.Sqrt, scale=1.0 / 64, bias=EPS)
            recip(gms[:, 0:4], gms[:, 4:8])
            tt("dve", og2.rearrange("p (h d) -> p h d", h=4), og.rearrange("p (h d) -> p h d", h=4),
               gms[:, 0:4].unsqueeze(2).broadcast_to([128, 4, 64]), ALU.mult)
            tt("dve", og.rearrange("p (h d) -> p h d", h=4), og2.rearrange("p (h d) -> p h d", h=4),
               gng.unsqueeze(1).broadcast_to([128, 4, 64]), ALU.mult)
            mxt = mixt[t % 2]
            tt("dve", mxt[:, 0:256], og, sir, ALU.mult)
            act(guv, psb[6][:, 0:512], AF.Gelu_apprx_tanh)
            P.op("dve", lambda e: e.bn_stats(out=bns[:, 0:6], in_=guv[:, 256:512]), reads=[guv[:, 256:512]], writes=[bns[:, 0:6]])
            P.op("dve", lambda e: e.bn_aggr(out=bna[:, 0:2], in_=bns[:, 0:6]), reads=[bns[:, 0:6]], writes=[bna[:, 0:2]])
            act(bna[:, 2:3], bna[:, 1:2], AF.Sqrt, bias=EPS)
            recip(bna[:, 3:4], bna[:, 2:3])
            ts("dve", vn, guv[:, 256:512], bna[:, 0:1], bna[:, 3:4], ALU.subtract, ALU.mult)
            tt("dve", vn, vn, lng, ALU.mult)
            tt("dve", vnb, vn, lnb, ALU.add)
            for g in range(4):
                mm(psb[6][:, g * 64:(g + 1) * 64], sgWT[:, g, :], vnb[:, g * 64:(g + 1) * 64], True, True)
            for g in range(4):
                stt(mxt[:, 256 + g * 64: 256 + (g + 1) * 64], psb[6][:, g * 64:(g + 1) * 64], ptT[:, 16 + g:17 + g],
                    guv[:, g * 64:(g + 1) * 64], ALU.add, ALU.mult)
            dtoks.append(dma("pool", mix[t * 128:(t + 1) * 128, 512:1024], mxt))
        if dbg and NT == 32:
            dtoks.append(dma("pool", d_kslT[:, :], kslT))
            dtoks.append(dma("pool", d_vsl[:, :], vsl.rearrange("p a b c -> p (a b c)")))
            dtoks.append(dma("pool", d_gates[:, :], gates.rearrange("p a b -> p (a b)")))
        if stop_after == "A":
            return dtoks

        cc = Carver(w_off, TOT)
        w1d = [alloc(cc, [32, 256], BF16), alloc(cc, [32, 256], BF16)]
        w2b = alloc(cc, [2, 2, 64], BF16)
        posTb = alloc(cc, [64], BF16, parts=64)
        hbias = alloc(cc, [2, 2], F32)
        hidT = alloc(cc, [2, 255], BF16)
        kct = alloc(cc, [2, 128], F32)
        kcr = alloc(cc, [2, 128], BF16)
        cst = [alloc(cc, [8, 256], F32), alloc(cc, [8, 256], F32)]
        ci = 0
        for kv, nm in enumerate(["cmp_w1_k", "cmp_w1_v"]):
            src = Wd[nm].rearrange("(l d) h -> d l h", d=64)
            for part in range(2):
                for l0 in range(0, 32, 8):
                    s = cst[ci % 2]
                    ci += 1
                    dma("sp", s[64 * part:64 * part + 64], src[:, l0:l0 + 8, :])
                    copy("pool", w1d[kv][64 * part:64 * part + 64, l0:l0 + 8, :], s[64 * part:64 * part + 64])
        for kv, nm in enumerate(["cmp_w2_k", "cmp_w2_v"]):
            s = cst[ci % 2]
            ci += 1
            sv = s[:, 0, 0:128].rearrange("p (c d) -> p c d", c=2)
            dma("sp", sv, Wd[nm].rearrange("(c p) d -> p c d", p=128))
            copy("pool", w2b[:, kv, :, :], sv)
        copy("dve", posTb[0:64, :], ptT[0:64, 32:96])
        for kv in range(2):
            rawT = kcrT if kv == 0 else vcrT
            for c in range(2):
                for l in range(32):
                    mm(psb[7][:, 2 * kv + c: 2 * kv + c + 1], w1d[kv][0:64, l, c * 128:(c + 1) * 128],
                       posTb[0:64, kv * 32 + l: kv * 32 + l + 1], l == 0, l == 31)
            copy("dve", hbias[:, kv, :], psb[7][:, 2 * kv: 2 * kv + 2])
            for g in range(2):
                for c in range(2):
                    for l in range(32):
                        rv = rawT[64 * g:64 * g + 64, :].rearrange("p (c s) -> p s c", s=16)
                        rhs = rv[:, l, 0:255] if l < 16 else rv[:, l - 16, 1:256]
                        mm(psb[c + 2 * g][:, 0:255], w1d[kv][64 * g:64 * g + 64, l, c * 128:(c + 1) * 128], rhs, l == 0, l == 31)
                    act(hidT[:, c, :], psb[c + 2 * g][:, 0:255], AF.Gelu_apprx_tanh, bias=hbias[:, kv, c:c + 1])
                for (c0, cn, ch) in [(0, 128, 0), (128, 127, 1)]:
                    for c in range(2):
                        mm(psb[4 + ch][0:cn, g * 64:(g + 1) * 64], hidT[:, c, c0:c0 + cn], w2b[:, kv, c, :], c == 0, c == 1)
                    if kv == 0:
                        copy("dve", kct[0:cn, ch, g * 64:(g + 1) * 64], psb[4 + ch][0:cn, g * 64:(g + 1) * 64])
                    else:
                        copy("dve", vca[0:cn, ch, g, 0:64], psb[4 + ch][0:cn, g * 64:(g + 1) * 64])
        k5 = kct.rearrange("p c (g two d) -> p c g two d", g=2, two=2)
        o5 = kcr.rearrange("p c (g two d) -> p c g two d", g=2, two=2)
        ta = alloc(cc, [2, 2, 32], F32)
        tb = alloc(cc, [2, 2, 32], F32)
        for (cn, ch) in [(128, 0), (127, 1)]:
            a1 = k5[0:cn, ch, :, 0, :]
            a2 = k5[0:cn, ch, :, 1, :]
            cb_ = ccos[0:cn, ch, :].unsqueeze(1).broadcast_to([cn, 2, 32])
            sb_ = csin[0:cn, ch, :].unsqueeze(1).broadcast_to([cn, 2, 32])
            tt("dve", ta[0:cn, ch], a1, cb_, ALU.mult)
            tt("dve", tb[0:cn, ch], a2, sb_, ALU.mult)
            tt("dve", o5[0:cn, ch, :, 0, :], ta[0:cn, ch], tb[0:cn, ch], ALU.subtract)
            tt("dve", ta[0:cn, ch], a2, cb_, ALU.mult)
            tt("dve", tb[0:cn, ch], a1, sb_, ALU.mult)
            tt("dve", o5[0:cn, ch, :, 1, :], ta[0:cn, ch], tb[0:cn, ch], ALU.add)
            b6_ = bank_bf(6)
            trp(b6_[:, ch * 128: ch * 128 + cn], kcr[0:cn, ch, :], identb[0:cn, 0:cn])
            copy("dve", kcT[:, ch * 128: ch * 128 + cn], b6_[:, ch * 128: ch * 128 + cn])

        if dbg:
            dtoks.append(dma("pool", d_kcT[:, :], kcT))
            dtoks.append(dma("pool", d_vca[:, :], vca.rearrange("p a b c -> p (a b c)")))
        if stop_after == "K":
            return dtoks
        cb2 = Carver(cc.off, TOT)
        qt = [alloc(cb2, [2, 512], BF16), alloc(cb2, [2, 512], BF16)]
        memset("pool", qt[0], 0.0)
        memset("pool", qt[1], 0.0)
        ET = [alloc(cb2, [512], BF16) for _ in range(4)]
        et_i = [0]
        sc = alloc(cb2, [64], F32)
        sc2 = alloc(cb2, [64], F32)
        impa = alloc(cb2, [64], F32)
        m8 = alloc(cb2, [16], F32)
        mb = alloc(cb2, [128], BF16)
        memset("pool", mb, 0.0)
        mbT = alloc(cb2, [128], BF16)
        dn2 = [alloc(cb2, [3, 4], F32), alloc(cb2, [3, 4], F32)]
        sg2 = [alloc(cb2, [3, 4], F32), alloc(cb2, [3, 4], F32)]
        oacc = [alloc(cb2, [512], F32), alloc(cb2, [512], F32)]
        otmp2 = [alloc(cb2, [256], F32), alloc(cb2, [256], F32)]
        onb = [alloc(cb2, [512], BF16), alloc(cb2, [512], BF16)]
        sbank = [0]

        def score_bank():
            b = sbank[0] % 2
            sbank[0] += 1
            return psb[b]

        def next_et():
            e = ET[et_i[0] % 4]
            et_i[0] += 1
            return e

        pend = [None]

        def push(item):
            item["score"]()
            if pend[0] is not None:
                pend[0]["pv"]()
                for f in pend[0]["post"]:
                    f()
            pend[0] = item

        def flush():
            if pend[0] is not None:
                pend[0]["pv"]()
                for f in pend[0]["post"]:
                    f()
                pend[0] = None

        for t in range(NT):
            q_t = qt[t % 2]
            dma("sp", q_t[0:64, 0, :], qTs[t, 0:64, :])
            dma("sp", q_t[64:128, 1, :], qTs[t, 64:128, :])
            oa = oacc[t % 2]
            for w in range(2):
                par = (2 * t + w) % 2
                dn = dn2[par]
                sg_ = sg2[par]
                otmp = otmp2[par]
                qw = q_t[:, w, :]
                gsl = gates[:, t, :].rearrange("p (h b) -> p h b", b=3)[:, 4 * w:4 * w + 4, :]
                oav = oa[:, w * 256:(w + 1) * 256].rearrange("p (h d) -> p h d", h=4)
                use_sel = t >= 8
                ncv = min(8 * t + 7, 255)
                chs = [(0, 128, 0)] + ([(128, 127, 1)] if ncv > 128 else [])
                for ci_, (c0, cn, ch) in enumerate(chs):
                    st = {}

                    def c_score(c0=c0, cn=cn, ch=ch, st=st, qw=qw, t=t):
                        pb = score_bank()
                        mm(pb[0:cn, :], kcT[:, c0:c0 + cn], qw, True, False)
                        mm(pb[0:cn, :].rearrange("p (h q) -> p h q", h=4), identb[:, 0:cn],
                           cmask[:, ch, t * 128:(t + 1) * 128].unsqueeze(1).broadcast_to([128, 4, 128]), False, True)
                        e = next_et()
                        act(e[0:cn, :], pb[0:cn, :], AF.Exp)
                        st["e"] = e

                    def c_pv(cn=cn, ch=ch, st=st, w=w, ci_=ci_, nch=len(chs)):
                        e = st["e"]
                        for hp in range(2):
                            ob = psb[4 + hp][:, 0:258].rearrange("p (h c) -> p h c", h=2)
                            for hh in range(2):
                                h = hp * 2 + hh
                                mm(ob[:, hh, :], e[0:cn, h * 128:(h + 1) * 128], vca[0:cn, ch, w, :],
                                   ci_ == 0 and hh == 0, ci_ == nch - 1 and hh == 1, sgc=True)

                    posts = []
                    if ci_ == len(chs) - 1:
                        def c_post(t=t, w=w, dn=dn, sg_=sg_, oav=oav, gsl=gsl, use_sel=use_sel):
                            for hp in range(2):
                                ob = psb[4 + hp][:, 0:258].rearrange("p (h c) -> p h c", h=2)
                                ts("dve", dn[:, 0, 2 * hp:2 * hp + 2], ob[:, :, 64], 1e-30, None, ALU.max)
                            recip(dn[:, 0, :], dn[:, 0, :])
                            if use_sel:
                                for h in range(4):
                                    ob = psb[4 + h // 2][:, 0:258].rearrange("p (h c) -> p h c", h=2)
                                    if h == 0:
                                        ts("dve", impa, ob[:, 0, 65:129], dn[:, 0, 0:1], None, ALU.mult)
                                    else:
                                        stt(impa, ob[:, h % 2, 65:129], dn[:, 0, h:h + 1], impa, ALU.mult, ALU.add)
                                tt("dve", sc, impa, rtab[:, 62 - 2 * t: 62 - 2 * t + 64], ALU.add)
                                ts("dve", sc[:, 0:1], sc[:, 0:1], 1e4, None, ALU.add)
                                P.op("dve", lambda e: e.max(out=m8[:, 0:8], in_=sc), reads=[sc], writes=[m8[:, 0:8]])
                                P.op("dve", lambda e: e.match_replace(out=sc2, in_to_replace=m8[:, 0:8], in_values=sc,
                                                                      imm_value=-1e30),
                                     reads=[sc, m8[:, 0:8]], writes=[sc2])
                                P.op("dve", lambda e: e.max(out=m8[:, 8:16], in_=sc2), reads=[sc2], writes=[m8[:, 8:16]])
                                ts("dve", mb[:, 0:64], sc, m8[:, 15:16], 1.0, ALU.is_ge, ALU.subtract)
                                b6 = bank_bf(6)
                                trp(b6[:, 0:128], mb, identb)
                                copy("act", mbT, b6[:, 0:128])
                            tt("dve", sg_[:, 0, :], dn[:, 0, :], gsl[:, :, 0], ALU.mult)
                            for hp in range(2):
                                ob = psb[4 + hp][:, 0:258].rearrange("p (h c) -> p h c", h=2)
                                tt("dve", oav[:, 2 * hp:2 * hp + 2, :], ob[:, :, 0:64],
                                   sg_[:, 0, 2 * hp:2 * hp + 2].unsqueeze(2).broadcast_to([128, 2, 64]), ALU.mult)
                        posts.append(c_post)
                    push(dict(score=c_score, pv=c_pv, post=posts))
                for br in (2, 1):
                    kT = kslT if br == 1 else kwnT
                    vv = vsl if br == 1 else vwn
                    js = list(range(0, t + 1)) if br == 1 else list(range(max(0, t - 4), t + 1))
                    obank = 3 if br == 2 else (2 if par == 0 else 7)
                    ob = psb[obank][:, 0:260].rearrange("p (h c) -> p h c", h=4)
                    for ji, j in enumerate(js):
                        st = {}

                        def b_score(br=br, j=j, t=t, kT=kT, qw=qw, st=st, use_sel=use_sel):
                            pb = score_bank()
                            extra = []
                            if br == 1 and use_sel:
                                extra.append(("sel", None))
                            if j == t:
                                extra.append(("m", diagm))
                            if br == 2 and j == t - 4:
                                extra.append(("m", winm))
                            mm(pb[:, :], kT[:, j * 128:(j + 1) * 128], qw, True, len(extra) == 0)
                            for xi, (kind, mk_) in enumerate(extra):
                                lastx = xi == len(extra) - 1
                                if kind == "sel":
                                    mm(pb[:, :].rearrange("p (h q) -> p h q", h=4), eexp[:, j * 128:(j + 1) * 128],
                                       mbT.unsqueeze(1).broadcast_to([128, 4, 128]), False, lastx)
                                else:
                                    mm(pb[:, :], identb, mk_, False, lastx)
                            e = next_et()
                            act(e, pb[:, :], AF.Exp)
                            st["e"] = e

                        def b_pv(ob=ob, vv=vv, j=j, w=w, st=st, ji=ji, nj=len(js)):
                            e = st["e"]
                            for h in range(4):
                                mm(ob[:, h, :], e[:, h * 128:(h + 1) * 128], vv[:, j, w, :], ji == 0 and h == 0,
                                   (ji == nj - 1) and h == 3, sgc=True)

                        posts = []
                        if ji == len(js) - 1:
                            def b_post(br=br, ob=ob, dn=dn, sg_=sg_, otmp=otmp, oav=oav, gsl=gsl):
                                ts("dve", dn[:, br, :], ob[:, :, 64], 1e-30, None, ALU.max)
                                recip(dn[:, br, :], dn[:, br, :])
                                tt("dve", sg_[:, br, :], dn[:, br, :], gsl[:, :, br], ALU.mult)
                                ov_ = otmp.rearrange("p (h d) -> p h d", h=4)
                                tt("dve", ov_, ob[:, :, 0:64], sg_[:, br, :].unsqueeze(2).broadcast_to([128, 4, 64]), ALU.mult)
                                tt("dve", oav, oav, ov_, ALU.add)
                            posts.append(b_post)
                            if br == 1 and w == 1:
                                def t_post(t=t, oa=oa):
                                    ob_ = onb[t % 2]
                                    copy("act", ob_, oa)
                                    dtoks.append(dma("pool", mix[t * 128:(t + 1) * 128, 0:512], ob_))
                                posts.append(t_post)
                        push(dict(score=b_score, pv=b_pv, post=posts))
        flush()
        if stop_after == "B":
            return dtoks

        c3 = Carver(KEND, TOT)
        wg_b = alloc(c3, [8, DFF], BF16)
        wu_b = alloc(c3, [8, DFF], BF16)
        wd_b = alloc(c3, [22, D], BF16)
        wo_b = alloc(c3, [8, D], BF16)
        HW = DFF // 2
        stg3 = [alloc(c3, [HW], F32), alloc(c3, [HW], F32)]
        fng = alloc(c3, [D], F32) if is_final else None
        si = 0
        for k in range(8):
            s = stg3[si % 2]; si += 1
            dma("sp", s[:, 0:D], Wd["w_out"][k * 128:(k + 1) * 128, :])
            copy("pool", wo_b[:, k, :], s[:, 0:D])
        for (wsrc, wdst) in (("w_gate", wg_b), ("w_up", wu_b)):
            for k in range(8):
                for hh in range(2):
                    s = stg3[si % 2]; si += 1
                    dma("sp", s, Wd[wsrc][k * 128:(k + 1) * 128, hh * HW:(hh + 1) * HW])
                    ts("pool", wdst[:, k, hh * HW:(hh + 1) * HW], s, ptT[:, 8 + k:9 + k], 1.0, ALU.mult, ALU.mult)
        for c2 in range(22):
            s = stg3[si % 2]; si += 1
            dma("sp", s[:, 0:D], Wd["w_down"][c2 * 128:(c2 + 1) * 128, :])
            copy("pool", wd_b[:, c2, :], s[:, 0:D])
        if is_final:
            dma("sp", fng, fnorm_d.partition_broadcast(128))
        mxb = [alloc(c3, [D], BF16), alloc(c3, [D], BF16)]
        xb3 = [alloc(c3, [D], F32), alloc(c3, [D], F32)]
        x1b = alloc(c3, [D], F32)
        h3T = alloc(c3, [8, 128], BF16)
        mxT = h3T
        ss3 = alloc(c3, [4], F32)
        sil = [alloc(c3, [512], F32), alloc(c3, [512], F32)]
        aT_in = alloc(c3, [DFF], BF16)
        junk3 = aT_in[:, 0:D]
        aT = alloc(c3, [22, 128], BF16)
        h3 = aT[:, 0:8, :].rearrange("p c t -> p (c t)")

        toks = []
        for t in range(NT):
            mx = mxb[t % 2]
            xs = xb3[t % 2]
            dma("sp", mx, mix[t * 128:(t + 1) * 128, :])
            dma("sp", xs, x_src[t * 128:(t + 1) * 128, :])
            b0 = bank_bf(0)
            for k in range(8):
                trp(b0[:, k * 128:(k + 1) * 128], mx[:, k * 128:(k + 1) * 128], identb)
            copy("act", mxT, b0.rearrange("p (k t) -> p k t", k=8))
            for hf in range(2):
                for k in range(8):
                    mm(psb[6 + hf][:, :], mxT[:, k, :], wo_b[:, k, hf * 512:(hf + 1) * 512], k == 0, k == 7)
            for hf in range(2):
                tt("dve", x1b[:, hf * 512:(hf + 1) * 512], psb[6 + hf][:, :], xs[:, hf * 512:(hf + 1) * 512], ALU.add)
            act(junk3, x1b, AF.Square, accum=ss3[:, 0:1])
            act(ss3[:, 1:2], ss3[:, 0:1], AF.Sqrt, scale=1.0 / D, bias=EPS)
            recip(ss3[:, 2:3], ss3[:, 1:2])
            ts("dve", h3, x1b, ss3[:, 2:3], None, ALU.mult)
            b1 = bank_bf(1)
            for k in range(8):
                trp(b1[:, k * 128:(k + 1) * 128], h3[:, k * 128:(k + 1) * 128], identb)
            copy("act", h3T, b1.rearrange("p (k t) -> p k t", k=8))
            for n in range(6):
                n0 = n * 512
                nn = min(512, DFF - n0)
                pg = psb[2 + n % 2]
                pu = psb[4 + n % 2]
                for k in range(8):
                    mm(pg[:, 0:nn], h3T[:, k, :], wg_b[:, k, n0:n0 + nn], k == 0, k == 7)
                for k in range(8):
                    mm(pu[:, 0:nn], h3T[:, k, :], wu_b[:, k, n0:n0 + nn], k == 0, k == 7)
                sl = sil[n % 2]
                act(sl[:, 0:nn], pg[:, 0:nn], AF.Silu)
                tt("dve", aT_in[:, n0:n0 + nn], sl[:, 0:nn], pu[:, 0:nn], ALU.mult)
            for r in range(3):
                bb = bank_bf(r % 2)
                c_lo = r * 8
                c_hi = min(22, c_lo + 8)
                for c in range(c_lo, c_hi):
                    trp(bb[:, (c - c_lo) * 128:(c - c_lo + 1) * 128], aT_in[:, c * 128:(c + 1) * 128], identb)
                copy("act" if r != 1 else "dve", aT[:, c_lo:c_hi, :],
                     bb[:, 0:(c_hi - c_lo) * 128].rearrange("p (c t) -> p c t", c=c_hi - c_lo))
            for hf in range(2):
                for c in range(22):
                    mm(psb[6 + hf][:, :], aT[:, c, :], wd_b[:, c, hf * 512:(hf + 1) * 512], c == 0, c == 21)
            x2 = xs
            for hf in range(2):
                tt("dve", x2[:, hf * 512:(hf + 1) * 512], psb[6 + hf][:, :], x1b[:, hf * 512:(hf + 1) * 512], ALU.add)
            if is_final:
                act(junk3, x2, AF.Square, accum=ss3[:, 0:1])
                act(ss3[:, 1:2], ss3[:, 0:1], AF.Sqrt, scale=1.0 / D, bias=EPS)
                recip(ss3[:, 2:3], ss3[:, 1:2])
                stt(x2, x2, ss3[:, 2:3], fng, ALU.mult, ALU.mult)
            toks.append(dma("pool", x_dst[t * 128:(t + 1) * 128, :], x2))
        return toks + (dtoks if dbg else [])

    if stacked:
        xmid = [P.dram("xs%d" % i, [S, D], F32) for i in range(2)]
        toks = None
        for l in range(nlayers):
            P.group = l
            for n, _ in WNAMES:
                Wd[n] = Wfull[n][l]
            src = x_in if l == 0 else xmid[(l - 1) % 2]
            dst = y_out if l == nlayers - 1 else xmid[l % 2]
            toks = layer(src, dst, final and l == nlayers - 1)
    else:
        toks = layer(x_src, y_out, final)
    P.emit(final_waits=toks)
    nc._prog_stats = {e: len(P.instrs[e]) for e in P.ENGS}
    return nc


_CACHE = {}


def kernel(**inputs):
    x = np.ascontiguousarray(np.asarray(inputs["x"], dtype=np.float32))
    B = x.shape[0]
    cb, cf = _consts()
    if "nc" not in _CACHE:
        _CACHE["nc"] = build_layer(final=True, nlayers=DEPTH, stacked=True)
    nc = _CACHE["nc"]
    shared = {"cb": cb, "cf": cf,
              "final_norm": np.ascontiguousarray(np.asarray(inputs["final_norm"], dtype=np.float32))}
    for n, _ in WNAMES:
        shared[n] = np.ascontiguousarray(np.asarray(inputs[n], dtype=np.float32))
    in_maps = []
    for b in range(B):
        m = dict(shared)
        m["x"] = x[b]
        in_maps.append(m)
    res = run_bass_kernel_spmd(nc, in_maps, core_ids=list(range(B)))
    return np.stack([np.asarray(res.results[b]["y"], dtype=np.float32) for b in range(B)], axis=0)
```

```python
import numpy as np
import ml_dtypes
import concourse.bass as bass
import concourse.mybir as mybir
from concourse.bass_utils import run_bass_kernel_spmd

F32 = mybir.dt.float32
BF16 = mybir.dt.bfloat16
AF = mybir.ActivationFunctionType
ALU = mybir.AluOpType
AX = mybir.AxisListType

S = 4096
D = 1024
NT = S // 128
DIN = 2600
DFF = 2816
EPS = 1e-6
BIG = 30000.0
DEPTH = 4

_ISZ = {}


def isz(dt):
    k = str(dt)
    if k not in _ISZ:
        _ISZ[k] = mybir.dt.size(dt)
    return _ISZ[k]


class Prog:
    ENGS = ["pe", "act", "dve", "pool", "sp"]
    NDSEM = 8

    def __init__(self, nc):
        self.nc = nc
        self.instrs = {e: [] for e in self.ENGS}
        self.bpp = {}
        self.recs = {}
        self.clock = {e: {x: -1 for x in self.ENGS} for e in self.ENGS}
        self.iclock = {e: [] for e in self.ENGS}
        self.dma_cnt = {e: 0 for e in self.ENGS}
        self.dma_known = {e: set() for e in self.ENGS}
        self.marked = {e: set() for e in self.ENGS}
        self.ctx = []
        self.group = 0
        self.igroup = {e: [] for e in self.ENGS}

    def sbuf(self, name, shape, dt):
        t = self.nc.sbuf_tensor(name, list(shape), dt)
        h = t.__enter__()
        self.ctx.append(t)
        n = 1
        for s in shape[1:]:
            n *= s
        self.bpp[name] = n * isz(dt)
        return h

    def psum(self, name, shape, dt):
        t = self.nc.psum_tensor(name, list(shape), dt)
        h = t.__enter__()
        self.ctx.append(t)
        n = 1
        for s in shape[1:]:
            n *= s
        self.bpp[name] = n * isz(dt)
        return h

    def dram(self, name, shape, dt, kind="Internal"):
        t = self.nc.dram_tensor(name, list(shape), dt, kind=kind)
        self.bpp[name] = None
        return t.ap()

    def region(self, a):
        name = a.tensor.name
        sz = isz(a.dtype)
        bpp = self.bpp.get(name, None)
        ap = a.ap
        if bpp is None:
            lo = a.offset
            ext = 0
            for st, cn in ap:
                ext += (cn - 1) * abs(st)
            return (name, 0, 1, lo * sz, (lo + ext + 1) * sz)
        off = a.offset * sz
        p0 = off // bpp
        lo = off % bpp
        if name.startswith("pb"):
            q0 = (p0 // 32) * 32
            q1 = ((p0 + ap[0][1] + 31) // 32) * 32
            return (name, q0, q1, 0, bpp)
        ext = 0
        for st, cn in ap[1:]:
            ext += (cn - 1) * abs(st)
        return (name, p0, p0 + ap[0][1], lo, lo + (ext + 1) * sz)

    def op(self, eng, fn, reads=(), writes=(), dma=False):
        import os
        lim = int(os.environ.get("KMAXOPS", "0"))
        self.nops = getattr(self, "nops", 0) + 1
        if lim and self.nops > lim:
            return None
        if os.environ.get("KLOG"):
            import sys as _s
            f = _s._getframe(1)
            while f is not None and f.f_code.co_name not in ("layer", "build_layer"):
                f = f.f_back
            print("OP", self.nops, eng, f.f_lineno if f else -1)
        idx = len(self.instrs[eng])
        waits = {}
        myclk = self.clock[eng]

        def need(tok):
            if tok[0] == "dma":
                if tok in self.dma_known[eng]:
                    return
                self.dma_known[eng].add(tok)
                waits[tok] = True
            else:
                e2, i2 = tok
                if myclk[e2] >= i2:
                    return
                waits[tok] = True

        if dma:
            q = eng
            di = self.dma_cnt[q]
            self.dma_cnt[q] += 1
            mytok = ("dma", q, di)
            if di >= self.NDSEM:
                need(("dma", q, di - self.NDSEM))
        else:
            mytok = (eng, idx)

        for (aps, kind) in ((reads, "R"), (writes, "W")):
            for a in aps:
                name, p0, p1, lo, hi = self.region(a)
                lst = self.recs.get(name, [])
                keep = []
                for r in lst:
                    (rp0, rp1, rlo, rhi, rkind, rtok) = r
                    ov = not (rp1 <= p0 or p1 <= rp0 or rhi <= lo or hi <= rlo)
                    rr = name.startswith("pb") and rtok[0] != eng
                    if ov and (rkind == "W" or kind == "W" or rr) and rtok != mytok:
                        if rtok[0] != "dma" and rtok[0] == eng and not dma:
                            if eng != "pe":
                                need(rtok)
                        else:
                            need(rtok)
                    cov = rp0 >= p0 and rp1 <= p1 and rlo >= lo and rhi <= hi
                    if kind == "W" and cov:
                        continue
                    if kind == "R" and rkind == "R" and cov and rtok[0] == mytok[0] and rtok[0] != "dma":
                        continue
                    keep.append(r)
                keep.append((p0, p1, lo, hi, kind, mytok))
                self.recs[name] = keep

        red = {}
        dmaw = []
        for tok in waits:
            if tok[0] == "dma":
                dmaw.append(tok)
            else:
                red[tok[0]] = max(red.get(tok[0], -1), tok[1])
        for e2, i2 in red.items():
            self.marked[e2].add(i2)
            oc = self.iclock[e2][i2]
            for x in self.ENGS:
                if oc[x] > myclk[x]:
                    myclk[x] = oc[x]
            if i2 > myclk[e2]:
                myclk[e2] = i2
        snap = dict(myclk)
        if not dma:
            snap[eng] = max(snap[eng], idx - 1)
        self.iclock[eng].append(snap)
        self.igroup[eng].append(self.group)
        self.instrs[eng].append(dict(fn=fn, waits=red, dmaw=dmaw, dma=(mytok if dma else None)))
        return mytok

    def emit(self, final_waits=()):
        nc = self.nc
        ENGH = {"pe": "tensor", "act": "scalar", "dve": "vector", "pool": "gpsimd", "sp": "sync"}
        rank = {}
        ngroups = self.group + 1
        sems = {}
        semctx = []
        for e in self.ENGS:
            cnt = [0] * ngroups
            rank[e] = {}
            for i in sorted(self.marked[e]):
                g = self.igroup[e][i]
                cnt[g] += 1
                rank[e][i] = (g, cnt[g])
            sems[e] = []
            for g in range(ngroups):
                c = nc.semaphore("s_%s_%d" % (e, g))
                sems[e].append(c.__enter__())
                semctx.append(c)
        dsems = {}
        for q in self.ENGS:
            if self.dma_cnt[q] > 0:
                lst = []
                for k in range(self.NDSEM):
                    c = nc.semaphore("d_%s_%d" % (q, k))
                    lst.append(c.__enter__())
                    semctx.append(c)
                dsems[q] = lst
        blk = nc.Block()
        block = blk.__enter__()
        prog = self

        def mk(e):
            def body(eng):
                for idx, ins in enumerate(prog.instrs[e]):
                    for e2, i2 in ins["waits"].items():
                        g2, v2 = rank[e2][i2]
                        eng.wait_ge(sems[e2][g2], v2)
                    for (_, q, di) in ins["dmaw"]:
                        eng.wait_ge(dsems[q][di % prog.NDSEM], 16 * (di // prog.NDSEM + 1))
                    r = ins["fn"](eng)
                    if ins["dma"] is not None:
                        (_, q, di) = ins["dma"]
                        r.then_inc(dsems[q][di % prog.NDSEM], 16)
                    elif idx in rank[e]:
                        r.then_inc(sems[e][rank[e][idx][0]], 1)
                if e == "sp":
                    for tok in final_waits:
                        if tok is None:
                            continue
                        (_, q, di) = tok
                        eng.wait_ge(dsems[q][di % prog.NDSEM], 16 * (di // prog.NDSEM + 1))
            return body

        for e in self.ENGS:
            if len(self.instrs[e]) == 0 and not (e == "sp" and final_waits):
                continue
            getattr(block, ENGH[e])(mk(e))
        blk.__exit__(None, None, None)
        for c in reversed(semctx):
            c.__exit__(None, None, None)
        for c in reversed(self.ctx):
            c.__exit__(None, None, None)


CB_ID = 0
CB_DIAG = 128
CB_WIN = 640
CB_BD = 1152
CB_ONES = 1280
CB_EEXP = 1408
CB_CMASK = 5504
CB_VCA = 13696
CB_HM = 13696 + 516
CB_BD256 = CB_HM + 64
NCB = CB_BD256 + 256
CF_ID = 0
CF_BDM = 128
CF_TRIL = 256
CF_RTAB = 384
CF_COS = 576
CF_SIN = 1600
CF_CCOS = 2624
CF_CSIN = 2688
CF_CM = 2752
NCF = 2816


def _consts():
    p = np.arange(128)[:, None]
    f = np.arange(128)[None, :]
    cb = np.zeros((128, NCB), np.float32)
    cb[:, CB_ID:CB_ID + 128] = (p == f)
    diag = np.where(p <= f, 0.0, -BIG)
    win = np.where(p > f, 0.0, -BIG)
    cb[:, CB_DIAG:CB_DIAG + 512] = np.tile(diag, (1, 4))
    cb[:, CB_WIN:CB_WIN + 512] = np.tile(win, (1, 4))
    bd = ((p // 64 == f // 64) & (p <= f)).astype(np.float32)
    cb[:, CB_BD:CB_BD + 128] = bd
    cb[:, CB_ONES:CB_ONES + 128] = 1.0
    m = np.arange(4096)[None, :]
    cb[:, CB_EEXP:CB_EEXP + 4096] = np.where((m // 64) == p, BIG, 0.0)
    for ch in range(2):
        c = ch * 128 + p
        cb[:, CB_CMASK + ch * 4096:CB_CMASK + (ch + 1) * 4096] = np.where(16 * c + 31 <= m, 0.0, -BIG)
    ncmp = 255
    cs = np.arange(ncmp) * 16
    ce = cs + 32
    bs = np.arange(64) * 64
    be = bs + 64
    ov = np.clip(np.minimum(ce[:, None], be[None, :]) - np.maximum(cs[:, None], bs[None, :]), 0, None) / 32.0
    vca = np.zeros((128, 2, 2, 129), np.float32)
    for ch in range(2):
        for pp in range(128):
            c = ch * 128 + pp
            if c < ncmp:
                vca[pp, ch, :, 64] = 1.0
                vca[pp, ch, :, 65:129] = ov[c][None, :]
    cb[:, CB_VCA:CB_VCA + 516] = vca.reshape(128, 516)
    for h in range(4):
        cb[:, CB_HM + h] = (np.arange(128) // 32 == h)
    cb[:, CB_BD256:CB_BD256 + 256] = ((np.arange(128)[:, None] // 32) == (np.arange(256)[None, :] // 64))

    cf = np.zeros((128, NCF), np.float32)
    cf[:, CF_ID:CF_ID + 128] = (p == f)
    cf[:, CF_BDM:CF_BDM + 128] = -bd / 16.0
    cf[:, CF_TRIL:CF_TRIL + 128] = (f <= p)
    j = np.arange(192)[None, :]
    mm = j - 62
    tbrel = (p >= 64).astype(np.int64)
    r = np.zeros((128, 192), np.float32)
    r[(mm == tbrel) | (mm == tbrel - 1)] = 1e4
    r[mm > tbrel] = -1e30
    cf[:, CF_RTAB:CF_RTAB + 192] = r
    half = 32
    inv = (1.0 / (np.float32(10000.0) ** (np.arange(half, dtype=np.float32) / np.float32(half)))).astype(np.float32)
    pos = (np.arange(32)[None, :] * 128 + np.arange(128)[:, None]).astype(np.float32)
    ang = (pos[:, :, None] * inv[None, None, :]).astype(np.float32)
    cf[:, CF_COS:CF_COS + 1024] = np.cos(ang).astype(np.float32).reshape(128, 1024)
    cf[:, CF_SIN:CF_SIN + 1024] = np.sin(ang).astype(np.float32).reshape(128, 1024)
    cpos = ((np.arange(2)[None, :] * 128 + np.arange(128)[:, None]) * 16 + 31).astype(np.float32)
    cang = (cpos[:, :, None] * inv[None, None, :]).astype(np.float32)
    cf[:, CF_CCOS:CF_CCOS + 64] = np.cos(cang).astype(np.float32).reshape(128, 64)
    cf[:, CF_CSIN:CF_CSIN + 64] = np.sin(cang).astype(np.float32).reshape(128, 64)
    for c in range(2):
        cf[:, CF_CM + c] = (np.arange(128) // 64 == c)
    return cb.astype(ml_dtypes.bfloat16), cf


WNAMES = [("attn_norm", [D]), ("w_in", [D, DIN]), ("cmp_pos_k", [32, 64]), ("cmp_w1_k", [2048, 256]),
          ("cmp_w2_k", [256, 64]), ("cmp_pos_v", [32, 64]), ("cmp_w1_v", [2048, 256]), ("cmp_w2_v", [256, 64]),
          ("gla_w_up", [16, 128]), ("gla_b_up", [128]), ("gla_norm", [64]), ("sg_ln_g", [256]), ("sg_ln_b", [256]),
          ("sg_w", [4, 128, 128]), ("sg_b", [4, 128]), ("w_out", [D, D]), ("ffn_norm", [D]),
          ("w_gate", [D, DFF]), ("w_up", [D, DFF]), ("w_down", [DFF, D])]


def build_layer(final=False, dbg=False, stop_after="C", nlayers=1, stacked=False):
    nc = bass.Bass("TRN2", target_bir_lowering=False)
    P = Prog(nc)
    x_in = P.dram("x", [S, D], F32, kind="ExternalInput")
    cb_d = P.dram("cb", [128, NCB], BF16, kind="ExternalInput")
    cf_d = P.dram("cf", [128, NCF], F32, kind="ExternalInput")
    if stacked:
        Wfull = {n: P.dram(n, [DEPTH] + s, F32, kind="ExternalInput") for n, s in WNAMES}
        Wd = {}
    else:
        Wd = {n: P.dram(n, s, F32, kind="ExternalInput") for n, s in WNAMES}
    fnorm_d = P.dram("final_norm", [D], F32, kind="ExternalInput")
    y_out = P.dram("y", [S, D], F32, kind="ExternalOutput")
    sk = "ExternalOutput" if dbg else "Internal"
    qTs = P.dram("qTs", [NT, 128, 512], BF16, kind=sk)
    mix = P.dram("mixs", [S, D], BF16, kind=sk)
    if dbg:
        d_kcT = P.dram("d_kcT", [128, 256], BF16, kind="ExternalOutput")
        d_vca = P.dram("d_vca", [128, 516], BF16, kind="ExternalOutput")
        d_kslT = P.dram("d_kslT", [128, 4096], BF16, kind="ExternalOutput")
        d_vsl = P.dram("d_vsl", [128, 32 * 2 * 65], BF16, kind="ExternalOutput")
        d_gates = P.dram("d_gates", [128, 32 * 24], F32, kind="ExternalOutput")
    dtoks = []

    ARENA = 104448
    arena = P.sbuf("arena", [128, ARENA], BF16)
    psb = [P.psum("pb%d" % i, [128, 512], F32) for i in range(8)]

    class Carver:
        def __init__(self, base, limit):
            self.off = base
            self.limit = limit

        def take(self, nbytes):
            nbytes = (nbytes + 63) // 64 * 64
            o = self.off
            self.off += nbytes
            assert self.off <= self.limit, (self.off, self.limit)
            return o

    def view(off_bytes, shape, dt, parts=128):
        n = 1
        for s in shape:
            n *= s
        if dt == BF16:
            a = arena[0:parts, off_bytes // 2: off_bytes // 2 + n]
        else:
            a = arena[0:parts, off_bytes // 2: off_bytes // 2 + n * 2].bitcast(F32)
        if len(shape) == 1:
            return a
        names = " ".join("d%d" % i for i in range(len(shape)))
        kw = {"d%d" % i: shape[i] for i in range(len(shape))}
        return a.rearrange("p (%s) -> p %s" % (names, names), **kw)

    def alloc(cv, shape, dt, parts=128):
        n = 1
        for s in shape:
            n *= s
        return view(cv.take(n * isz(dt)), shape, dt, parts)

    TOT = ARENA * 2
    cvK = Carver(0, 8 * 1024)
    identb = alloc(cvK, [128], BF16)
    identf = alloc(cvK, [128], F32)
    diagm = alloc(cvK, [512], BF16)
    winm = alloc(cvK, [512], BF16)
    bdmask = alloc(cvK, [128], BF16)
    onesb = alloc(cvK, [128], BF16)
    bdmf = alloc(cvK, [128], F32)
    trilf = alloc(cvK, [128], F32)
    rtab = alloc(cvK, [192], F32)
    ptT = alloc(cvK, [128], F32)
    gng = alloc(cvK, [64], F32)
    bupb = alloc(cvK, [128], BF16)
    wupb = alloc(cvK, [128], BF16)
    hm4 = alloc(cvK, [4], BF16)
    bd256 = alloc(cvK, [256], BF16)
    cm2 = alloc(cvK, [2], F32)
    KEND = cvK.off

    def dma(q, out, in_):
        return P.op(q, lambda e: e.dma_start(out=out, in_=in_), reads=[in_], writes=[out], dma=True)

    def act(out, in_, func, scale=None, bias=None, accum=None):
        kw = {}
        rd = [in_]
        wr = [out]
        if scale is not None:
            kw["scale"] = scale
            if not isinstance(scale, (int, float)):
                rd.append(scale)
        if bias is not None:
            kw["bias"] = bias
            if not isinstance(bias, (int, float)):
                rd.append(bias)
        if accum is not None:
            kw["accum_out"] = accum
            wr.append(accum)
        return P.op("act", lambda e: e.activation(out=out, in_=in_, func=func, **kw), reads=rd, writes=wr)

    def mm(out, lhsT, rhs, start, stop, tp=None, sgc=False):
        kw = {}
        if tp is not None:
            kw["tile_position"] = tp
        if sgc:
            kw["skip_group_check"] = True
        return P.op("pe", lambda e: e.matmul(out, lhsT=lhsT, rhs=rhs, start=start, stop=stop, **kw),
                    reads=[lhsT, rhs], writes=[out])

    def trp(out, in_, ident):
        return P.op("pe", lambda e: e.transpose(out, in_, ident), reads=[in_, ident], writes=[out])

    def tt(eng, out, in0, in1, op):
        return P.op(eng, lambda e: e.tensor_tensor(out=out, in0=in0, in1=in1, op=op), reads=[in0, in1], writes=[out])

    def ts(eng, out, in0, s1, s2, op0, op1=None):
        rd = [in0]
        if not isinstance(s1, (int, float)):
            rd.append(s1)
        if s2 is not None and not isinstance(s2, (int, float)):
            rd.append(s2)
        if op1 is None:
            return P.op(eng, lambda e: e.tensor_scalar(out=out, in0=in0, scalar1=s1, scalar2=None, op0=op0),
                        reads=rd, writes=[out])
        return P.op(eng, lambda e: e.tensor_scalar(out=out, in0=in0, scalar1=s1, scalar2=s2, op0=op0, op1=op1),
                    reads=rd, writes=[out])

    def stt(out, in0, sc, in1, op0, op1):
        rd = [in0, in1]
        if not isinstance(sc, (int, float)):
            rd.append(sc)
        return P.op("dve", lambda e: e.scalar_tensor_tensor(out=out, in0=in0, scalar=sc, in1=in1, op0=op0, op1=op1),
                    reads=rd, writes=[out])

    def copy(eng, out, in_):
        if eng == "act":
            return act(out, in_, AF.Copy)
        return P.op(eng, lambda e: e.tensor_copy(out=out, in_=in_), reads=[in_], writes=[out])

    def memset(eng, out, val):
        return P.op(eng, lambda e: e.memset(out, val), reads=[], writes=[out])

    def recip(out, in_):
        return P.op("dve", lambda e: e.reciprocal(out=out, in_=in_), reads=[in_], writes=[out])

    def bank_bf(i):
        return psb[i][:].bitcast(BF16)

    dma("sp", identb, cb_d[:, CB_ID:CB_ID + 128])
    dma("sp", diagm, cb_d[:, CB_DIAG:CB_DIAG + 512])
    dma("sp", winm, cb_d[:, CB_WIN:CB_WIN + 512])
    dma("sp", bdmask, cb_d[:, CB_BD:CB_BD + 128])
    dma("sp", onesb, cb_d[:, CB_ONES:CB_ONES + 128])
    dma("sp", identf, cf_d[:, CF_ID:CF_ID + 128])
    dma("sp", bdmf, cf_d[:, CF_BDM:CF_BDM + 128])
    dma("sp", trilf, cf_d[:, CF_TRIL:CF_TRIL + 128])
    dma("sp", rtab, cf_d[:, CF_RTAB:CF_RTAB + 192])
    dma("sp", hm4, cb_d[:, CB_HM:CB_HM + 4])
    dma("sp", bd256, cb_d[:, CB_BD256:CB_BD256 + 256])
    dma("sp", cm2, cf_d[:, CF_CM:CF_CM + 2])

    x_src = x_in
    fins = []

    def layer(x_src, x_dst, is_final):
        cv = Carver(KEND, TOT)
        cmask = alloc(cv, [2, 4096], BF16)
        eexp = alloc(cv, [4096], BF16)
        cosT = alloc(cv, [32, 32], F32)
        sinT = alloc(cv, [32, 32], F32)
        ccos = alloc(cv, [2, 32], F32)
        csin = alloc(cv, [2, 32], F32)
        gates = alloc(cv, [32, 24], F32)
        kslT = alloc(cv, [4096], BF16)
        kwnT = alloc(cv, [4096], BF16)
        vsl = alloc(cv, [32, 2, 65], BF16)
        vwn = alloc(cv, [32, 2, 65], BF16)
        kcrT = alloc(cv, [4096], BF16)
        vcrT = alloc(cv, [4096], BF16)
        kcT = alloc(cv, [256], BF16)
        vca = alloc(cv, [2, 2, 129], BF16)
        lng = alloc(cv, [256], F32)
        lnb = alloc(cv, [256], F32)
        sgWT = alloc(cv, [4, 128], BF16)
        st_f = alloc(cv, [256], F32)
        st_b = alloc(cv, [2, 256], BF16)
        w_off = cv.off
        win_b = alloc(cv, [8, DIN], BF16)
        wk_off = cv.off

        dma("sp", cmask, cb_d[:, CB_CMASK:CB_CMASK + 8192].rearrange("p (a b) -> p a b", a=2))
        dma("sp", eexp, cb_d[:, CB_EEXP:CB_EEXP + 4096])
        dma("sp", cosT, cf_d[:, CF_COS:CF_COS + 1024].rearrange("p (a b) -> p a b", a=32))
        dma("sp", sinT, cf_d[:, CF_SIN:CF_SIN + 1024].rearrange("p (a b) -> p a b", a=32))
        dma("sp", ccos, cf_d[:, CF_CCOS:CF_CCOS + 64].rearrange("p (a b) -> p a b", a=2))
        dma("sp", csin, cf_d[:, CF_CSIN:CF_CSIN + 64].rearrange("p (a b) -> p a b", a=2))
        dma("sp", vca, cb_d[:, CB_VCA:CB_VCA + 516].rearrange("p (a b c) -> p a b c", a=2, b=2))
        dma("sp", gng, Wd["gla_norm"].partition_broadcast(128))
        dma("sp", lng, Wd["sg_ln_g"].partition_broadcast(128))
        dma("sp", lnb, Wd["sg_ln_b"].partition_broadcast(128))

        cw = Carver(wk_off, TOT)
        stg = [alloc(cw, [DFF], F32), alloc(cw, [DFF], F32)]
        stg_i = [0]

        def stage():
            s = stg[stg_i[0] % 2]
            stg_i[0] += 1
            return s

        pt = stage()[:, 0:128]
        memset("pool", pt, 0.0)
        dma("sp", pt[0:8, :], Wd["attn_norm"].rearrange("(k p) -> k p", p=128))
        dma("sp", pt[8:16, :], Wd["ffn_norm"].rearrange("(k p) -> k p", p=128))
        dma("sp", pt[16:20, :], Wd["sg_b"])
        dma("sp", pt[32:64, 0:64], Wd["cmp_pos_k"])
        dma("sp", pt[64:96, 0:64], Wd["cmp_pos_v"])
        trp(psb[7][:, 0:128], pt, identf)
        copy("dve", ptT, psb[7][:, 0:128])
        s2 = stage()
        dma("sp", s2[0:16, 0:128], Wd["gla_w_up"])
        dma("sp", s2[0:1, 128:256], Wd["gla_b_up"].rearrange("(o n) -> o n", o=1))
        copy("pool", wupb[0:16, :], s2[0:16, 0:128])
        copy("pool", bupb[0:1, :], s2[0:1, 128:256])
        s3 = stage()
        s3v = s3[:, 0:512].rearrange("p (g j) -> p g j", g=4)
        dma("sp", s3v, Wd["sg_w"].rearrange("g i j -> i g j"))
        sgm = alloc(cw, [4, 128], BF16)
        tt("dve", sgm, s3v, trilf.unsqueeze(1).broadcast_to([128, 4, 128]), ALU.mult)
        b7b = bank_bf(7)
        for g in range(4):
            trp(b7b[:, 256 + g * 128: 256 + (g + 1) * 128], sgm[:, g, :], identb)
        copy("dve", sgWT, b7b[:, 256:768].rearrange("p (g i) -> p g i", g=4))
        for k in range(8):
            s = stage()
            dma("sp", s[:, 0:DIN], Wd["w_in"][k * 128:(k + 1) * 128, :])
            ts("pool", win_b[:, k, :], s[:, 0:DIN], ptT[:, k:k + 1], 1.0, ALU.mult, ALU.mult)
        memset("pool", vsl, 1.0)
        memset("pool", vwn, 1.0)
        memset("pool", st_f, 0.0)
        memset("pool", st_b, 0.0)
        if NT != 32:
            memset("pool", kcrT, 0.0)
            memset("pool", vcrT, 0.0)

        xbuf = [alloc(cw, [D], F32), alloc(cw, [D], F32)]
        hbf = alloc(cw, [D], BF16)
        junk = hbf
        hT = [alloc(cw, [8, 128], BF16), alloc(cw, [8, 128], BF16)]
        ssq = alloc(cw, [4], F32)
        ri = alloc(cw, [12, 64], F32)
        ro = alloc(cw, [768], BF16)
        rt1 = alloc(cw, [12, 32], F32)
        rt2 = alloc(cw, [12, 32], F32)
        qTst = [alloc(cw, [512], BF16), alloc(cw, [512], BF16)]
        rawb = alloc(cw, [256], BF16)
        alT = alloc(cw, [128], BF16)
        e1 = alloc(cw, [128], F32)
        spl = alloc(cw, [128], F32)
        epn = alloc(cw, [2, 128], F32)
        bcs = alloc(cw, [128], F32)
        qkg = alloc(cw, [2, 128], BF16)
        kgT = alloc(cw, [128], BF16)
        qTm = alloc(cw, [4, 128], BF16)
        qTc = alloc(cw, [2, 128], BF16)
        kgc = alloc(cw, [2, 128], BF16)
        tS2 = alloc(cw, [256], F32)
        memset("pool", qTc, 0.0)
        vg = alloc(cw, [256], BF16)
        attm = alloc(cw, [4, 128], BF16)
        ebc = alloc(cw, [2], F32)
        tS = alloc(cw, [256], F32)
        og = alloc(cw, [256], F32)
        og2 = alloc(cw, [256], F32)
        gms = alloc(cw, [8], F32)
        sir = alloc(cw, [256], F32)
        mixt = [alloc(cw, [512], BF16), alloc(cw, [512], BF16)]
        guv = alloc(cw, [512], F32)
        bns = alloc(cw, [8], F32)
        bna = alloc(cw, [4], F32)
        vn = alloc(cw, [256], F32)
        vnb = alloc(cw, [256], BF16)

        for t in range(NT):
            xs = xbuf[t % 2]
            hTt = hT[t % 2]
            dma("sp", xs, x_src[t * 128:(t + 1) * 128, :])
            act(junk, xs, AF.Square, accum=ssq[:, 0:1])
            act(ssq[:, 1:2], ssq[:, 0:1], AF.Sqrt, scale=1.0 / D, bias=EPS)
            recip(ssq[:, 2:3], ssq[:, 1:2])
            ts("dve", hbf, xs, ssq[:, 2:3], None, ALU.mult)
            b0 = bank_bf(0)
            for k in range(8):
                trp(b0[:, k * 128:(k + 1) * 128], hbf[:, k * 128:(k + 1) * 128], identb)
            copy("act", hTt, b0.rearrange("p (k t) -> p k t", k=8))
            chunks = [(0, 512, 1), (512, 1024, 2), (1024, 1432, 3), (1432, 1832, 4), (1832, 2088, 5), (2088, 2600, 6)]
            for (c0, c1, bk) in chunks:
                for k in range(8):
                    mm(psb[bk][:, 0:c1 - c0], hTt[:, k, :], win_b[:, k, c0:c1], k == 0, k == 7)
            for k in range(8):
                mm(psb[7][0:16, 256:384], win_b[:, k, 1816:1832], hTt[:, k, :], k == 0, k == 7)
            riv = ri.rearrange("p (w r) d -> p w r d", w=2)
            act(riv[:, :, 0:4, :], psb[1][:, 0:512].rearrange("p (w r d) -> p w r d", w=2, r=4), AF.Copy, scale=0.125)
            copy("dve", riv[:, :, 4, :], psb[2][:, 256:384].rearrange("p (w d) -> p w d", w=2))
            copy("dve", riv[:, :, 5, :], psb[3][:, 0:128].rearrange("p (w d) -> p w d", w=2))
            copy("act", rawb, psb[2][:, 0:256])
            copy("dve", vsl[:, t, :, 0:64], psb[2][:, 384:512].rearrange("p (w d) -> p w d", w=2))
            copy("dve", vwn[:, t, :, 0:64], psb[3][:, 128:256].rearrange("p (w d) -> p w d", w=2))
            act(gates[:, t, :], psb[3][:, 256:280], AF.Sigmoid)
            act(sir, psb[5][:, 0:256], AF.Silu)
            r4 = ri.rearrange("p (w r) (two d) -> p w r two d", w=2, two=2)
            x1 = r4[:, :, :, 0, :]
            x2 = r4[:, :, :, 1, :]
            cs_ = cosT[:, t, :].unsqueeze(1).unsqueeze(1).broadcast_to([128, 2, 6, 32])
            sn_ = sinT[:, t, :].unsqueeze(1).unsqueeze(1).broadcast_to([128, 2, 6, 32])
            t1 = rt1.rearrange("p (w r) d -> p w r d", w=2)
            t2 = rt2.rearrange("p (w r) d -> p w r d", w=2)
            rov = ro.rearrange("p (r w two d) -> p w r two d", r=6, w=2, two=2)
            tt("dve", t1, x1, cs_, ALU.mult)
            tt("dve", t2, x2, sn_, ALU.mult)
            tt("dve", rov[:, :, :, 0, :], t1, t2, ALU.subtract)
            tt("dve", t1, x2, cs_, ALU.mult)
            tt("dve", t2, x1, sn_, ALU.mult)
            tt("dve", rov[:, :, :, 1, :], t1, t2, ALU.add)
            b1 = bank_bf(1)
            for r in range(6):
                trp(b1[:, r * 128:(r + 1) * 128], ro[:, r * 128:(r + 1) * 128], identb)
            qst = qTst[t % 2]
            copy("act", qst, b1[:, 0:512])
            dma("pool", qTs[t], qst)
            copy("dve", kslT[:, t * 128:(t + 1) * 128], b1[:, 512:640])
            copy("dve", kwnT[:, t * 128:(t + 1) * 128], b1[:, 640:768])
            b2 = bank_bf(2)
            trp(b2[:, 0:128], rawb[:, 0:128], identb)
            trp(b2[:, 128:256], rawb[:, 128:256], identb)
            copy("act", kcrT[:, t * 128:(t + 1) * 128], b2[:, 0:128])
            copy("act", vcrT[:, t * 128:(t + 1) * 128], b2[:, 128:256])
            copy("dve", alT[0:16, :], psb[7][0:16, 256:384])
            mm(psb[7][:, 0:128], alT[0:16, :], wupb[0:16, :], True, False)
            mm(psb[7][:, 0:128], onesb[0:1, :], bupb[0:1, :], False, True)
            act(e1, psb[7][:, 0:128], AF.Exp, scale=-1.0)
            act(spl, e1, AF.Ln, bias=1.0)
            mm(psb[7][:, 128:256], bdmf, spl, True, True)
            act(epn[:, 0, :], psb[7][:, 128:256], AF.Exp)
            act(epn[:, 1, :], psb[7][:, 128:256], AF.Exp, scale=-1.0)
            copy("dve", bcs, psb[7][:, 128:256])
            stt(qkg[:, 0, :], psb[3][:, 280:408], 32.0 ** -0.5, epn[:, 0, :], ALU.mult, ALU.mult)
            tt("dve", qkg[:, 1, :], psb[4][:, 0:128], epn[:, 1, :], ALU.mult)
            copy("act", vg, psb[4][:, 128:384])
            trp(b2[:, 256:384], qkg[:, 0, :], identb)
            trp(b2[:, 384:512], qkg[:, 1, :], identb)
            copy("act", kgT, b2[:, 384:512])
            tt("dve", qTm, b2[:, 256:384].unsqueeze(1).broadcast_to([128, 4, 128]),
               hm4.unsqueeze(2).broadcast_to([128, 4, 128]), ALU.mult)
            copy("dve", qTc[:, 0, 0:64], b2[:, 256:320])
            copy("dve", qTc[:, 1, 64:128], b2[:, 320:384])
            tt("dve", kgc, qkg[:, 1, :].unsqueeze(1).broadcast_to([128, 2, 128]),
               cm2.unsqueeze(2).broadcast_to([128, 2, 128]), ALU.mult)
            trp(psb[7][:, 384:512], bcs, identf)
            act(ebc[:, 0:1], psb[7][:, 447:448], AF.Exp)
            act(ebc[:, 1:2], psb[7][:, 511:512], AF.Exp)
            mm(psb[3][:, 0:512], kgT, qTm.rearrange("p h i -> p (h i)"), True, True)
            tt("dve", attm, psb[3][:, 0:512].rearrange("p (h i) -> p h i", h=4),
               bdmask.unsqueeze(1).broadcast_to([128, 4, 128]), ALU.mult)
            first = True
            for h in range(4):
                mm(psb[4][:, h * 64:(h + 1) * 64], attm[:, h, :], vg[:, h * 64:(h + 1) * 64], first, False, sgc=True)
                first = False
            mm(psb[4][:, 0:256], qTc[:, 0, :], st_b[:, 0, :], False, False, sgc=True)
            mm(psb[5][:, 256:512], kgc[:, 0, :], vg, True, True)
            tt("dve", tS, psb[5][:, 256:512], bd256, ALU.mult)
            tt("dve", tS2, tS, st_f, ALU.add)
            ts("dve", st_f, tS2, ebc[:, 0:1], None, ALU.mult)
            copy("dve", st_b[:, 1, :], st_f)
            mm(psb[4][:, 0:256], qTc[:, 1, :], st_b[:, 1, :], False, True, sgc=True)
            mm(psb[5][:, 256:512], kgc[:, 1, :], vg, True, True)
            tt("dve", tS, psb[5][:, 256:512], bd256, ALU.mult)
            tt("dve", tS2, tS, st_f, ALU.add)
            ts("dve", st_f, tS2, ebc[:, 1:2], None, ALU.mult)
            copy("dve", st_b[:, 0, :], st_f)
            copy("act", og, psb[4][:, 0:256])
            tt("dve", og2, og, og, ALU.mult)
            P.op("dve", lambda e: e.tensor_reduce(out=gms[:, 0:4], in_=og2.rearrange("p (h d) -> p h d", h=4),
                                                  axis=AX.X, op=ALU.add),
                 reads=[og2], writes=[gms[:, 0:4]])
            act(gms[:, 4:8], gms[:, 0:4], AF.Sqrt, scale=1.0 / 64, bias=EPS)
            recip(gms[:, 0:4], gms[:, 4:8])
            tt("dve", og2.rearrange("p (h d) -> p h d", h=4), og.rearrange("p (h d) -> p h d", h=4),
               gms[:, 0:4].unsqueeze(2).broadcast_to([128, 4, 64]), ALU.mult)
            tt("dve", og.rearrange("p (h d) -> p h d", h=4), og2.rearrange("p (h d) -> p h d", h=4),
               gng.unsqueeze(1).broadcast_to([128, 4, 64]), ALU.mult)
            mxt = mixt[t % 2]
            tt("dve", mxt[:, 0:256], og, sir, ALU.mult)
            act(guv, psb[6][:, 0:512], AF.Gelu_apprx_tanh)
            P.op("dve", lambda e: e.bn_stats(out=bns[:, 0:6], in_=guv[:, 256:512]), reads=[guv[:, 256:512]], writes=[bns[:, 0:6]])
            P.op("dve", lambda e: e.bn_aggr(out=bna[:, 0:2], in_=bns[:, 0:6]), reads=[bns[:, 0:6]], writes=[bna[:, 0:2]])
            act(bna[:, 2:3], bna[:, 1:2], AF.Sqrt, bias=EPS)
            recip(bna[:, 3:4], bna[:, 2:3])
            ts("dve", vn, guv[:, 256:512], bna[:, 0:1], bna[:, 3:4], ALU.subtract, ALU.mult)
            tt("dve", vn, vn, lng, ALU.mult)
            tt("dve", vnb, vn, lnb, ALU.add)
            for g in range(4):
                mm(psb[6][:, g * 64:(g + 1) * 64], sgWT[:, g, :], vnb[:, g * 64:(g + 1) * 64], True, True)
            for g in range(4):
                stt(mxt[:, 256 + g * 64: 256 + (g + 1) * 64], psb[6][:, g * 64:(g + 1) * 64], ptT[:, 16 + g:17 + g],
                    guv[:, g * 64:(g + 1) * 64], ALU.add, ALU.mult)
            dtoks.append(dma("pool", mix[t * 128:(t + 1) * 128, 512:1024], mxt))
        if dbg and NT == 32:
            dtoks.append(dma("pool", d_kslT[:, :], kslT))
            dtoks.append(dma("pool", d_vsl[:, :], vsl.rearrange("p a b c -> p (a b c)")))
            dtoks.append(dma("pool", d_gates[:, :], gates.rearrange("p a b -> p (a b)")))
        if stop_after == "A":
            return dtoks

        cc = Carver(w_off, TOT)
        w1d = [alloc(cc, [32, 256], BF16), alloc(cc, [32, 256], BF16)]
        w2b = alloc(cc, [2, 2, 64], BF16)
        posTb = alloc(cc, [64], BF16, parts=64)
        hbias = alloc(cc, [2, 2], F32)
        hidT = alloc(cc, [2, 255], BF16)
        kct = alloc(cc, [2, 128], F32)
        kcr = alloc(cc, [2, 128], BF16)
        cst = [alloc(cc, [8, 256], F32), alloc(cc, [8, 256], F32)]
        ci = 0
        for kv, nm in enumerate(["cmp_w1_k", "cmp_w1_v"]):
            src = Wd[nm].rearrange("(l d) h -> d l h", d=64)
            for part in range(2):
                for l0 in range(0, 32, 8):
                    s = cst[ci % 2]
                    ci += 1
                    dma("sp", s[64 * part:64 * part + 64], src[:, l0:l0 + 8, :])
                    copy("pool", w1d[kv][64 * part:64 * part + 64, l0:l0 + 8, :], s[64 * part:64 * part + 64])
        for kv, nm in enumerate(["cmp_w2_k", "cmp_w2_v"]):
            s = cst[ci % 2]
            ci += 1
            sv = s[:, 0, 0:128].rearrange("p (c d) -> p c d", c=2)
            dma("sp", sv, Wd[nm].rearrange("(c p) d -> p c d", p=128))
            copy("pool", w2b[:, kv, :, :], sv)
        copy("dve", posTb[0:64, :], ptT[0:64, 32:96])
        for kv in range(2):
            rawT = kcrT if kv == 0 else vcrT
            for c in range(2):
                for l in range(32):
                    mm(psb[7][:, 2 * kv + c: 2 * kv + c + 1], w1d[kv][0:64, l, c * 128:(c + 1) * 128],
                       posTb[0:64, kv * 32 + l: kv * 32 + l + 1], l == 0, l == 31)
            copy("dve", hbias[:, kv, :], psb[7][:, 2 * kv: 2 * kv + 2])
            for g in range(2):
                for c in range(2):
                    for l in range(32):
                        rv = rawT[64 * g:64 * g + 64, :].rearrange("p (c s) -> p s c", s=16)
                        rhs = rv[:, l, 0:255] if l < 16 else rv[:, l - 16, 1:256]
                        mm(psb[c + 2 * g][:, 0:255], w1d[kv][64 * g:64 * g + 64, l, c * 128:(c + 1) * 128], rhs, l == 0, l == 31)
                    act(hidT[:, c, :], psb[c + 2 * g][:, 0:255], AF.Gelu_apprx_tanh, bias=hbias[:, kv, c:c + 1])
                for (c0, cn, ch) in [(0, 128, 0), (128, 127, 1)]:
                    for c in range(2):
                        mm(psb[4 + ch][0:cn, g * 64:(g + 1) * 64], hidT[:, c, c0:c0 + cn], w2b[:, kv, c, :], c == 0, c == 1)
                    if kv == 0:
                        copy("dve", kct[0:cn, ch, g * 64:(g + 1) * 64], psb[4 + ch][0:cn, g * 64:(g + 1) * 64])
                    else:
                        copy("dve", vca[0:cn, ch, g, 0:64], psb[4 + ch][0:cn, g * 64:(g + 1) * 64])
        k5 = kct.rearrange("p c (g two d) -> p c g two d", g=2, two=2)
        o5 = kcr.rearrange("p c (g two d) -> p c g two d", g=2, two=2)
        ta = alloc(cc, [2, 2, 32], F32)
        tb = alloc(cc, [2, 2, 32], F32)
        for (cn, ch) in [(128, 0), (127, 1)]:
            a1 = k5[0:cn, ch, :, 0, :]
            a2 = k5[0:cn, ch, :, 1, :]
            cb_ = ccos[0:cn, ch, :].unsqueeze(1).broadcast_to([cn, 2, 32])
            sb_ = csin[0:cn, ch, :].unsqueeze(1).broadcast_to([cn, 2, 32])
            tt("dve", ta[0:cn, ch], a1, cb_, ALU.mult)
            tt("dve", tb[0:cn, ch], a2, sb_, ALU.mult)
            tt("dve", o5[0:cn, ch, :, 0, :], ta[0:cn, ch], tb[0:cn, ch], ALU.subtract)
            tt("dve", ta[0:cn, ch], a2, cb_, ALU.mult)
            tt("dve", tb[0:cn, ch], a1, sb_, ALU.mult)
            tt("dve", o5[0:cn, ch, :, 1, :], ta[0:cn, ch], tb[0:cn, ch], ALU.add)
            b6_ = bank_bf(6)
            trp(b6_[:, ch * 128: ch * 128 + cn], kcr[0:cn, ch, :], identb[0:cn, 0:cn])
            copy("dve", kcT[:, ch * 128: ch * 128 + cn], b6_[:, ch * 128: ch * 128 + cn])

        if dbg:
            dtoks.append(dma("pool", d_kcT[:, :], kcT))
            dtoks.append(dma("pool", d_vca[:, :], vca.rearrange("p a b c -> p (a b c)")))
        if stop_after == "K":
            return dtoks
        cb2 = Carver(cc.off, TOT)
        qt = [alloc(cb2, [2, 512], BF16), alloc(cb2, [2, 512], BF16)]
        memset("pool", qt[0], 0.0)
        memset("pool", qt[1], 0.0)
        ET = [alloc(cb2, [512], BF16) for _ in range(4)]
        et_i = [0]
        sc = alloc(cb2, [64], F32)
        sc2 = alloc(cb2, [64], F32)
        impa = alloc(cb2, [64], F32)
        m8 = alloc(cb2, [16], F32)
        mb = alloc(cb2, [128], BF16)
        memset("pool", mb, 0.0)
        mbT = alloc(cb2, [128], BF16)
        dn2 = [alloc(cb2, [3, 4], F32), alloc(cb2, [3, 4], F32)]
        sg2 = [alloc(cb2, [3, 4], F32), alloc(cb2, [3, 4], F32)]
        oacc = [alloc(cb2, [512], F32), alloc(cb2, [512], F32)]
        otmp2 = [alloc(cb2, [256], F32), alloc(cb2, [256], F32)]
        onb = [alloc(cb2, [512], BF16), alloc(cb2, [512], BF16)]
        sbank = [0]

        def score_bank():
            b = sbank[0] % 2
            sbank[0] += 1
            return psb[b]

        def next_et():
            e = ET[et_i[0] % 4]
            et_i[0] += 1
            return e

        pend = [None]

        def push(item):
            item["score"]()
            if pend[0] is not None:
                pend[0]["pv"]()
                for f in pend[0]["post"]:
                    f()
            pend[0] = item

        def flush():
            if pend[0] is not None:
                pend[0]["pv"]()
                for f in pend[0]["post"]:
                    f()
                pend[0] = None

        for t in range(NT):
            q_t = qt[t % 2]
            dma("sp", q_t[0:64, 0, :], qTs[t, 0:64, :])
            dma("sp", q_t[64:128, 1, :], qTs[t, 64:128, :])
            oa = oacc[t % 2]
            for w in range(2):
                par = (2 * t + w) % 2
                dn = dn2[par]
                sg_ = sg2[par]
                otmp = otmp2[par]
                qw = q_t[:, w, :]
                gsl = gates[:, t, :].rearrange("p (h b) -> p h b", b=3)[:, 4 * w:4 * w + 4, :]
                oav = oa[:, w * 256:(w + 1) * 256].rearrange("p (h d) -> p h d", h=4)
                use_sel = t >= 8
                ncv = min(8 * t + 7, 255)
                chs = [(0, 128, 0)] + ([(128, 127, 1)] if ncv > 128 else [])
                for ci_, (c0, cn, ch) in enumerate(chs):
                    st = {}

                    def c_score(c0=c0, cn=cn, ch=ch, st=st, qw=qw, t=t):
                        pb = score_bank()
                        mm(pb[0:cn, :], kcT[:, c0:c0 + cn], qw, True, False)
                        mm(pb[0:cn, :].rearrange("p (h q) -> p h q", h=4), identb[:, 0:cn],
                           cmask[:, ch, t * 128:(t + 1) * 128].unsqueeze(1).broadcast_to([128, 4, 128]), False, True)
                        e = next_et()
                        act(e[0:cn, :], pb[0:cn, :], AF.Exp)
                        st["e"] = e

                    def c_pv(cn=cn, ch=ch, st=st, w=w, ci_=ci_, nch=len(chs)):
                        e = st["e"]
                        for hp in range(2):
                            ob = psb[4 + hp][:, 0:258].rearrange("p (h c) -> p h c", h=2)
                            for hh in range(2):
                                h = hp * 2 + hh
                                mm(ob[:, hh, :], e[0:cn, h * 128:(h + 1) * 128], vca[0:cn, ch, w, :],
                                   ci_ == 0 and hh == 0, ci_ == nch - 1 and hh == 1, sgc=True)

                    posts = []
                    if ci_ == len(chs) - 1:
                        def c_post(t=t, w=w, dn=dn, sg_=sg_, oav=oav, gsl=gsl, use_sel=use_sel):
                            for hp in range(2):
                                ob = psb[4 + hp][:, 0:258].rearrange("p (h c) -> p h c", h=2)
                                ts("dve", dn[:, 0, 2 * hp:2 * hp + 2], ob[:, :, 64], 1e-30, None, ALU.max)
                            recip(dn[:, 0, :], dn[:, 0, :])
                            if use_sel:
                                for h in range(4):
                                    ob = psb[4 + h // 2][:, 0:258].rearrange("p (h c) -> p h c", h=2)
                                    if h == 0:
                                        ts("dve", impa, ob[:, 0, 65:129], dn[:, 0, 0:1], None, ALU.mult)
                                    else:
                                        stt(impa, ob[:, h % 2, 65:129], dn[:, 0, h:h + 1], impa, ALU.mult, ALU.add)
                                tt("dve", sc, impa, rtab[:, 62 - 2 * t: 62 - 2 * t + 64], ALU.add)
                                ts("dve", sc[:, 0:1], sc[:, 0:1], 1e4, None, ALU.add)
                                P.op("dve", lambda e: e.max(out=m8[:, 0:8], in_=sc), reads=[sc], writes=[m8[:, 0:8]])
                                P.op("dve", lambda e: e.match_replace(out=sc2, in_to_replace=m8[:, 0:8], in_values=sc,
                                                                      imm_value=-1e30),
                                     reads=[sc, m8[:, 0:8]], writes=[sc2])
                                P.op("dve", lambda e: e.max(out=m8[:, 8:16], in_=sc2), reads=[sc2], writes=[m8[:, 8:16]])
                                ts("dve", mb[:, 0:64], sc, m8[:, 15:16], 1.0, ALU.is_ge, ALU.subtract)
                                b6 = bank_bf(6)
                                trp(b6[:, 0:128], mb, identb)
                                copy("act", mbT, b6[:, 0:128])
                            tt("dve", sg_[:, 0, :], dn[:, 0, :], gsl[:, :, 0], ALU.mult)
                            for hp in range(2):
                                ob = psb[4 + hp][:, 0:258].rearrange("p (h c) -> p h c", h=2)
                                tt("dve", oav[:, 2 * hp:2 * hp + 2, :], ob[:, :, 0:64],
                                   sg_[:, 0, 2 * hp:2 * hp + 2].unsqueeze(2).broadcast_to([128, 2, 64]), ALU.mult)
                        posts.append(c_post)
                    push(dict(score=c_score, pv=c_pv, post=posts))
                for br in (2, 1):
                    kT = kslT if br == 1 else kwnT
                    vv = vsl if br == 1 else vwn
                    js = list(range(0, t + 1)) if br == 1 else list(range(max(0, t - 4), t + 1))
                    obank = 3 if br == 2 else (2 if par == 0 else 7)
                    ob = psb[obank][:, 0:260].rearrange("p (h c) -> p h c", h=4)
                    for ji, j in enumerate(js):
                        st = {}

                        def b_score(br=br, j=j, t=t, kT=kT, qw=qw, st=st, use_sel=use_sel):
                            pb = score_bank()
                            extra = []
                            if br == 1 and use_sel:
                                extra.append(("sel", None))
                            if j == t:
                                extra.append(("m", diagm))
                            if br == 2 and j == t - 4:
                                extra.append(("m", winm))
                            mm(pb[:, :], kT[:, j * 128:(j + 1) * 128], qw, True, len(extra) == 0)
                            for xi, (kind, mk_) in enumerate(extra):
                                lastx = xi == len(extra) - 1
                                if kind == "sel":
                                    mm(pb[:, :].rearrange("p (h q) -> p h q", h=4), eexp[:, j * 128:(j + 1) * 128],
                                       mbT.unsqueeze(1).broadcast_to([128, 4, 128]), False, lastx)
                                else:
                                    mm(pb[:, :], identb, mk_, False, lastx)
                            e = next_et()
                            act(e, pb[:, :], AF.Exp)
                            st["e"] = e

                        def b_pv(ob=ob, vv=vv, j=j, w=w, st=st, ji=ji, nj=len(js)):
                            e = st["e"]
                            for h in range(4):
                                mm(ob[:, h, :], e[:, h * 128:(h + 1) * 128], vv[:, j, w, :], ji == 0 and h == 0,
                                   (ji == nj - 1) and h == 3, sgc=True)

                        posts = []
                        if ji == len(js) - 1:
                            def b_post(br=br, ob=ob, dn=dn, sg_=sg_, otmp=otmp, oav=oav, gsl=gsl):
                                ts("dve", dn[:, br, :], ob[:, :, 64], 1e-30, None, ALU.max)
                                recip(dn[:, br, :], dn[:, br, :])
                                tt("dve", sg_[:, br, :], dn[:, br, :], gsl[:, :, br], ALU.mult)
                                ov_ = otmp.rearrange("p (h d) -> p h d", h=4)
                                tt("dve", ov_, ob[:, :, 0:64], sg_[:, br, :].unsqueeze(2).broadcast_to([128, 4, 64]), ALU.mult)
                                tt("dve", oav, oav, ov_, ALU.add)
                            posts.append(b_post)
                            if br == 1 and w == 1:
                                def t_post(t=t, oa=oa):
                                    ob_ = onb[t % 2]
                                    copy("act", ob_, oa)
                                    dtoks.append(dma("pool", mix[t * 128:(t + 1) * 128, 0:512], ob_))
                                posts.append(t_post)
                        push(dict(score=b_score, pv=b_pv, post=posts))
        flush()
        if stop_after == "B":
            return dtoks

        c3 = Carver(KEND, TOT)
        wg_b = alloc(c3, [8, DFF], BF16)
        wu_b = alloc(c3, [8, DFF], BF16)
        wd_b = alloc(c3, [22, D], BF16)
        wo_b = alloc(c3, [8, D], BF16)
        HW = DFF // 2
        fng = alloc(c3, [D], F32) if is_final else None
        stg_off3 = c3.off
        stg3 = [alloc(c3, [HW], F32), alloc(c3, [HW], F32)]
        c3 = Carver(stg_off3, TOT)
        si = 0
        for k in range(8):
            s = stg3[si % 2]; si += 1
            dma("sp", s[:, 0:D], Wd["w_out"][k * 128:(k + 1) * 128, :])
            copy("pool", wo_b[:, k, :], s[:, 0:D])
        for (wsrc, wdst) in (("w_gate", wg_b), ("w_up", wu_b)):
            for k in range(8):
                for hh in range(2):
                    s = stg3[si % 2]; si += 1
                    dma("sp", s, Wd[wsrc][k * 128:(k + 1) * 128, hh * HW:(hh + 1) * HW])
                    ts("pool", wdst[:, k, hh * HW:(hh + 1) * HW], s, ptT[:, 8 + k:9 + k], 1.0, ALU.mult, ALU.mult)
        for c2 in range(22):
            s = stg3[si % 2]; si += 1
            dma("sp", s[:, 0:D], Wd["w_down"][c2 * 128:(c2 + 1) * 128, :])
            copy("pool", wd_b[:, c2, :], s[:, 0:D])
        if is_final:
            dma("sp", fng, fnorm_d.partition_broadcast(128))
        mxb = [alloc(c3, [D], BF16), alloc(c3, [D], BF16)]
        xb3 = [alloc(c3, [D], F32), alloc(c3, [D], F32)]
        x1b2 = [alloc(c3, [D], F32), alloc(c3, [D], F32)]
        h3T = alloc(c3, [8, 128], BF16)
        mxT = alloc(c3, [8, 128], BF16)
        h3 = alloc(c3, [D], BF16)
        ss32 = [alloc(c3, [4], F32), alloc(c3, [4], F32)]
        sil = [alloc(c3, [512], F32), alloc(c3, [512], F32)]
        aT_in = alloc(c3, [DFF], BF16)
        junk3 = aT_in[:, 0:D]
        aT = alloc(c3, [22, 128], BF16)

        toks = []

        def L(t):
            dma("sp", mxb[t % 2], mix[t * 128:(t + 1) * 128, :])
            dma("sp", xb3[t % 2], x_src[t * 128:(t + 1) * 128, :])

        def Fa(t):
            mx = mxb[t % 2]
            b0 = bank_bf(0)
            for k in range(8):
                trp(b0[:, k * 128:(k + 1) * 128], mx[:, k * 128:(k + 1) * 128], identb)
            copy("act", mxT, b0.rearrange("p (k t) -> p k t", k=8))

        def Fb(t):
            xs = xb3[t % 2]
            x1b = x1b2[t % 2]
            ss3 = ss32[t % 2]
            for hf in range(2):
                for k in range(8):
                    mm(psb[4 + hf][:, :], mxT[:, k, :], wo_b[:, k, hf * 512:(hf + 1) * 512], k == 0, k == 7)
            for hf in range(2):
                tt("dve", x1b[:, hf * 512:(hf + 1) * 512], psb[4 + hf][:, :], xs[:, hf * 512:(hf + 1) * 512], ALU.add)
            act(junk3, x1b, AF.Square, accum=ss3[:, 0:1])
            act(ss3[:, 1:2], ss3[:, 0:1], AF.Sqrt, scale=1.0 / D, bias=EPS)
            recip(ss3[:, 2:3], ss3[:, 1:2])
            ts("dve", h3, x1b, ss3[:, 2:3], None, ALU.mult)

        def Fc(t):
            b1 = bank_bf(1)
            for k in range(8):
                trp(b1[:, k * 128:(k + 1) * 128], h3[:, k * 128:(k + 1) * 128], identb)
            copy("act", h3T, b1.rearrange("p (k t) -> p k t", k=8))

        def G(t):
            for n in range(6):
                n0 = n * 512
                nn = min(512, DFF - n0)
                pg = psb[2]
                pu = psb[3]
                for k in range(8):
                    mm(pg[:, 0:nn], h3T[:, k, :], wg_b[:, k, n0:n0 + nn], k == 0, k == 7)
                for k in range(8):
                    mm(pu[:, 0:nn], h3T[:, k, :], wu_b[:, k, n0:n0 + nn], k == 0, k == 7)
                sl = sil[n % 2]
                act(sl[:, 0:nn], pg[:, 0:nn], AF.Silu)
                tt("dve", aT_in[:, n0:n0 + nn], sl[:, 0:nn], pu[:, 0:nn], ALU.mult)

        def T(t):
            for r in range(3):
                bb = bank_bf((r + 1) % 2)
                c_lo = r * 8
                c_hi = min(22, c_lo + 8)
                for c in range(c_lo, c_hi):
                    trp(bb[:, (c - c_lo) * 128:(c - c_lo + 1) * 128], aT_in[:, c * 128:(c + 1) * 128], identb)
                copy("act" if r != 1 else "dve", aT[:, c_lo:c_hi, :],
                     bb[:, 0:(c_hi - c_lo) * 128].rearrange("p (c t) -> p c t", c=c_hi - c_lo))

        def Dn(t):
            x1b = x1b2[t % 2]
            ss3 = ss32[t % 2]
            for hf in range(2):
                for c in range(22):
                    mm(psb[6 + hf][:, :], aT[:, c, :], wd_b[:, c, hf * 512:(hf + 1) * 512], c == 0, c == 21)
            x2 = xb3[t % 2]
            for hf in range(2):
                tt("dve", x2[:, hf * 512:(hf + 1) * 512], psb[6 + hf][:, :], x1b[:, hf * 512:(hf + 1) * 512], ALU.add)
            if is_final:
                act(x1b, x2, AF.Square, accum=ss3[:, 0:1])
                act(ss3[:, 1:2], ss3[:, 0:1], AF.Sqrt, scale=1.0 / D, bias=EPS)
                recip(ss3[:, 2:3], ss3[:, 1:2])
                stt(x2, x2, ss3[:, 2:3], fng, ALU.mult, ALU.mult)
            toks.append(dma("pool", x_dst[t * 128:(t + 1) * 128, :], x2))

        L(0)
        Fa(0)
        Fb(0)
        Fc(0)
        G(0)
        for t in range(NT):
            nx = t + 1 < NT
            if nx:
                L(t + 1)
                Fa(t + 1)
            T(t)
            if nx:
                Fb(t + 1)
            Dn(t)
            if nx:
                Fc(t + 1)
                G(t + 1)
        return toks + (dtoks if dbg else [])

    if stacked:
        xmid = [P.dram("xs%d" % i, [S, D], F32) for i in range(2)]
        toks = None
        for l in range(nlayers):
            P.group = l
            for n, _ in WNAMES:
                Wd[n] = Wfull[n][l]
            src = x_in if l == 0 else xmid[(l - 1) % 2]
            dst = y_out if l == nlayers - 1 else xmid[l % 2]
            toks = layer(src, dst, final and l == nlayers - 1)
    else:
        toks = layer(x_src, y_out, final)
    P.emit(final_waits=toks)
    nc._prog_stats = {e: len(P.instrs[e]) for e in P.ENGS}
    return nc


_CACHE = {}


def kernel(**inputs):
    x = np.ascontiguousarray(np.asarray(inputs["x"], dtype=np.float32))
    B = x.shape[0]
    cb, cf = _consts()
    if "nc" not in _CACHE:
        _CACHE["nc"] = build_layer(final=True, nlayers=DEPTH, stacked=True)
    nc = _CACHE["nc"]
    shared = {"cb": cb, "cf": cf,
              "final_norm": np.ascontiguousarray(np.asarray(inputs["final_norm"], dtype=np.float32))}
    for n, _ in WNAMES:
        shared[n] = np.ascontiguousarray(np.asarray(inputs[n], dtype=np.float32))
    in_maps = []
    for b in range(B):
        m = dict(shared)
        m["x"] = x[b]
        in_maps.append(m)
    res = run_bass_kernel_spmd(nc, in_maps, core_ids=list(range(B)))
    return np.stack([np.asarray(res.results[b]["y"], dtype=np.float32) for b in range(B)], axis=0)
```

```python
import numpy as np
import ml_dtypes
import concourse.bass as bass
import concourse.mybir as mybir
from concourse.bass_utils import run_bass_kernel_spmd

F32 = mybir.dt.float32
BF16 = mybir.dt.bfloat16
AF = mybir.ActivationFunctionType
ALU = mybir.AluOpType
AX = mybir.AxisListType

S = 4096
D = 1024
NT = S // 128
DIN = 2600
DFF = 2816
EPS = 1e-6
BIG = 30000.0
DEPTH = 4

_ISZ = {}


def isz(dt):
    k = str(dt)
    if k not in _ISZ:
        _ISZ[k] = mybir.dt.size(dt)
    return _ISZ[k]


class Prog:
    ENGS = ["pe", "act", "dve", "pool", "sp"]
    NDSEM = 8

    def __init__(self, nc):
        self.nc = nc
        self.instrs = {e: [] for e in self.ENGS}
        self.bpp = {}
        self.recs = {}
        self.clock = {e: {x: -1 for x in self.ENGS} for e in self.ENGS}
        self.iclock = {e: [] for e in self.ENGS}
        self.dma_cnt = {e: 0 for e in self.ENGS}
        self.dma_known = {e: set() for e in self.ENGS}
        self.marked = {e: set() for e in self.ENGS}
        self.ctx = []
        self.group = 0
        self.igroup = {e: [] for e in self.ENGS}

    def sbuf(self, name, shape, dt):
        t = self.nc.sbuf_tensor(name, list(shape), dt)
        h = t.__enter__()
        self.ctx.append(t)
        n = 1
        for s in shape[1:]:
            n *= s
        self.bpp[name] = n * isz(dt)
        return h

    def psum(self, name, shape, dt):
        t = self.nc.psum_tensor(name, list(shape), dt)
        h = t.__enter__()
        self.ctx.append(t)
        n = 1
        for s in shape[1:]:
            n *= s
        self.bpp[name] = n * isz(dt)
        return h

    def dram(self, name, shape, dt, kind="Internal"):
        t = self.nc.dram_tensor(name, list(shape), dt, kind=kind)
        self.bpp[name] = None
        return t.ap()

    def region(self, a):
        name = a.tensor.name
        sz = isz(a.dtype)
        bpp = self.bpp.get(name, None)
        ap = a.ap
        if bpp is None:
            lo = a.offset
            ext = 0
            for st, cn in ap:
                ext += (cn - 1) * abs(st)
            return (name, 0, 1, lo * sz, (lo + ext + 1) * sz)
        off = a.offset * sz
        p0 = off // bpp
        lo = off % bpp
        if name.startswith("pb"):
            q0 = (p0 // 32) * 32
            q1 = ((p0 + ap[0][1] + 31) // 32) * 32
            return (name, q0, q1, 0, bpp)
        ext = 0
        for st, cn in ap[1:]:
            ext += (cn - 1) * abs(st)
        return (name, p0, p0 + ap[0][1], lo, lo + (ext + 1) * sz)

    def op(self, eng, fn, reads=(), writes=(), dma=False):
        import os
        lim = int(os.environ.get("KMAXOPS", "0"))
        self.nops = getattr(self, "nops", 0) + 1
        if lim and self.nops > lim:
            return None
        if os.environ.get("KLOG"):
            import sys as _s
            f = _s._getframe(1)
            while f is not None and f.f_code.co_name not in ("layer", "build_layer"):
                f = f.f_back
            print("OP", self.nops, eng, f.f_lineno if f else -1)
        idx = len(self.instrs[eng])
        waits = {}
        myclk = self.clock[eng]

        def need(tok):
            if tok[0] == "dma":
                if tok in self.dma_known[eng]:
                    return
                self.dma_known[eng].add(tok)
                waits[tok] = True
            else:
                e2, i2 = tok
                if myclk[e2] >= i2:
                    return
                waits[tok] = True

        if dma:
            q = eng
            di = self.dma_cnt[q]
            self.dma_cnt[q] += 1
            mytok = ("dma", q, di)
            if di >= self.NDSEM:
                need(("dma", q, di - self.NDSEM))
        else:
            mytok = (eng, idx)

        for (aps, kind) in ((reads, "R"), (writes, "W")):
            for a in aps:
                name, p0, p1, lo, hi = self.region(a)
                lst = self.recs.get(name, [])
                keep = []
                for r in lst:
                    (rp0, rp1, rlo, rhi, rkind, rtok) = r
                    ov = not (rp1 <= p0 or p1 <= rp0 or rhi <= lo or hi <= rlo)
                    rr = name.startswith("pb") and rtok[0] != eng
                    if ov and (rkind == "W" or kind == "W" or rr) and rtok != mytok:
                        if rtok[0] != "dma" and rtok[0] == eng and not dma:
                            if eng != "pe":
                                need(rtok)
                        else:
                            need(rtok)
                    cov = rp0 >= p0 and rp1 <= p1 and rlo >= lo and rhi <= hi
                    if kind == "W" and cov:
                        continue
                    if kind == "R" and rkind == "R" and cov and rtok[0] == mytok[0] and rtok[0] != "dma":
                        continue
                    keep.append(r)
                keep.append((p0, p1, lo, hi, kind, mytok))
                self.recs[name] = keep

        red = {}
        dmaw = []
        for tok in waits:
            if tok[0] == "dma":
                dmaw.append(tok)
            else:
                red[tok[0]] = max(red.get(tok[0], -1), tok[1])
        for e2, i2 in red.items():
            self.marked[e2].add(i2)
            oc = self.iclock[e2][i2]
            for x in self.ENGS:
                if oc[x] > myclk[x]:
                    myclk[x] = oc[x]
            if i2 > myclk[e2]:
                myclk[e2] = i2
        snap = dict(myclk)
        if not dma:
            snap[eng] = max(snap[eng], idx - 1)
        self.iclock[eng].append(snap)
        self.igroup[eng].append(self.group)
        self.instrs[eng].append(dict(fn=fn, waits=red, dmaw=dmaw, dma=(mytok if dma else None)))
        return mytok

    def emit(self, final_waits=()):
        nc = self.nc
        ENGH = {"pe": "tensor", "act": "scalar", "dve": "vector", "pool": "gpsimd", "sp": "sync"}
        rank = {}
        ngroups = self.group + 1
        sems = {}
        semctx = []
        for e in self.ENGS:
            cnt = [0] * ngroups
            rank[e] = {}
            for i in sorted(self.marked[e]):
                g = self.igroup[e][i]
                cnt[g] += 1
                rank[e][i] = (g, cnt[g])
            sems[e] = []
            for g in range(ngroups):
                c = nc.semaphore("s_%s_%d" % (e, g))
                sems[e].append(c.__enter__())
                semctx.append(c)
        dsems = {}
        for q in self.ENGS:
            if self.dma_cnt[q] > 0:
                lst = []
                for k in range(self.NDSEM):
                    c = nc.semaphore("d_%s_%d" % (q, k))
                    lst.append(c.__enter__())
                    semctx.append(c)
                dsems[q] = lst
        blk = nc.Block()
        block = blk.__enter__()
        prog = self

        def mk(e):
            def body(eng):
                for idx, ins in enumerate(prog.instrs[e]):
                    for e2, i2 in ins["waits"].items():
                        g2, v2 = rank[e2][i2]
                        eng.wait_ge(sems[e2][g2], v2)
                    for (_, q, di) in ins["dmaw"]:
                        eng.wait_ge(dsems[q][di % prog.NDSEM], 16 * (di // prog.NDSEM + 1))
                    r = ins["fn"](eng)
                    if ins["dma"] is not None:
                        (_, q, di) = ins["dma"]
                        r.then_inc(dsems[q][di % prog.NDSEM], 16)
                    elif idx in rank[e]:
                        r.then_inc(sems[e][rank[e][idx][0]], 1)
                if e == "sp":
                    for tok in final_waits:
                        if tok is None:
                            continue
                        (_, q, di) = tok
                        eng.wait_ge(dsems[q][di % prog.NDSEM], 16 * (di // prog.NDSEM + 1))
            return body

        for e in self.ENGS:
            if len(self.instrs[e]) == 0 and not (e == "sp" and final_waits):
                continue
            getattr(block, ENGH[e])(mk(e))
        blk.__exit__(None, None, None)
        for c in reversed(semctx):
            c.__exit__(None, None, None)
        for c in reversed(self.ctx):
            c.__exit__(None, None, None)


CB_ID = 0
CB_DIAG = 128
CB_WIN = 640
CB_BD = 1152
CB_ONES = 1280
CB_EEXP = 1408
CB_CMASK = 5504
CB_VCA = 13696
CB_HM = 13696 + 516
CB_BD256 = CB_HM + 64
NCB = CB_BD256 + 256
CF_ID = 0
CF_BDM = 128
CF_TRIL = 256
CF_RTAB = 384
CF_COS = 576
CF_SIN = 1600
CF_CCOS = 2624
CF_CSIN = 2688
CF_CM = 2752
NCF = 2816


def _consts():
    p = np.arange(128)[:, None]
    f = np.arange(128)[None, :]
    cb = np.zeros((128, NCB), np.float32)
    cb[:, CB_ID:CB_ID + 128] = (p == f)
    diag = np.where(p <= f, 0.0, -BIG)
    win = np.where(p > f, 0.0, -BIG)
    cb[:, CB_DIAG:CB_DIAG + 512] = np.tile(diag, (1, 4))
    cb[:, CB_WIN:CB_WIN + 512] = np.tile(win, (1, 4))
    bd = ((p // 64 == f // 64) & (p <= f)).astype(np.float32)
    cb[:, CB_BD:CB_BD + 128] = bd
    cb[:, CB_ONES:CB_ONES + 128] = 1.0
    m = np.arange(4096)[None, :]
    cb[:, CB_EEXP:CB_EEXP + 4096] = np.where((m // 64) == p, BIG, 0.0)
    for ch in range(2):
        c = ch * 128 + p
        cb[:, CB_CMASK + ch * 4096:CB_CMASK + (ch + 1) * 4096] = np.where(16 * c + 31 <= m, 0.0, -BIG)
    ncmp = 255
    cs = np.arange(ncmp) * 16
    ce = cs + 32
    bs = np.arange(64) * 64
    be = bs + 64
    ov = np.clip(np.minimum(ce[:, None], be[None, :]) - np.maximum(cs[:, None], bs[None, :]), 0, None) / 32.0
    vca = np.zeros((128, 2, 2, 129), np.float32)
    for ch in range(2):
        for pp in range(128):
            c = ch * 128 + pp
            if c < ncmp:
                vca[pp, ch, :, 64] = 1.0
                vca[pp, ch, :, 65:129] = ov[c][None, :]
    cb[:, CB_VCA:CB_VCA + 516] = vca.reshape(128, 516)
    for h in range(4):
        cb[:, CB_HM + h] = (np.arange(128) // 32 == h)
    cb[:, CB_BD256:CB_BD256 + 256] = ((np.arange(128)[:, None] // 32) == (np.arange(256)[None, :] // 64))

    cf = np.zeros((128, NCF), np.float32)
    cf[:, CF_ID:CF_ID + 128] = (p == f)
    cf[:, CF_BDM:CF_BDM + 128] = -bd / 16.0
    cf[:, CF_TRIL:CF_TRIL + 128] = (f <= p)
    j = np.arange(192)[None, :]
    mm = j - 62
    tbrel = (p >= 64).astype(np.int64)
    r = np.zeros((128, 192), np.float32)
    r[(mm == tbrel) | (mm == tbrel - 1)] = 1e4
    r[mm > tbrel] = -1e30
    cf[:, CF_RTAB:CF_RTAB + 192] = r
    half = 32
    inv = (1.0 / (np.float32(10000.0) ** (np.arange(half, dtype=np.float32) / np.float32(half)))).astype(np.float32)
    pos = (np.arange(32)[None, :] * 128 + np.arange(128)[:, None]).astype(np.float32)
    ang = (pos[:, :, None] * inv[None, None, :]).astype(np.float32)
    cf[:, CF_COS:CF_COS + 1024] = np.cos(ang).astype(np.float32).reshape(128, 1024)
    cf[:, CF_SIN:CF_SIN + 1024] = np.sin(ang).astype(np.float32).reshape(128, 1024)
    cpos = ((np.arange(2)[None, :] * 128 + np.arange(128)[:, None]) * 16 + 31).astype(np.float32)
    cang = (cpos[:, :, None] * inv[None, None, :]).astype(np.float32)
    cf[:, CF_CCOS:CF_CCOS + 64] = np.cos(cang).astype(np.float32).reshape(128, 64)
    cf[:, CF_CSIN:CF_CSIN + 64] = np.sin(cang).astype(np.float32).reshape(128, 64)
    for c in range(2):
        cf[:, CF_CM + c] = (np.arange(128) // 64 == c)
    return cb.astype(ml_dtypes.bfloat16), cf


WNAMES = [("attn_norm", [D]), ("w_in", [D, DIN]), ("cmp_pos_k", [32, 64]), ("cmp_w1_k", [2048, 256]),
          ("cmp_w2_k", [256, 64]), ("cmp_pos_v", [32, 64]), ("cmp_w1_v", [2048, 256]), ("cmp_w2_v", [256, 64]),
          ("gla_w_up", [16, 128]), ("gla_b_up", [128]), ("gla_norm", [64]), ("sg_ln_g", [256]), ("sg_ln_b", [256]),
          ("sg_w", [4, 128, 128]), ("sg_b", [4, 128]), ("w_out", [D, D]), ("ffn_norm", [D]),
          ("w_gate", [D, DFF]), ("w_up", [D, DFF]), ("w_down", [DFF, D])]


def build_layer(final=False, dbg=False, stop_after="C", nlayers=1, stacked=False):
    nc = bass.Bass("TRN2", target_bir_lowering=False)
    P = Prog(nc)
    x_in = P.dram("x", [S, D], F32, kind="ExternalInput")
    cb_d = P.dram("cb", [128, NCB], BF16, kind="ExternalInput")
    cf_d = P.dram("cf", [128, NCF], F32, kind="ExternalInput")
    if stacked:
        Wfull = {n: P.dram(n, [DEPTH] + s, F32, kind="ExternalInput") for n, s in WNAMES}
        Wd = {}
    else:
        Wd = {n: P.dram(n, s, F32, kind="ExternalInput") for n, s in WNAMES}
    fnorm_d = P.dram("final_norm", [D], F32, kind="ExternalInput")
    y_out = P.dram("y", [S, D], F32, kind="ExternalOutput")
    sk = "ExternalOutput" if dbg else "Internal"
    qTs = P.dram("qTs", [NT, 128, 512], BF16, kind=sk)
    mix = P.dram("mixs", [S, D], BF16, kind=sk)
    if dbg:
        d_kcT = P.dram("d_kcT", [128, 256], BF16, kind="ExternalOutput")
        d_vca = P.dram("d_vca", [128, 516], BF16, kind="ExternalOutput")
        d_kslT = P.dram("d_kslT", [128, 4096], BF16, kind="ExternalOutput")
        d_vsl = P.dram("d_vsl", [128, 32 * 2 * 65], BF16, kind="ExternalOutput")
        d_gates = P.dram("d_gates", [128, 32 * 24], F32, kind="ExternalOutput")
    dtoks = []

    ARENA = 104448
    arena = P.sbuf("arena", [128, ARENA], BF16)
    psb = [P.psum("pb%d" % i, [128, 512], F32) for i in range(8)]

    class Carver:
        def __init__(self, base, limit):
            self.off = base
            self.limit = limit

        def take(self, nbytes):
            nbytes = (nbytes + 63) // 64 * 64
            o = self.off
            self.off += nbytes
            assert self.off <= self.limit, (self.off, self.limit)
            return o

    def view(off_bytes, shape, dt, parts=128):
        n = 1
        for s in shape:
            n *= s
        if dt == BF16:
            a = arena[0:parts, off_bytes // 2: off_bytes // 2 + n]
        else:
            a = arena[0:parts, off_bytes // 2: off_bytes // 2 + n * 2].bitcast(F32)
        if len(shape) == 1:
            return a
        names = " ".join("d%d" % i for i in range(len(shape)))
        kw = {"d%d" % i: shape[i] for i in range(len(shape))}
        return a.rearrange("p (%s) -> p %s" % (names, names), **kw)

    def alloc(cv, shape, dt, parts=128):
        n = 1
        for s in shape:
            n *= s
        return view(cv.take(n * isz(dt)), shape, dt, parts)

    TOT = ARENA * 2
    cvK = Carver(0, 8 * 1024)
    identb = alloc(cvK, [128], BF16)
    identf = alloc(cvK, [128], F32)
    diagm = alloc(cvK, [512], BF16)
    winm = alloc(cvK, [512], BF16)
    bdmask = alloc(cvK, [128], BF16)
    onesb = alloc(cvK, [128], BF16)
    bdmf = alloc(cvK, [128], F32)
    trilf = alloc(cvK, [128], F32)
    rtab = alloc(cvK, [192], F32)
    ptT = alloc(cvK, [128], F32)
    gng = alloc(cvK, [64], F32)
    bupb = alloc(cvK, [128], BF16)
    wupb = alloc(cvK, [128], BF16)
    hm4 = alloc(cvK, [4], BF16)
    bd256 = alloc(cvK, [256], BF16)
    cm2 = alloc(cvK, [2], F32)
    KEND = cvK.off

    def dma(q, out, in_):
        return P.op(q, lambda e: e.dma_start(out=out, in_=in_), reads=[in_], writes=[out], dma=True)

    def act(out, in_, func, scale=None, bias=None, accum=None):
        kw = {}
        rd = [in_]
        wr = [out]
        if scale is not None:
            kw["scale"] = scale
            if not isinstance(scale, (int, float)):
                rd.append(scale)
        if bias is not None:
            kw["bias"] = bias
            if not isinstance(bias, (int, float)):
                rd.append(bias)
        if accum is not None:
            kw["accum_out"] = accum
            wr.append(accum)
        return P.op("act", lambda e: e.activation(out=out, in_=in_, func=func, **kw), reads=rd, writes=wr)

    def mm(out, lhsT, rhs, start, stop, tp=None, sgc=False):
        kw = {}
        if tp is not None:
            kw["tile_position"] = tp
        if sgc:
            kw["skip_group_check"] = True
        return P.op("pe", lambda e: e.matmul(out, lhsT=lhsT, rhs=rhs, start=start, stop=stop, **kw),
                    reads=[lhsT, rhs], writes=[out])

    def trp(out, in_, ident):
        return P.op("pe", lambda e: e.transpose(out, in_, ident), reads=[in_, ident], writes=[out])

    def tt(eng, out, in0, in1, op):
        return P.op(eng, lambda e: e.tensor_tensor(out=out, in0=in0, in1=in1, op=op), reads=[in0, in1], writes=[out])

    def ts(eng, out, in0, s1, s2, op0, op1=None):
        rd = [in0]
        if not isinstance(s1, (int, float)):
            rd.append(s1)
        if s2 is not None and not isinstance(s2, (int, float)):
            rd.append(s2)
        if op1 is None:
            return P.op(eng, lambda e: e.tensor_scalar(out=out, in0=in0, scalar1=s1, scalar2=None, op0=op0),
                        reads=rd, writes=[out])
        return P.op(eng, lambda e: e.tensor_scalar(out=out, in0=in0, scalar1=s1, scalar2=s2, op0=op0, op1=op1),
                    reads=rd, writes=[out])

    def stt(out, in0, sc, in1, op0, op1):
        rd = [in0, in1]
        if not isinstance(sc, (int, float)):
            rd.append(sc)
        return P.op("dve", lambda e: e.scalar_tensor_tensor(out=out, in0=in0, scalar=sc, in1=in1, op0=op0, op1=op1),
                    reads=rd, writes=[out])

    def copy(eng, out, in_):
        if eng == "act":
            return act(out, in_, AF.Copy)
        return P.op(eng, lambda e: e.tensor_copy(out=out, in_=in_), reads=[in_], writes=[out])

    def memset(eng, out, val):
        return P.op(eng, lambda e: e.memset(out, val), reads=[], writes=[out])

    def recip(out, in_):
        return P.op("dve", lambda e: e.reciprocal(out=out, in_=in_), reads=[in_], writes=[out])

    def bank_bf(i):
        return psb[i][:].bitcast(BF16)

    dma("sp", identb, cb_d[:, CB_ID:CB_ID + 128])
    dma("sp", diagm, cb_d[:, CB_DIAG:CB_DIAG + 512])
    dma("sp", winm, cb_d[:, CB_WIN:CB_WIN + 512])
    dma("sp", bdmask, cb_d[:, CB_BD:CB_BD + 128])
    dma("sp", onesb, cb_d[:, CB_ONES:CB_ONES + 128])
    dma("sp", identf, cf_d[:, CF_ID:CF_ID + 128])
    dma("sp", bdmf, cf_d[:, CF_BDM:CF_BDM + 128])
    dma("sp", trilf, cf_d[:, CF_TRIL:CF_TRIL + 128])
    dma("sp", rtab, cf_d[:, CF_RTAB:CF_RTAB + 192])
    dma("sp", hm4, cb_d[:, CB_HM:CB_HM + 4])
    dma("sp", bd256, cb_d[:, CB_BD256:CB_BD256 + 256])
    dma("sp", cm2, cf_d[:, CF_CM:CF_CM + 2])

    x_src = x_in
    fins = []

    def layer(x_src, x_dst, is_final):
        cv = Carver(KEND, TOT)
        cmask = alloc(cv, [2, 4096], BF16)
        KX = [alloc(cv, [4096], BF16), alloc(cv, [4096], BF16)]
        cosT = alloc(cv, [32, 32], F32)
        sinT = alloc(cv, [32, 32], F32)
        ccos = alloc(cv, [2, 32], F32)
        csin = alloc(cv, [2, 32], F32)
        gates = alloc(cv, [32, 24], F32)
        kwnT = alloc(cv, [4096], BF16)
        vsl = alloc(cv, [32, 2, 65], BF16)
        vwn = alloc(cv, [32, 2, 65], BF16)
        kcrT = alloc(cv, [4096], BF16)
        vcrT = alloc(cv, [4096], BF16)
        kcT = alloc(cv, [256], BF16)
        vca = alloc(cv, [2, 2, 129], BF16)
        lng = alloc(cv, [256], F32)
        lnb = alloc(cv, [256], F32)
        sgWT = alloc(cv, [4, 128], BF16)
        st_f = alloc(cv, [256], F32)
        st_b = alloc(cv, [2, 256], BF16)
        w_off = cv.off
        win_b = alloc(cv, [8, DIN], BF16)
        wk_off = cv.off

        dma("sp", cmask, cb_d[:, CB_CMASK:CB_CMASK + 8192].rearrange("p (a b) -> p a b", a=2))
        dma("sp", KX[0][64:128, :], cb_d[0:64, CB_EEXP:CB_EEXP + 4096])
        dma("sp", KX[1][0:64, :], cb_d[0:64, CB_EEXP:CB_EEXP + 4096])
        dma("sp", cosT, cf_d[:, CF_COS:CF_COS + 1024].rearrange("p (a b) -> p a b", a=32))
        dma("sp", sinT, cf_d[:, CF_SIN:CF_SIN + 1024].rearrange("p (a b) -> p a b", a=32))
        dma("sp", ccos, cf_d[:, CF_CCOS:CF_CCOS + 64].rearrange("p (a b) -> p a b", a=2))
        dma("sp", csin, cf_d[:, CF_CSIN:CF_CSIN + 64].rearrange("p (a b) -> p a b", a=2))
        dma("sp", vca, cb_d[:, CB_VCA:CB_VCA + 516].rearrange("p (a b c) -> p a b c", a=2, b=2))
        dma("sp", gng, Wd["gla_norm"].partition_broadcast(128))
        dma("sp", lng, Wd["sg_ln_g"].partition_broadcast(128))
        dma("sp", lnb, Wd["sg_ln_b"].partition_broadcast(128))

        cw = Carver(wk_off, TOT)
        stg = [alloc(cw, [DFF], F32), alloc(cw, [DFF], F32)]
        stg_i = [0]

        def stage():
            s = stg[stg_i[0] % 2]
            stg_i[0] += 1
            return s

        pt = stage()[:, 0:128]
        memset("pool", pt, 0.0)
        dma("sp", pt[0:8, :], Wd["attn_norm"].rearrange("(k p) -> k p", p=128))
        dma("sp", pt[8:16, :], Wd["ffn_norm"].rearrange("(k p) -> k p", p=128))
        dma("sp", pt[16:20, :], Wd["sg_b"])
        dma("sp", pt[32:64, 0:64], Wd["cmp_pos_k"])
        dma("sp", pt[64:96, 0:64], Wd["cmp_pos_v"])
        trp(psb[7][:, 0:128], pt, identf)
        copy("dve", ptT, psb[7][:, 0:128])
        s2 = stage()
        dma("sp", s2[0:16, 0:128], Wd["gla_w_up"])
        dma("sp", s2[0:1, 128:256], Wd["gla_b_up"].rearrange("(o n) -> o n", o=1))
        copy("pool", wupb[0:16, :], s2[0:16, 0:128])
        copy("pool", bupb[0:1, :], s2[0:1, 128:256])
        s3 = stage()
        s3v = s3[:, 0:512].rearrange("p (g j) -> p g j", g=4)
        dma("sp", s3v, Wd["sg_w"].rearrange("g i j -> i g j"))
        sgm = alloc(cw, [4, 128], BF16)
        tt("dve", sgm, s3v, trilf.unsqueeze(1).broadcast_to([128, 4, 128]), ALU.mult)
        b7b = bank_bf(7)
        for g in range(4):
            trp(b7b[:, 256 + g * 128: 256 + (g + 1) * 128], sgm[:, g, :], identb)
        copy("dve", sgWT, b7b[:, 256:768].rearrange("p (g i) -> p g i", g=4))
        for k in range(8):
            s = stage()
            dma("sp", s[:, 0:DIN], Wd["w_in"][k * 128:(k + 1) * 128, :])
            ts("pool", win_b[:, k, :], s[:, 0:DIN], ptT[:, k:k + 1], 1.0, ALU.mult, ALU.mult)
        memset("pool", vsl, 1.0)
        memset("pool", vwn, 1.0)
        memset("pool", st_f, 0.0)
        memset("pool", st_b, 0.0)
        if NT != 32:
            memset("pool", kcrT, 0.0)
            memset("pool", vcrT, 0.0)

        xbuf = [alloc(cw, [D], F32), alloc(cw, [D], F32)]
        hbf = alloc(cw, [D], BF16)
        junk = hbf
        hT = [alloc(cw, [8, 128], BF16), alloc(cw, [8, 128], BF16)]
        ssq = alloc(cw, [4], F32)
        ri = alloc(cw, [12, 64], F32)
        ro = alloc(cw, [768], BF16)
        rt1 = alloc(cw, [12, 32], F32)
        rt2 = alloc(cw, [12, 32], F32)
        qTst = [alloc(cw, [512], BF16), alloc(cw, [512], BF16)]
        rawb = alloc(cw, [256], BF16)
        alT = alloc(cw, [128], BF16)
        e1 = alloc(cw, [128], F32)
        spl = alloc(cw, [128], F32)
        epn = alloc(cw, [2, 128], F32)
        bcs = alloc(cw, [128], F32)
        qkg = alloc(cw, [2, 128], BF16)
        kgT = alloc(cw, [128], BF16)
        qTm = alloc(cw, [4, 128], BF16)
        qTc = alloc(cw, [2, 128], BF16)
        kgc = alloc(cw, [2, 128], BF16)
        tS2 = alloc(cw, [256], F32)
        memset("pool", qTc, 0.0)
        vg = alloc(cw, [256], BF16)
        attm = alloc(cw, [4, 128], BF16)
        ebc = alloc(cw, [2], F32)
        tS = alloc(cw, [256], F32)
        og = alloc(cw, [256], F32)
        og2 = alloc(cw, [256], F32)
        gms = alloc(cw, [8], F32)
        sir = alloc(cw, [256], F32)
        mixt = [alloc(cw, [512], BF16), alloc(cw, [512], BF16)]
        guv = alloc(cw, [512], F32)
        bns = alloc(cw, [8], F32)
        bna = alloc(cw, [4], F32)
        vn = alloc(cw, [256], F32)
        vnb = alloc(cw, [256], BF16)

        for t in range(NT):
            xs = xbuf[t % 2]
            hTt = hT[t % 2]
            dma("sp", xs, x_src[t * 128:(t + 1) * 128, :])
            act(junk, xs, AF.Square, accum=ssq[:, 0:1])
            act(ssq[:, 1:2], ssq[:, 0:1], AF.Sqrt, scale=1.0 / D, bias=EPS)
            recip(ssq[:, 2:3], ssq[:, 1:2])
            ts("dve", hbf, xs, ssq[:, 2:3], None, ALU.mult)
            b0 = bank_bf(0)
            for k in range(8):
                trp(b0[:, k * 128:(k + 1) * 128], hbf[:, k * 128:(k + 1) * 128], identb)
            copy("act", hTt, b0.rearrange("p (k t) -> p k t", k=8))
            chunks = [(0, 512, 1), (512, 1024, 2), (1024, 1432, 3), (1432, 1832, 4), (1832, 2088, 5), (2088, 2600, 6)]
            for (c0, c1, bk) in chunks:
                for k in range(8):
                    mm(psb[bk][:, 0:c1 - c0], hTt[:, k, :], win_b[:, k, c0:c1], k == 0, k == 7)
            for k in range(8):
                mm(psb[7][0:16, 256:384], win_b[:, k, 1816:1832], hTt[:, k, :], k == 0, k == 7)
            riv = ri.rearrange("p (w r) d -> p w r d", w=2)
            act(riv[:, :, 0:4, :], psb[1][:, 0:512].rearrange("p (w r d) -> p w r d", w=2, r=4), AF.Copy, scale=0.125)
            copy("dve", riv[:, :, 4, :], psb[2][:, 256:384].rearrange("p (w d) -> p w d", w=2))
            copy("dve", riv[:, :, 5, :], psb[3][:, 0:128].rearrange("p (w d) -> p w d", w=2))
            copy("act", rawb, psb[2][:, 0:256])
            copy("dve", vsl[:, t, :, 0:64], psb[2][:, 384:512].rearrange("p (w d) -> p w d", w=2))
            copy("dve", vwn[:, t, :, 0:64], psb[3][:, 128:256].rearrange("p (w d) -> p w d", w=2))
            act(gates[:, t, :], psb[3][:, 256:280], AF.Sigmoid)
            act(sir, psb[5][:, 0:256], AF.Silu)
            r4 = ri.rearrange("p (w r) (two d) -> p w r two d", w=2, two=2)
            x1 = r4[:, :, :, 0, :]
            x2 = r4[:, :, :, 1, :]
            cs_ = cosT[:, t, :].unsqueeze(1).unsqueeze(1).broadcast_to([128, 2, 6, 32])
            sn_ = sinT[:, t, :].unsqueeze(1).unsqueeze(1).broadcast_to([128, 2, 6, 32])
            t1 = rt1.rearrange("p (w r) d -> p w r d", w=2)
            t2 = rt2.rearrange("p (w r) d -> p w r d", w=2)
            rov = ro.rearrange("p (r w two d) -> p w r two d", r=6, w=2, two=2)
            tt("dve", t1, x1, cs_, ALU.mult)
            tt("dve", t2, x2, sn_, ALU.mult)
            tt("dve", rov[:, :, :, 0, :], t1, t2, ALU.subtract)
            tt("dve", t1, x2, cs_, ALU.mult)
            tt("dve", t2, x1, sn_, ALU.mult)
            tt("dve", rov[:, :, :, 1, :], t1, t2, ALU.add)
            b1 = bank_bf(1)
            for r in range(6):
                trp(b1[:, r * 128:(r + 1) * 128], ro[:, r * 128:(r + 1) * 128], identb)
            qst = qTst[t % 2]
            copy("act", qst, b1[:, 0:512])
            dma("pool", qTs[t], qst)
            copy("dve", KX[0][0:64, t * 128:(t + 1) * 128], b1[0:64, 512:640])
            copy("dve", KX[1][64:128, t * 128:(t + 1) * 128], b1[64:128, 512:640])
            copy("dve", kwnT[:, t * 128:(t + 1) * 128], b1[:, 640:768])
            b2 = bank_bf(2)
            trp(b2[:, 0:128], rawb[:, 0:128], identb)
            trp(b2[:, 128:256], rawb[:, 128:256], identb)
            copy("act", kcrT[:, t * 128:(t + 1) * 128], b2[:, 0:128])
            copy("act", vcrT[:, t * 128:(t + 1) * 128], b2[:, 128:256])
            copy("dve", alT[0:16, :], psb[7][0:16, 256:384])
            mm(psb[7][:, 0:128], alT[0:16, :], wupb[0:16, :], True, False)
            mm(psb[7][:, 0:128], onesb[0:1, :], bupb[0:1, :], False, True)
            act(e1, psb[7][:, 0:128], AF.Exp, scale=-1.0)
            act(spl, e1, AF.Ln, bias=1.0)
            mm(psb[7][:, 128:256], bdmf, spl, True, True)
            act(epn[:, 0, :], psb[7][:, 128:256], AF.Exp)
            act(epn[:, 1, :], psb[7][:, 128:256], AF.Exp, scale=-1.0)
            copy("dve", bcs, psb[7][:, 128:256])
            stt(qkg[:, 0, :], psb[3][:, 280:408], 32.0 ** -0.5, epn[:, 0, :], ALU.mult, ALU.mult)
            tt("dve", qkg[:, 1, :], psb[4][:, 0:128], epn[:, 1, :], ALU.mult)
            copy("act", vg, psb[4][:, 128:384])
            trp(b2[:, 256:384], qkg[:, 0, :], identb)
            trp(b2[:, 384:512], qkg[:, 1, :], identb)
            copy("act", kgT, b2[:, 384:512])
            tt("dve", qTm, b2[:, 256:384].unsqueeze(1).broadcast_to([128, 4, 128]),
               hm4.unsqueeze(2).broadcast_to([128, 4, 128]), ALU.mult)
            copy("dve", qTc[:, 0, 0:64], b2[:, 256:320])
            copy("dve", qTc[:, 1, 64:128], b2[:, 320:384])
            tt("dve", kgc, qkg[:, 1, :].unsqueeze(1).broadcast_to([128, 2, 128]),
               cm2.unsqueeze(2).broadcast_to([128, 2, 128]), ALU.mult)
            trp(psb[7][:, 384:512], bcs, identf)
            act(ebc[:, 0:1], psb[7][:, 447:448], AF.Exp)
            act(ebc[:, 1:2], psb[7][:, 511:512], AF.Exp)
            mm(psb[3][:, 0:512], kgT, qTm.rearrange("p h i -> p (h i)"), True, True)
            tt("dve", attm, psb[3][:, 0:512].rearrange("p (h i) -> p h i", h=4),
               bdmask.unsqueeze(1).broadcast_to([128, 4, 128]), ALU.mult)
            first = True
            for h in range(4):
                mm(psb[4][:, h * 64:(h + 1) * 64], attm[:, h, :], vg[:, h * 64:(h + 1) * 64], first, False, sgc=True)
                first = False
            mm(psb[4][:, 0:256], qTc[:, 0, :], st_b[:, 0, :], False, False, sgc=True)
            mm(psb[5][:, 256:512], kgc[:, 0, :], vg, True, True)
            tt("dve", tS, psb[5][:, 256:512], bd256, ALU.mult)
            tt("dve", tS2, tS, st_f, ALU.add)
            ts("dve", st_f, tS2, ebc[:, 0:1], None, ALU.mult)
            copy("dve", st_b[:, 1, :], st_f)
            mm(psb[4][:, 0:256], qTc[:, 1, :], st_b[:, 1, :], False, True, sgc=True)
            mm(psb[5][:, 256:512], kgc[:, 1, :], vg, True, True)
            tt("dve", tS, psb[5][:, 256:512], bd256, ALU.mult)
            tt("dve", tS2, tS, st_f, ALU.add)
            ts("dve", st_f, tS2, ebc[:, 1:2], None, ALU.mult)
            copy("dve", st_b[:, 0, :], st_f)
            copy("act", og, psb[4][:, 0:256])
            tt("dve", og2, og, og, ALU.mult)
            P.op("dve", lambda e: e.tensor_reduce(out=gms[:, 0:4], in_=og2.rearrange("p (h d) -> p h d", h=4),
                                                  axis=AX.X, op=ALU.add),
                 reads=[og2], writes=[gms[:, 0:4]])
            act(gms[:, 4:8], gms[:, 0:4], AF.Sqrt, scale=1.0 / 64, bias=EPS)
            recip(gms[:, 0:4], gms[:, 4:8])
            tt("dve", og2.rearrange("p (h d) -> p h d", h=4), og.rearrange("p (h d) -> p h d", h=4),
               gms[:, 0:4].unsqueeze(2).broadcast_to([128, 4, 64]), ALU.mult)
            tt("dve", og.rearrange("p (h d) -> p h d", h=4), og2.rearrange("p (h d) -> p h d", h=4),
               gng.unsqueeze(1).broadcast_to([128, 4, 64]), ALU.mult)
            mxt = mixt[t % 2]
            tt("dve", mxt[:, 0:256], og, sir, ALU.mult)
            act(guv, psb[6][:, 0:512], AF.Gelu_apprx_tanh)
            P.op("dve", lambda e: e.bn_stats(out=bns[:, 0:6], in_=guv[:, 256:512]), reads=[guv[:, 256:512]], writes=[bns[:, 0:6]])
            P.op("dve", lambda e: e.bn_aggr(out=bna[:, 0:2], in_=bns[:, 0:6]), reads=[bns[:, 0:6]], writes=[bna[:, 0:2]])
            act(bna[:, 2:3], bna[:, 1:2], AF.Sqrt, bias=EPS)
            recip(bna[:, 3:4], bna[:, 2:3])
            ts("dve", vn, guv[:, 256:512], bna[:, 0:1], bna[:, 3:4], ALU.subtract, ALU.mult)
            tt("dve", vn, vn, lng, ALU.mult)
            tt("dve", vnb, vn, lnb, ALU.add)
            for g in range(4):
                mm(psb[6][:, g * 64:(g + 1) * 64], sgWT[:, g, :], vnb[:, g * 64:(g + 1) * 64], True, True)
            for g in range(4):
                stt(mxt[:, 256 + g * 64: 256 + (g + 1) * 64], psb[6][:, g * 64:(g + 1) * 64], ptT[:, 16 + g:17 + g],
                    guv[:, g * 64:(g + 1) * 64], ALU.add, ALU.mult)
            dtoks.append(dma("pool", mix[t * 128:(t + 1) * 128, 512:1024], mxt))
        if dbg and NT == 32:
            dtoks.append(dma("pool", d_kslT[:, :], KX[0]))
            dtoks.append(dma("pool", d_vsl[:, :], vsl.rearrange("p a b c -> p (a b c)")))
            dtoks.append(dma("pool", d_gates[:, :], gates.rearrange("p a b -> p (a b)")))
        if stop_after == "A":
            return dtoks

        cc = Carver(w_off, TOT)
        w1d = [alloc(cc, [32, 256], BF16), alloc(cc, [32, 256], BF16)]
        w2b = alloc(cc, [2, 2, 64], BF16)
        posTb = alloc(cc, [64], BF16, parts=64)
        hbias = alloc(cc, [2, 2], F32)
        hidT = alloc(cc, [2, 255], BF16)
        kct = alloc(cc, [2, 128], F32)
        kcr = alloc(cc, [2, 128], BF16)
        cst = [alloc(cc, [8, 256], F32), alloc(cc, [8, 256], F32)]
        ci = 0
        for kv, nm in enumerate(["cmp_w1_k", "cmp_w1_v"]):
            src = Wd[nm].rearrange("(l d) h -> d l h", d=64)
            for part in range(2):
                for l0 in range(0, 32, 8):
                    s = cst[ci % 2]
                    ci += 1
                    dma("sp", s[64 * part:64 * part + 64], src[:, l0:l0 + 8, :])
                    copy("pool", w1d[kv][64 * part:64 * part + 64, l0:l0 + 8, :], s[64 * part:64 * part + 64])
        for kv, nm in enumerate(["cmp_w2_k", "cmp_w2_v"]):
            s = cst[ci % 2]
            ci += 1
            sv = s[:, 0, 0:128].rearrange("p (c d) -> p c d", c=2)
            dma("sp", sv, Wd[nm].rearrange("(c p) d -> p c d", p=128))
            copy("pool", w2b[:, kv, :, :], sv)
        copy("dve", posTb[0:64, :], ptT[0:64, 32:96])
        for kv in range(2):
            rawT = kcrT if kv == 0 else vcrT
            for c in range(2):
                for l in range(32):
                    mm(psb[7][:, 2 * kv + c: 2 * kv + c + 1], w1d[kv][0:64, l, c * 128:(c + 1) * 128],
                       posTb[0:64, kv * 32 + l: kv * 32 + l + 1], l == 0, l == 31)
            copy("dve", hbias[:, kv, :], psb[7][:, 2 * kv: 2 * kv + 2])
            for g in range(2):
                for c in range(2):
                    for l in range(32):
                        rv = rawT[64 * g:64 * g + 64, :].rearrange("p (c s) -> p s c", s=16)
                        rhs = rv[:, l, 0:255] if l < 16 else rv[:, l - 16, 1:256]
                        mm(psb[c + 2 * g][:, 0:255], w1d[kv][64 * g:64 * g + 64, l, c * 128:(c + 1) * 128], rhs, l == 0, l == 31)
                    act(hidT[:, c, :], psb[c + 2 * g][:, 0:255], AF.Gelu_apprx_tanh, bias=hbias[:, kv, c:c + 1])
                for (c0, cn, ch) in [(0, 128, 0), (128, 127, 1)]:
                    for c in range(2):
                        mm(psb[4 + ch][0:cn, g * 64:(g + 1) * 64], hidT[:, c, c0:c0 + cn], w2b[:, kv, c, :], c == 0, c == 1)
                    if kv == 0:
                        copy("dve", kct[0:cn, ch, g * 64:(g + 1) * 64], psb[4 + ch][0:cn, g * 64:(g + 1) * 64])
                    else:
                        copy("dve", vca[0:cn, ch, g, 0:64], psb[4 + ch][0:cn, g * 64:(g + 1) * 64])
        k5 = kct.rearrange("p c (g two d) -> p c g two d", g=2, two=2)
        o5 = kcr.rearrange("p c (g two d) -> p c g two d", g=2, two=2)
        ta = alloc(cc, [2, 2, 32], F32)
        tb = alloc(cc, [2, 2, 32], F32)
        for (cn, ch) in [(128, 0), (127, 1)]:
            a1 = k5[0:cn, ch, :, 0, :]
            a2 = k5[0:cn, ch, :, 1, :]
            cb_ = ccos[0:cn, ch, :].unsqueeze(1).broadcast_to([cn, 2, 32])
            sb_ = csin[0:cn, ch, :].unsqueeze(1).broadcast_to([cn, 2, 32])
            tt("dve", ta[0:cn, ch], a1, cb_, ALU.mult)
            tt("dve", tb[0:cn, ch], a2, sb_, ALU.mult)
            tt("dve", o5[0:cn, ch, :, 0, :], ta[0:cn, ch], tb[0:cn, ch], ALU.subtract)
            tt("dve", ta[0:cn, ch], a2, cb_, ALU.mult)
            tt("dve", tb[0:cn, ch], a1, sb_, ALU.mult)
            tt("dve", o5[0:cn, ch, :, 1, :], ta[0:cn, ch], tb[0:cn, ch], ALU.add)
            b6_ = bank_bf(6)
            trp(b6_[:, ch * 128: ch * 128 + cn], kcr[0:cn, ch, :], identb[0:cn, 0:cn])
            copy("dve", kcT[:, ch * 128: ch * 128 + cn], b6_[:, ch * 128: ch * 128 + cn])

        if dbg:
            dtoks.append(dma("pool", d_kcT[:, :], kcT))
            dtoks.append(dma("pool", d_vca[:, :], vca.rearrange("p a b c -> p (a b c)")))
        if stop_after == "K":
            return dtoks
        cb2 = Carver(cc.off, TOT)
        qt = [alloc(cb2, [2, 512], BF16), alloc(cb2, [2, 512], BF16)]
        memset("pool", qt[0], 0.0)
        memset("pool", qt[1], 0.0)
        ET = [alloc(cb2, [512], BF16) for _ in range(4)]
        et_i = [0]
        sc = alloc(cb2, [64], F32)
        sc2 = alloc(cb2, [64], F32)
        impa = alloc(cb2, [64], F32)
        m8 = alloc(cb2, [16], F32)
        mbw = [alloc(cb2, [128], BF16), alloc(cb2, [128], BF16)]
        memset("pool", mbw[0], 0.0)
        memset("pool", mbw[1], 0.0)
        qsel = [[alloc(cb2, [512], BF16) for _ in range(2)] for _ in range(2)]
        dn2 = [alloc(cb2, [3, 4], F32), alloc(cb2, [3, 4], F32)]
        sg2 = [alloc(cb2, [3, 4], F32), alloc(cb2, [3, 4], F32)]
        oacc = [alloc(cb2, [512], F32), alloc(cb2, [512], F32)]
        otmp2 = [alloc(cb2, [256], F32), alloc(cb2, [256], F32)]
        onb = [alloc(cb2, [512], BF16), alloc(cb2, [512], BF16)]
        sbank = [0]

        def score_bank():
            b = sbank[0] % 2
            sbank[0] += 1
            return psb[b]

        def next_et():
            e = ET[et_i[0] % 4]
            et_i[0] += 1
            return e

        pend = [None]

        def push(item):
            item["score"]()
            if pend[0] is not None:
                pend[0]["pv"]()
                for f in pend[0]["post"]:
                    f()
            pend[0] = item

        def flush():
            if pend[0] is not None:
                pend[0]["pv"]()
                for f in pend[0]["post"]:
                    f()
                pend[0] = None

        for t in range(NT):
            q_t = qt[t % 2]
            dma("sp", q_t[0:64, 0, :], qTs[t, 0:64, :])
            dma("sp", q_t[64:128, 1, :], qTs[t, 64:128, :])
            oa = oacc[t % 2]
            for w in range(2):
                par = (2 * t + w) % 2
                dn = dn2[par]
                sg_ = sg2[par]
                otmp = otmp2[par]
                qw = q_t[:, w, :]
                gsl = gates[:, t, :].rearrange("p (h b) -> p h b", b=3)[:, 4 * w:4 * w + 4, :]
                oav = oa[:, w * 256:(w + 1) * 256].rearrange("p (h d) -> p h d", h=4)
                use_sel = t >= 8
                qs_w = qsel[w][t % 2]
                if use_sel:
                    dma("sp", qs_w[64 * w:64 * w + 64, :], qTs[t, 64 * w:64 * w + 64, :])
                ncv = min(8 * t + 7, 255)
                chs = [(0, 128, 0)] + ([(128, 127, 1)] if ncv > 128 else [])
                for ci_, (c0, cn, ch) in enumerate(chs):
                    st = {}

                    def c_score(c0=c0, cn=cn, ch=ch, st=st, qw=qw, t=t):
                        pb = score_bank()
                        mm(pb[0:cn, :], kcT[:, c0:c0 + cn], qw, True, False)
                        mm(pb[0:cn, :].rearrange("p (h q) -> p h q", h=4), identb[:, 0:cn],
                           cmask[:, ch, t * 128:(t + 1) * 128].unsqueeze(1).broadcast_to([128, 4, 128]), False, True)
                        e = next_et()
                        act(e[0:cn, :], pb[0:cn, :], AF.Exp)
                        st["e"] = e

                    def c_pv(cn=cn, ch=ch, st=st, w=w, ci_=ci_, nch=len(chs)):
                        e = st["e"]
                        for hp in range(2):
                            ob = psb[4 + hp][:, 0:258].rearrange("p (h c) -> p h c", h=2)
                            for hh in range(2):
                                h = hp * 2 + hh
                                mm(ob[:, hh, :], e[0:cn, h * 128:(h + 1) * 128], vca[0:cn, ch, w, :],
                                   ci_ == 0 and hh == 0, ci_ == nch - 1 and hh == 1, sgc=True)

                    posts = []
                    if ci_ == len(chs) - 1:
                        def c_post(t=t, w=w, dn=dn, sg_=sg_, oav=oav, gsl=gsl, use_sel=use_sel, qs_w=qs_w):
                            for hp in range(2):
                                ob = psb[4 + hp][:, 0:258].rearrange("p (h c) -> p h c", h=2)
                                ts("dve", dn[:, 0, 2 * hp:2 * hp + 2], ob[:, :, 64], 1e-30, None, ALU.max)
                            recip(dn[:, 0, :], dn[:, 0, :])
                            if use_sel:
                                for h in range(4):
                                    ob = psb[4 + h // 2][:, 0:258].rearrange("p (h c) -> p h c", h=2)
                                    if h == 0:
                                        ts("dve", impa, ob[:, 0, 65:129], dn[:, 0, 0:1], None, ALU.mult)
                                    else:
                                        stt(impa, ob[:, h % 2, 65:129], dn[:, 0, h:h + 1], impa, ALU.mult, ALU.add)
                                tt("dve", sc, impa, rtab[:, 62 - 2 * t: 62 - 2 * t + 64], ALU.add)
                                ts("dve", sc[:, 0:1], sc[:, 0:1], 1e4, None, ALU.add)
                                P.op("dve", lambda e: e.max(out=m8[:, 0:8], in_=sc), reads=[sc], writes=[m8[:, 0:8]])
                                P.op("dve", lambda e: e.match_replace(out=sc2, in_to_replace=m8[:, 0:8], in_values=sc,
                                                                      imm_value=-1e30),
                                     reads=[sc, m8[:, 0:8]], writes=[sc2])
                                P.op("dve", lambda e: e.max(out=m8[:, 8:16], in_=sc2), reads=[sc2], writes=[m8[:, 8:16]])
                                o0 = 64 * (1 - w)
                                ts("dve", mbw[w][:, o0:o0 + 64], sc, m8[:, 15:16], 1.0, ALU.is_ge, ALU.subtract)
                                b6 = bank_bf(6)
                                trp(b6[:, 0:128], mbw[w], identb)
                                copy("act", qs_w[o0:o0 + 64, :].rearrange("p (h q) -> p h q", h=4),
                                     b6[o0:o0 + 64, 0:128].unsqueeze(1).broadcast_to([64, 4, 128]))
                            tt("dve", sg_[:, 0, :], dn[:, 0, :], gsl[:, :, 0], ALU.mult)
                            for hp in range(2):
                                ob = psb[4 + hp][:, 0:258].rearrange("p (h c) -> p h c", h=2)
                                tt("dve", oav[:, 2 * hp:2 * hp + 2, :], ob[:, :, 0:64],
                                   sg_[:, 0, 2 * hp:2 * hp + 2].unsqueeze(2).broadcast_to([128, 2, 64]), ALU.mult)
                        posts.append(c_post)
                    push(dict(score=c_score, pv=c_pv, post=posts))
                for br in (2, 1):
                    kT = KX[w] if br == 1 else kwnT
                    vv = vsl if br == 1 else vwn
                    js = list(range(0, t + 1)) if br == 1 else list(range(max(0, t - 4), t + 1))
                    obank = 3 if br == 2 else (2 if par == 0 else 7)
                    ob = psb[obank][:, 0:260].rearrange("p (h c) -> p h c", h=4)
                    for ji, j in enumerate(js):
                        st = {}

                        def b_score(br=br, j=j, t=t, kT=kT, qw=qw, st=st, use_sel=use_sel, qs_w=qs_w):
                            pb = score_bank()
                            extra = []
                            if br == 1 and use_sel:
                                qw = qs_w
                            if j == t:
                                extra.append(("m", diagm))
                            if br == 2 and j == t - 4:
                                extra.append(("m", winm))
                            mm(pb[:, :], kT[:, j * 128:(j + 1) * 128], qw, True, len(extra) == 0)
                            for xi, (kind, mk_) in enumerate(extra):
                                lastx = xi == len(extra) - 1
                                mm(pb[:, :], identb, mk_, False, lastx)
                            e = next_et()
                            act(e, pb[:, :], AF.Exp)
                            st["e"] = e

                        def b_pv(ob=ob, vv=vv, j=j, w=w, st=st, ji=ji, nj=len(js)):
                            e = st["e"]
                            for h in range(4):
                                mm(ob[:, h, :], e[:, h * 128:(h + 1) * 128], vv[:, j, w, :], ji == 0 and h == 0,
                                   (ji == nj - 1) and h == 3, sgc=True)

                        posts = []
                        if ji == len(js) - 1:
                            def b_post(br=br, ob=ob, dn=dn, sg_=sg_, otmp=otmp, oav=oav, gsl=gsl):
                                ts("dve", dn[:, br, :], ob[:, :, 64], 1e-30, None, ALU.max)
                                recip(dn[:, br, :], dn[:, br, :])
                                tt("dve", sg_[:, br, :], dn[:, br, :], gsl[:, :, br], ALU.mult)
                                ov_ = otmp.rearrange("p (h d) -> p h d", h=4)
                                tt("dve", ov_, ob[:, :, 0:64], sg_[:, br, :].unsqueeze(2).broadcast_to([128, 4, 64]), ALU.mult)
                                tt("dve", oav, oav, ov_, ALU.add)
                            posts.append(b_post)
                            if br == 1 and w == 1:
                                def t_post(t=t, oa=oa):
                                    ob_ = onb[t % 2]
                                    copy("act", ob_, oa)
                                    dtoks.append(dma("pool", mix[t * 128:(t + 1) * 128, 0:512], ob_))
                                posts.append(t_post)
                        push(dict(score=b_score, pv=b_pv, post=posts))
        flush()
        if stop_after == "B":
            return dtoks

        c3 = Carver(KEND, TOT)
        wg_b = alloc(c3, [8, DFF], BF16)
        wu_b = alloc(c3, [8, DFF], BF16)
        wd_b = alloc(c3, [22, D], BF16)
        wo_b = alloc(c3, [8, D], BF16)
        HW = DFF // 2
        fng = alloc(c3, [D], F32) if is_final else None
        stg_off3 = c3.off
        stg3 = [alloc(c3, [HW], F32) for _ in range(4)]
        c3 = Carver(stg_off3, TOT)
        si = 0
        for k in range(8):
            s = stg3[si % 4]; si += 1
            dma("sp", s[:, 0:D], Wd["w_out"][k * 128:(k + 1) * 128, :])
            copy("pool", wo_b[:, k, :], s[:, 0:D])
        for (wsrc, wdst) in (("w_gate", wg_b), ("w_up", wu_b)):
            for k in range(8):
                for hh in range(2):
                    s = stg3[si % 4]; si += 1
                    dma("sp", s, Wd[wsrc][k * 128:(k + 1) * 128, hh * HW:(hh + 1) * HW])
                    ts("pool", wdst[:, k, hh * HW:(hh + 1) * HW], s, ptT[:, 8 + k:9 + k], 1.0, ALU.mult, ALU.mult)
        for c2 in range(22):
            s = stg3[si % 4]; si += 1
            dma("sp", s[:, 0:D], Wd["w_down"][c2 * 128:(c2 + 1) * 128, :])
            copy("pool", wd_b[:, c2, :], s[:, 0:D])
        if is_final:
            dma("sp", fng, fnorm_d.partition_broadcast(128))
        mxb = [alloc(c3, [D], BF16), alloc(c3, [D], BF16)]
        xb3 = [alloc(c3, [D], F32), alloc(c3, [D], F32)]
        x1b2 = [alloc(c3, [D], F32), alloc(c3, [D], F32)]
        h3T = alloc(c3, [8, 128], BF16)
        mxT = alloc(c3, [8, 128], BF16)
        h3 = alloc(c3, [D], BF16)
        ss32 = [alloc(c3, [4], F32), alloc(c3, [4], F32)]
        sil = [alloc(c3, [512], F32), alloc(c3, [512], F32)]
        aT_in = alloc(c3, [DFF], BF16)
        junk3 = aT_in[:, 0:D]
        aT = alloc(c3, [22, 128], BF16)

        toks = []

        def L(t):
            dma("sp", mxb[t % 2], mix[t * 128:(t + 1) * 128, :])
            dma("sp", xb3[t % 2], x_src[t * 128:(t + 1) * 128, :])

        def Fa(t):
            mx = mxb[t % 2]
            b0 = bank_bf(0)
            for k in range(8):
                trp(b0[:, k * 128:(k + 1) * 128], mx[:, k * 128:(k + 1) * 128], identb)
            copy("act", mxT, b0.rearrange("p (k t) -> p k t", k=8))

        def Fb(t):
            xs = xb3[t % 2]
            x1b = x1b2[t % 2]
            ss3 = ss32[t % 2]
            for hf in range(2):
                for k in range(8):
                    mm(psb[4 + hf][:, :], mxT[:, k, :], wo_b[:, k, hf * 512:(hf + 1) * 512], k == 0, k == 7)
            for hf in range(2):
                tt("dve", x1b[:, hf * 512:(hf + 1) * 512], psb[4 + hf][:, :], xs[:, hf * 512:(hf + 1) * 512], ALU.add)
            act(junk3, x1b, AF.Square, accum=ss3[:, 0:1])
            act(ss3[:, 1:2], ss3[:, 0:1], AF.Sqrt, scale=1.0 / D, bias=EPS)
            recip(ss3[:, 2:3], ss3[:, 1:2])
            ts("dve", h3, x1b, ss3[:, 2:3], None, ALU.mult)

        def Fc(t):
            b1 = bank_bf(1)
            for k in range(8):
                trp(b1[:, k * 128:(k + 1) * 128], h3[:, k * 128:(k + 1) * 128], identb)
            copy("act", h3T, b1.rearrange("p (k t) -> p k t", k=8))

        def G(t):
            for n in range(6):
                n0 = n * 512
                nn = min(512, DFF - n0)
                pg = psb[2]
                pu = psb[3]
                for k in range(8):
                    mm(pg[:, 0:nn], h3T[:, k, :], wg_b[:, k, n0:n0 + nn], k == 0, k == 7)
                for k in range(8):
                    mm(pu[:, 0:nn], h3T[:, k, :], wu_b[:, k, n0:n0 + nn], k == 0, k == 7)
                sl = sil[n % 2]
                act(sl[:, 0:nn], pg[:, 0:nn], AF.Silu)
                tt("dve", aT_in[:, n0:n0 + nn], sl[:, 0:nn], pu[:, 0:nn], ALU.mult)

        def T(t):
            for r in range(3):
                bb = bank_bf((r + 1) % 2)
                c_lo = r * 8
                c_hi = min(22, c_lo + 8)
                for c in range(c_lo, c_hi):
                    trp(bb[:, (c - c_lo) * 128:(c - c_lo + 1) * 128], aT_in[:, c * 128:(c + 1) * 128], identb)
                copy("act" if r != 1 else "dve", aT[:, c_lo:c_hi, :],
                     bb[:, 0:(c_hi - c_lo) * 128].rearrange("p (c t) -> p c t", c=c_hi - c_lo))

        def Dn(t):
            x1b = x1b2[t % 2]
            ss3 = ss32[t % 2]
            for hf in range(2):
                for c in range(22):
                    mm(psb[6 + hf][:, :], aT[:, c, :], wd_b[:, c, hf * 512:(hf + 1) * 512], c == 0, c == 21)
            x2 = xb3[t % 2]
            for hf in range(2):
                tt("dve", x2[:, hf * 512:(hf + 1) * 512], psb[6 + hf][:, :], x1b[:, hf * 512:(hf + 1) * 512], ALU.add)
            if is_final:
                act(x1b, x2, AF.Square, accum=ss3[:, 0:1])
                act(ss3[:, 1:2], ss3[:, 0:1], AF.Sqrt, scale=1.0 / D, bias=EPS)
                recip(ss3[:, 2:3], ss3[:, 1:2])
                stt(x2, x2, ss3[:, 2:3], fng, ALU.mult, ALU.mult)
            toks.append(dma("pool", x_dst[t * 128:(t + 1) * 128, :], x2))

        L(0)
        Fa(0)
        Fb(0)
        Fc(0)
        G(0)
        for t in range(NT):
            nx = t + 1 < NT
            if nx:
                L(t + 1)
                Fa(t + 1)
            T(t)
            if nx:
                Fb(t + 1)
            Dn(t)
            if nx:
                Fc(t + 1)
                G(t + 1)
        return toks + (dtoks if dbg else [])

    if stacked:
        xmid = [P.dram("xs%d" % i, [S, D], F32) for i in range(2)]
        toks = None
        for l in range(nlayers):
            P.group = l
            for n, _ in WNAMES:
                Wd[n] = Wfull[n][l]
            src = x_in if l == 0 else xmid[(l - 1) % 2]
            dst = y_out if l == nlayers - 1 else xmid[l % 2]
            toks = layer(src, dst, final and l == nlayers - 1)
    else:
        toks = layer(x_src, y_out, final)
    P.emit(final_waits=toks)
    nc._prog_stats = {e: len(P.instrs[e]) for e in P.ENGS}
    return nc


_CACHE = {}


def kernel(**inputs):
    x = np.ascontiguousarray(np.asarray(inputs["x"], dtype=np.float32))
    B = x.shape[0]
    cb, cf = _consts()
    if "nc" not in _CACHE:
        _CACHE["nc"] = build_layer(final=True, nlayers=DEPTH, stacked=True)
    nc = _CACHE["nc"]
    shared = {"cb": cb, "cf": cf,
              "final_norm": np.ascontiguousarray(np.asarray(inputs["final_norm"], dtype=np.float32))}
    for n, _ in WNAMES:
        shared[n] = np.ascontiguousarray(np.asarray(inputs[n], dtype=np.float32))
    in_maps = []
    for b in range(B):
        m = dict(shared)
        m["x"] = x[b]
        in_maps.append(m)
    res = run_bass_kernel_spmd(nc, in_maps, core_ids=list(range(B)))
    return np.stack([np.asarray(res.results[b]["y"], dtype=np.float32) for b in range(B)], axis=0)
```

```python
import numpy as np
import ml_dtypes
import concourse.bass as bass
import concourse.mybir as mybir
from concourse.bass_utils import run_bass_kernel_spmd

F32 = mybir.dt.float32
BF16 = mybir.dt.bfloat16
AF = mybir.ActivationFunctionType
ALU = mybir.AluOpType
AX = mybir.AxisListType

S = 4096
D = 1024
NT = S // 128
DIN = 2600
DFF = 2816
EPS = 1e-6
BIG = 30000.0
DEPTH = 4

_ISZ = {}


def isz(dt):
    k = str(dt)
    if k not in _ISZ:
        _ISZ[k] = mybir.dt.size(dt)
    return _ISZ[k]


class Prog:
    ENGS = ["pe", "act", "dve", "pool", "sp"]
    NDSEM = 8

    def __init__(self, nc):
        self.nc = nc
        self.instrs = {e: [] for e in self.ENGS}
        self.bpp = {}
        self.recs = {}
        self.clock = {e: {x: -1 for x in self.ENGS} for e in self.ENGS}
        self.iclock = {e: [] for e in self.ENGS}
        self.dma_cnt = {e: 0 for e in self.ENGS}
        self.dma_known = {e: set() for e in self.ENGS}
        self.marked = {e: set() for e in self.ENGS}
        self.ctx = []
        self.group = 0
        self.igroup = {e: [] for e in self.ENGS}

    def sbuf(self, name, shape, dt):
        t = self.nc.sbuf_tensor(name, list(shape), dt)
        h = t.__enter__()
        self.ctx.append(t)
        n = 1
        for s in shape[1:]:
            n *= s
        self.bpp[name] = n * isz(dt)
        return h

    def psum(self, name, shape, dt):
        t = self.nc.psum_tensor(name, list(shape), dt)
        h = t.__enter__()
        self.ctx.append(t)
        n = 1
        for s in shape[1:]:
            n *= s
        self.bpp[name] = n * isz(dt)
        return h

    def dram(self, name, shape, dt, kind="Internal"):
        t = self.nc.dram_tensor(name, list(shape), dt, kind=kind)
        self.bpp[name] = None
        return t.ap()

    def region(self, a):
        name = a.tensor.name
        sz = isz(a.dtype)
        bpp = self.bpp.get(name, None)
        ap = a.ap
        if bpp is None:
            lo = a.offset
            ext = 0
            for st, cn in ap:
                ext += (cn - 1) * abs(st)
            return (name, 0, 1, lo * sz, (lo + ext + 1) * sz)
        off = a.offset * sz
        p0 = off // bpp
        lo = off % bpp
        if name.startswith("pb"):
            q0 = (p0 // 32) * 32
            q1 = ((p0 + ap[0][1] + 31) // 32) * 32
            return (name, q0, q1, 0, bpp)
        ext = 0
        for st, cn in ap[1:]:
            ext += (cn - 1) * abs(st)
        return (name, p0, p0 + ap[0][1], lo, lo + (ext + 1) * sz)

    def op(self, eng, fn, reads=(), writes=(), dma=False):
        import os
        lim = int(os.environ.get("KMAXOPS", "0"))
        self.nops = getattr(self, "nops", 0) + 1
        if lim and self.nops > lim:
            return None
        if os.environ.get("KLOG"):
            import sys as _s
            f = _s._getframe(1)
            while f is not None and f.f_code.co_name not in ("layer", "build_layer"):
                f = f.f_back
            print("OP", self.nops, eng, f.f_lineno if f else -1)
        idx = len(self.instrs[eng])
        waits = {}
        myclk = self.clock[eng]

        def need(tok):
            if tok[0] == "dma":
                if tok in self.dma_known[eng]:
                    return
                self.dma_known[eng].add(tok)
                waits[tok] = True
            else:
                e2, i2 = tok
                if myclk[e2] >= i2:
                    return
                waits[tok] = True

        if dma:
            q = eng
            di = self.dma_cnt[q]
            self.dma_cnt[q] += 1
            mytok = ("dma", q, di)
            if di >= self.NDSEM:
                need(("dma", q, di - self.NDSEM))
        else:
            mytok = (eng, idx)

        for (aps, kind) in ((reads, "R"), (writes, "W")):
            for a in aps:
                name, p0, p1, lo, hi = self.region(a)
                lst = self.recs.get(name, [])
                keep = []
                for r in lst:
                    (rp0, rp1, rlo, rhi, rkind, rtok) = r
                    ov = not (rp1 <= p0 or p1 <= rp0 or rhi <= lo or hi <= rlo)
                    rr = name.startswith("pb") and rtok[0] != eng
                    if ov and (rkind == "W" or kind == "W" or rr) and rtok != mytok:
                        if rtok[0] != "dma" and rtok[0] == eng and not dma:
                            if eng != "pe":
                                need(rtok)
                        else:
                            need(rtok)
                    cov = rp0 >= p0 and rp1 <= p1 and rlo >= lo and rhi <= hi
                    if kind == "W" and cov:
                        continue
                    if kind == "R" and rkind == "R" and cov and rtok[0] == mytok[0] and rtok[0] != "dma":
                        continue
                    keep.append(r)
                keep.append((p0, p1, lo, hi, kind, mytok))
                self.recs[name] = keep

        red = {}
        dmaw = []
        for tok in waits:
            if tok[0] == "dma":
                dmaw.append(tok)
            else:
                red[tok[0]] = max(red.get(tok[0], -1), tok[1])
        for e2, i2 in red.items():
            self.marked[e2].add(i2)
            oc = self.iclock[e2][i2]
            for x in self.ENGS:
                if oc[x] > myclk[x]:
                    myclk[x] = oc[x]
            if i2 > myclk[e2]:
                myclk[e2] = i2
        snap = dict(myclk)
        if not dma:
            snap[eng] = max(snap[eng], idx - 1)
        self.iclock[eng].append(snap)
        self.igroup[eng].append(self.group)
        self.instrs[eng].append(dict(fn=fn, waits=red, dmaw=dmaw, dma=(mytok if dma else None)))
        return mytok

    def emit(self, final_waits=()):
        nc = self.nc
        ENGH = {"pe": "tensor", "act": "scalar", "dve": "vector", "pool": "gpsimd", "sp": "sync"}
        rank = {}
        ngroups = self.group + 1
        sems = {}
        semctx = []
        for e in self.ENGS:
            cnt = [0] * ngroups
            rank[e] = {}
            for i in sorted(self.marked[e]):
                g = self.igroup[e][i]
                cnt[g] += 1
                rank[e][i] = (g, cnt[g])
            sems[e] = []
            for g in range(ngroups):
                c = nc.semaphore("s_%s_%d" % (e, g))
                sems[e].append(c.__enter__())
                semctx.append(c)
        dsems = {}
        for q in self.ENGS:
            if self.dma_cnt[q] > 0:
                lst = []
                for k in range(self.NDSEM):
                    c = nc.semaphore("d_%s_%d" % (q, k))
                    lst.append(c.__enter__())
                    semctx.append(c)
                dsems[q] = lst
        blk = nc.Block()
        block = blk.__enter__()
        prog = self

        def mk(e):
            def body(eng):
                for idx, ins in enumerate(prog.instrs[e]):
                    for e2, i2 in ins["waits"].items():
                        g2, v2 = rank[e2][i2]
                        eng.wait_ge(sems[e2][g2], v2)
                    for (_, q, di) in ins["dmaw"]:
                        eng.wait_ge(dsems[q][di % prog.NDSEM], 16 * (di // prog.NDSEM + 1))
                    r = ins["fn"](eng)
                    if ins["dma"] is not None:
                        (_, q, di) = ins["dma"]
                        r.then_inc(dsems[q][di % prog.NDSEM], 16)
                    elif idx in rank[e]:
                        r.then_inc(sems[e][rank[e][idx][0]], 1)
                if e == "sp":
                    for tok in final_waits:
                        if tok is None:
                            continue
                        (_, q, di) = tok
                        eng.wait_ge(dsems[q][di % prog.NDSEM], 16 * (di // prog.NDSEM + 1))
            return body

        for e in self.ENGS:
            if len(self.instrs[e]) == 0 and not (e == "sp" and final_waits):
                continue
            getattr(block, ENGH[e])(mk(e))
        blk.__exit__(None, None, None)
        for c in reversed(semctx):
            c.__exit__(None, None, None)
        for c in reversed(self.ctx):
            c.__exit__(None, None, None)


CB_ID = 0
CB_DIAG = 128
CB_WIN = 640
CB_BD = 1152
CB_ONES = 1280
CB_EEXP = 1408
CB_CMASK = 5504
CB_VCA = 13696
CB_HM = 13696 + 516
CB_BD256 = CB_HM + 64
NCB = CB_BD256 + 256
CF_ID = 0
CF_BDM = 128
CF_TRIL = 256
CF_RTAB = 384
CF_COS = 576
CF_SIN = 1600
CF_CCOS = 2624
CF_CSIN = 2688
CF_CM = 2752
NCF = 2816


def _consts():
    p = np.arange(128)[:, None]
    f = np.arange(128)[None, :]
    cb = np.zeros((128, NCB), np.float32)
    cb[:, CB_ID:CB_ID + 128] = (p == f)
    diag = np.where(p <= f, 0.0, -BIG)
    win = np.where(p > f, 0.0, -BIG)
    cb[:, CB_DIAG:CB_DIAG + 512] = np.tile(diag, (1, 4))
    cb[:, CB_WIN:CB_WIN + 512] = np.tile(win, (1, 4))
    bd = ((p // 64 == f // 64) & (p <= f)).astype(np.float32)
    cb[:, CB_BD:CB_BD + 128] = bd
    cb[:, CB_ONES:CB_ONES + 128] = 1.0
    m = np.arange(4096)[None, :]
    cb[:, CB_EEXP:CB_EEXP + 4096] = np.where((m // 64) == p, BIG, 0.0)
    for ch in range(2):
        c = ch * 128 + p
        cb[:, CB_CMASK + ch * 4096:CB_CMASK + (ch + 1) * 4096] = np.where(16 * c + 31 <= m, 0.0, -BIG)
    ncmp = 255
    cs = np.arange(ncmp) * 16
    ce = cs + 32
    bs = np.arange(64) * 64
    be = bs + 64
    ov = np.clip(np.minimum(ce[:, None], be[None, :]) - np.maximum(cs[:, None], bs[None, :]), 0, None) / 32.0
    vca = np.zeros((128, 2, 2, 129), np.float32)
    for ch in range(2):
        for pp in range(128):
            c = ch * 128 + pp
            if c < ncmp:
                vca[pp, ch, :, 64] = 1.0
                vca[pp, ch, :, 65:129] = ov[c][None, :]
    cb[:, CB_VCA:CB_VCA + 516] = vca.reshape(128, 516)
    for h in range(4):
        cb[:, CB_HM + h] = (np.arange(128) // 32 == h)
    cb[:, CB_BD256:CB_BD256 + 256] = ((np.arange(128)[:, None] // 32) == (np.arange(256)[None, :] // 64))

    cf = np.zeros((128, NCF), np.float32)
    cf[:, CF_ID:CF_ID + 128] = (p == f)
    cf[:, CF_BDM:CF_BDM + 128] = -bd / 16.0
    cf[:, CF_TRIL:CF_TRIL + 128] = (f <= p)
    j = np.arange(192)[None, :]
    mm = j - 62
    tbrel = (p >= 64).astype(np.int64)
    r = np.zeros((128, 192), np.float32)
    r[(mm == tbrel) | (mm == tbrel - 1)] = 1e4
    r[mm > tbrel] = -1e30
    cf[:, CF_RTAB:CF_RTAB + 192] = r
    half = 32
    inv = (1.0 / (np.float32(10000.0) ** (np.arange(half, dtype=np.float32) / np.float32(half)))).astype(np.float32)
    pos = (np.arange(32)[None, :] * 128 + np.arange(128)[:, None]).astype(np.float32)
    ang = (pos[:, :, None] * inv[None, None, :]).astype(np.float32)
    cf[:, CF_COS:CF_COS + 1024] = np.cos(ang).astype(np.float32).reshape(128, 1024)
    cf[:, CF_SIN:CF_SIN + 1024] = np.sin(ang).astype(np.float32).reshape(128, 1024)
    cpos = ((np.arange(2)[None, :] * 128 + np.arange(128)[:, None]) * 16 + 31).astype(np.float32)
    cang = (cpos[:, :, None] * inv[None, None, :]).astype(np.float32)
    cf[:, CF_CCOS:CF_CCOS + 64] = np.cos(cang).astype(np.float32).reshape(128, 64)
    cf[:, CF_CSIN:CF_CSIN + 64] = np.sin(cang).astype(np.float32).reshape(128, 64)
    for c in range(2):
        cf[:, CF_CM + c] = (np.arange(128) // 64 == c)
    return cb.astype(ml_dtypes.bfloat16), cf


WNAMES = [("attn_norm", [D]), ("w_in", [D, DIN]), ("cmp_pos_k", [32, 64]), ("cmp_w1_k", [2048, 256]),
          ("cmp_w2_k", [256, 64]), ("cmp_pos_v", [32, 64]), ("cmp_w1_v", [2048, 256]), ("cmp_w2_v", [256, 64]),
          ("gla_w_up", [16, 128]), ("gla_b_up", [128]), ("gla_norm", [64]), ("sg_ln_g", [256]), ("sg_ln_b", [256]),
          ("sg_w", [4, 128, 128]), ("sg_b", [4, 128]), ("w_out", [D, D]), ("ffn_norm", [D]),
          ("w_gate", [D, DFF]), ("w_up", [D, DFF]), ("w_down", [DFF, D])]


def build_layer(final=False, dbg=False, stop_after="C", nlayers=1, stacked=False):
    nc = bass.Bass("TRN2", target_bir_lowering=False)
    P = Prog(nc)
    x_in = P.dram("x", [S, D], F32, kind="ExternalInput")
    cb_d = P.dram("cb", [128, NCB], BF16, kind="ExternalInput")
    cf_d = P.dram("cf", [128, NCF], F32, kind="ExternalInput")
    if stacked:
        Wfull = {n: P.dram(n, [DEPTH] + s, F32, kind="ExternalInput") for n, s in WNAMES}
        Wd = {}
    else:
        Wd = {n: P.dram(n, s, F32, kind="ExternalInput") for n, s in WNAMES}
    fnorm_d = P.dram("final_norm", [D], F32, kind="ExternalInput")
    y_out = P.dram("y", [S, D], F32, kind="ExternalOutput")
    sk = "ExternalOutput" if dbg else "Internal"
    qTs = P.dram("qTs", [NT, 128, 512], BF16, kind=sk)
    mix = P.dram("mixs", [S, D], BF16, kind=sk)
    if dbg:
        d_kcT = P.dram("d_kcT", [128, 256], BF16, kind="ExternalOutput")
        d_vca = P.dram("d_vca", [128, 516], BF16, kind="ExternalOutput")
        d_kslT = P.dram("d_kslT", [128, 4096], BF16, kind="ExternalOutput")
        d_vsl = P.dram("d_vsl", [128, 32 * 2 * 65], BF16, kind="ExternalOutput")
        d_gates = P.dram("d_gates", [128, 32 * 24], F32, kind="ExternalOutput")
    dtoks = []

    ARENA = 104448
    arena = P.sbuf("arena", [128, ARENA], BF16)
    psb = [P.psum("pb%d" % i, [128, 512], F32) for i in range(8)]

    class Carver:
        def __init__(self, base, limit):
            self.off = base
            self.limit = limit

        def take(self, nbytes):
            nbytes = (nbytes + 63) // 64 * 64
            o = self.off
            self.off += nbytes
            assert self.off <= self.limit, (self.off, self.limit)
            return o

    def view(off_bytes, shape, dt, parts=128):
        n = 1
        for s in shape:
            n *= s
        if dt == BF16:
            a = arena[0:parts, off_bytes // 2: off_bytes // 2 + n]
        else:
            a = arena[0:parts, off_bytes // 2: off_bytes // 2 + n * 2].bitcast(F32)
        if len(shape) == 1:
            return a
        names = " ".join("d%d" % i for i in range(len(shape)))
        kw = {"d%d" % i: shape[i] for i in range(len(shape))}
        return a.rearrange("p (%s) -> p %s" % (names, names), **kw)

    def alloc(cv, shape, dt, parts=128):
        n = 1
        for s in shape:
            n *= s
        return view(cv.take(n * isz(dt)), shape, dt, parts)

    TOT = ARENA * 2
    cvK = Carver(0, 8 * 1024)
    identb = alloc(cvK, [128], BF16)
    identf = alloc(cvK, [128], F32)
    diagm = alloc(cvK, [512], BF16)
    winm = alloc(cvK, [512], BF16)
    bdmask = alloc(cvK, [128], BF16)
    onesb = alloc(cvK, [128], BF16)
    bdmf = alloc(cvK, [128], F32)
    trilf = alloc(cvK, [128], F32)
    rtab = alloc(cvK, [192], F32)
    ptT = alloc(cvK, [128], F32)
    gng = alloc(cvK, [64], F32)
    bupb = alloc(cvK, [128], BF16)
    wupb = alloc(cvK, [128], BF16)
    hm4 = alloc(cvK, [4], BF16)
    bd256 = alloc(cvK, [256], BF16)
    cm2 = alloc(cvK, [2], F32)
    KEND = cvK.off

    def dma(q, out, in_):
        return P.op(q, lambda e: e.dma_start(out=out, in_=in_), reads=[in_], writes=[out], dma=True)

    def act(out, in_, func, scale=None, bias=None, accum=None):
        kw = {}
        rd = [in_]
        wr = [out]
        if scale is not None:
            kw["scale"] = scale
            if not isinstance(scale, (int, float)):
                rd.append(scale)
        if bias is not None:
            kw["bias"] = bias
            if not isinstance(bias, (int, float)):
                rd.append(bias)
        if accum is not None:
            kw["accum_out"] = accum
            wr.append(accum)
        return P.op("act", lambda e: e.activation(out=out, in_=in_, func=func, **kw), reads=rd, writes=wr)

    def mm(out, lhsT, rhs, start, stop, tp=None, sgc=False):
        kw = {}
        if tp is not None:
            kw["tile_position"] = tp
        if sgc:
            kw["skip_group_check"] = True
        return P.op("pe", lambda e: e.matmul(out, lhsT=lhsT, rhs=rhs, start=start, stop=stop, **kw),
                    reads=[lhsT, rhs], writes=[out])

    def trp(out, in_, ident):
        return P.op("pe", lambda e: e.transpose(out, in_, ident), reads=[in_, ident], writes=[out])

    def tt(eng, out, in0, in1, op):
        return P.op(eng, lambda e: e.tensor_tensor(out=out, in0=in0, in1=in1, op=op), reads=[in0, in1], writes=[out])

    def ts(eng, out, in0, s1, s2, op0, op1=None):
        rd = [in0]
        if not isinstance(s1, (int, float)):
            rd.append(s1)
        if s2 is not None and not isinstance(s2, (int, float)):
            rd.append(s2)
        if op1 is None:
            return P.op(eng, lambda e: e.tensor_scalar(out=out, in0=in0, scalar1=s1, scalar2=None, op0=op0),
                        reads=rd, writes=[out])
        return P.op(eng, lambda e: e.tensor_scalar(out=out, in0=in0, scalar1=s1, scalar2=s2, op0=op0, op1=op1),
                    reads=rd, writes=[out])

    def stt(out, in0, sc, in1, op0, op1):
        rd = [in0, in1]
        if not isinstance(sc, (int, float)):
            rd.append(sc)
        return P.op("dve", lambda e: e.scalar_tensor_tensor(out=out, in0=in0, scalar=sc, in1=in1, op0=op0, op1=op1),
                    reads=rd, writes=[out])

    def copy(eng, out, in_):
        if eng == "act":
            return act(out, in_, AF.Copy)
        return P.op(eng, lambda e: e.tensor_copy(out=out, in_=in_), reads=[in_], writes=[out])

    def memset(eng, out, val):
        return P.op(eng, lambda e: e.memset(out, val), reads=[], writes=[out])

    def recip(out, in_):
        return P.op("dve", lambda e: e.reciprocal(out=out, in_=in_), reads=[in_], writes=[out])

    def bank_bf(i):
        return psb[i][:].bitcast(BF16)

    dma("sp", identb, cb_d[:, CB_ID:CB_ID + 128])
    dma("sp", diagm, cb_d[:, CB_DIAG:CB_DIAG + 512])
    dma("sp", winm, cb_d[:, CB_WIN:CB_WIN + 512])
    dma("sp", bdmask, cb_d[:, CB_BD:CB_BD + 128])
    dma("sp", onesb, cb_d[:, CB_ONES:CB_ONES + 128])
    dma("sp", identf, cf_d[:, CF_ID:CF_ID + 128])
    dma("sp", bdmf, cf_d[:, CF_BDM:CF_BDM + 128])
    dma("sp", trilf, cf_d[:, CF_TRIL:CF_TRIL + 128])
    dma("sp", rtab, cf_d[:, CF_RTAB:CF_RTAB + 192])
    dma("sp", hm4, cb_d[:, CB_HM:CB_HM + 4])
    dma("sp", bd256, cb_d[:, CB_BD256:CB_BD256 + 256])
    dma("sp", cm2, cf_d[:, CF_CM:CF_CM + 2])

    x_src = x_in
    fins = []

    def layer(x_src, x_dst, is_final):
        cv = Carver(KEND, TOT)
        cmask = alloc(cv, [2, 4096], BF16)
        KX = [alloc(cv, [4096], BF16), alloc(cv, [4096], BF16)]
        cosT = alloc(cv, [32, 32], F32)
        sinT = alloc(cv, [32, 32], F32)
        ccos = alloc(cv, [2, 32], F32)
        csin = alloc(cv, [2, 32], F32)
        gates = alloc(cv, [32, 24], F32)
        kwnT = alloc(cv, [4096], BF16)
        vsl = alloc(cv, [32, 2, 65], BF16)
        vwn = alloc(cv, [32, 2, 65], BF16)
        kcrT = alloc(cv, [4096], BF16)
        vcrT = alloc(cv, [4096], BF16)
        kcT = alloc(cv, [256], BF16)
        vca = alloc(cv, [2, 2, 129], BF16)
        lng = alloc(cv, [256], F32)
        lnb = alloc(cv, [256], F32)
        sgWT = alloc(cv, [4, 128], BF16)
        st_f = alloc(cv, [256], F32)
        st_b = alloc(cv, [2, 256], BF16)
        w_off = cv.off
        win_b = alloc(cv, [8, DIN], BF16)
        wk_off = cv.off

        dma("sp", cmask, cb_d[:, CB_CMASK:CB_CMASK + 8192].rearrange("p (a b) -> p a b", a=2))
        dma("sp", KX[0][64:128, :], cb_d[0:64, CB_EEXP:CB_EEXP + 4096])
        dma("sp", KX[1][0:64, :], cb_d[0:64, CB_EEXP:CB_EEXP + 4096])
        dma("sp", cosT, cf_d[:, CF_COS:CF_COS + 1024].rearrange("p (a b) -> p a b", a=32))
        dma("sp", sinT, cf_d[:, CF_SIN:CF_SIN + 1024].rearrange("p (a b) -> p a b", a=32))
        dma("sp", ccos, cf_d[:, CF_CCOS:CF_CCOS + 64].rearrange("p (a b) -> p a b", a=2))
        dma("sp", csin, cf_d[:, CF_CSIN:CF_CSIN + 64].rearrange("p (a b) -> p a b", a=2))
        dma("sp", vca, cb_d[:, CB_VCA:CB_VCA + 516].rearrange("p (a b c) -> p a b c", a=2, b=2))
        dma("sp", gng, Wd["gla_norm"].partition_broadcast(128))
        dma("sp", lng, Wd["sg_ln_g"].partition_broadcast(128))
        dma("sp", lnb, Wd["sg_ln_b"].partition_broadcast(128))

        cw = Carver(wk_off, TOT)
        stg = [alloc(cw, [DFF], F32), alloc(cw, [DFF], F32)]
        stg_i = [0]

        def stage():
            s = stg[stg_i[0] % 2]
            stg_i[0] += 1
            return s

        pt = stage()[:, 0:128]
        memset("pool", pt, 0.0)
        dma("sp", pt[0:8, :], Wd["attn_norm"].rearrange("(k p) -> k p", p=128))
        dma("sp", pt[8:16, :], Wd["ffn_norm"].rearrange("(k p) -> k p", p=128))
        dma("sp", pt[16:20, :], Wd["sg_b"])
        dma("sp", pt[32:64, 0:64], Wd["cmp_pos_k"])
        dma("sp", pt[64:96, 0:64], Wd["cmp_pos_v"])
        trp(psb[7][:, 0:128], pt, identf)
        copy("dve", ptT, psb[7][:, 0:128])
        s2 = stage()
        dma("sp", s2[0:16, 0:128], Wd["gla_w_up"])
        dma("sp", s2[0:1, 128:256], Wd["gla_b_up"].rearrange("(o n) -> o n", o=1))
        copy("pool", wupb[0:16, :], s2[0:16, 0:128])
        copy("pool", bupb[0:1, :], s2[0:1, 128:256])
        s3 = stage()
        s3v = s3[:, 0:512].rearrange("p (g j) -> p g j", g=4)
        dma("sp", s3v, Wd["sg_w"].rearrange("g i j -> i g j"))
        sgm = alloc(cw, [4, 128], BF16)
        tt("dve", sgm, s3v, trilf.unsqueeze(1).broadcast_to([128, 4, 128]), ALU.mult)
        b7b = bank_bf(7)
        for g in range(4):
            trp(b7b[:, 256 + g * 128: 256 + (g + 1) * 128], sgm[:, g, :], identb)
        copy("dve", sgWT, b7b[:, 256:768].rearrange("p (g i) -> p g i", g=4))
        for k in range(8):
            s = stage()
            dma("sp", s[:, 0:DIN], Wd["w_in"][k * 128:(k + 1) * 128, :])
            ts("pool", win_b[:, k, :], s[:, 0:DIN], ptT[:, k:k + 1], 1.0, ALU.mult, ALU.mult)
        memset("pool", vsl, 1.0)
        memset("pool", vwn, 1.0)
        memset("pool", st_f, 0.0)
        memset("pool", st_b, 0.0)
        if NT != 32:
            memset("pool", kcrT, 0.0)
            memset("pool", vcrT, 0.0)

        xbuf = [alloc(cw, [D], F32), alloc(cw, [D], F32)]
        hbf = alloc(cw, [D], BF16)
        junk = hbf
        hT = [alloc(cw, [8, 128], BF16), alloc(cw, [8, 128], BF16)]
        ssq = alloc(cw, [4], F32)
        ri = alloc(cw, [12, 64], F32)
        ro = alloc(cw, [768], BF16)
        rt1 = alloc(cw, [12, 32], F32)
        rt2 = alloc(cw, [12, 32], F32)
        qTst = [alloc(cw, [512], BF16), alloc(cw, [512], BF16)]
        rawb = alloc(cw, [256], BF16)
        alT = alloc(cw, [128], BF16)
        e1 = alloc(cw, [128], F32)
        spl = alloc(cw, [128], F32)
        epn = alloc(cw, [2, 128], F32)
        bcs = alloc(cw, [128], F32)
        qkg = alloc(cw, [2, 128], BF16)
        kgT = alloc(cw, [128], BF16)
        qTm = alloc(cw, [4, 128], BF16)
        qTc = alloc(cw, [2, 128], BF16)
        kgc = alloc(cw, [2, 128], BF16)
        tS2 = alloc(cw, [256], F32)
        memset("pool", qTc, 0.0)
        vg = alloc(cw, [256], BF16)
        attm = alloc(cw, [4, 128], BF16)
        ebc = alloc(cw, [2], F32)
        tS = alloc(cw, [256], F32)
        og = alloc(cw, [256], F32)
        og2 = alloc(cw, [256], F32)
        gms = alloc(cw, [8], F32)
        sir = alloc(cw, [256], F32)
        mixt = [alloc(cw, [512], BF16), alloc(cw, [512], BF16)]
        guv = alloc(cw, [512], F32)
        bns = alloc(cw, [8], F32)
        bna = alloc(cw, [4], F32)
        vn = alloc(cw, [256], F32)
        vnb = alloc(cw, [256], BF16)

        for t in range(NT):
            xs = xbuf[t % 2]
            hTt = hT[t % 2]
            dma("sp", xs, x_src[t * 128:(t + 1) * 128, :])
            act(junk, xs, AF.Square, accum=ssq[:, 0:1])
            act(ssq[:, 1:2], ssq[:, 0:1], AF.Sqrt, scale=1.0 / D, bias=EPS)
            recip(ssq[:, 2:3], ssq[:, 1:2])
            ts("dve", hbf, xs, ssq[:, 2:3], None, ALU.mult)
            b0 = bank_bf(0)
            for k in range(8):
                trp(b0[:, k * 128:(k + 1) * 128], hbf[:, k * 128:(k + 1) * 128], identb)
            copy("act", hTt, b0.rearrange("p (k t) -> p k t", k=8))
            chunks = [(0, 512, 1), (512, 1024, 2), (1024, 1432, 3), (1432, 1832, 4), (1832, 2088, 5), (2088, 2600, 6)]
            for (c0, c1, bk) in chunks:
                for k in range(8):
                    mm(psb[bk][:, 0:c1 - c0], hTt[:, k, :], win_b[:, k, c0:c1], k == 0, k == 7)
            for k in range(8):
                mm(psb[7][0:16, 256:384], win_b[:, k, 1816:1832], hTt[:, k, :], k == 0, k == 7)
            riv = ri.rearrange("p (w r) d -> p w r d", w=2)
            act(riv[:, :, 0:4, :], psb[1][:, 0:512].rearrange("p (w r d) -> p w r d", w=2, r=4), AF.Copy, scale=0.125)
            copy("dve", riv[:, :, 4, :], psb[2][:, 256:384].rearrange("p (w d) -> p w d", w=2))
            copy("dve", riv[:, :, 5, :], psb[3][:, 0:128].rearrange("p (w d) -> p w d", w=2))
            copy("act", rawb, psb[2][:, 0:256])
            copy("dve", vsl[:, t, :, 0:64], psb[2][:, 384:512].rearrange("p (w d) -> p w d", w=2))
            copy("dve", vwn[:, t, :, 0:64], psb[3][:, 128:256].rearrange("p (w d) -> p w d", w=2))
            act(gates[:, t, :], psb[3][:, 256:280], AF.Sigmoid)
            act(sir, psb[5][:, 0:256], AF.Silu)
            r4 = ri.rearrange("p (w r) (two d) -> p w r two d", w=2, two=2)
            x1 = r4[:, :, :, 0, :]
            x2 = r4[:, :, :, 1, :]
            cs_ = cosT[:, t, :].unsqueeze(1).unsqueeze(1).broadcast_to([128, 2, 6, 32])
            sn_ = sinT[:, t, :].unsqueeze(1).unsqueeze(1).broadcast_to([128, 2, 6, 32])
            t1 = rt1.rearrange("p (w r) d -> p w r d", w=2)
            t2 = rt2.rearrange("p (w r) d -> p w r d", w=2)
            rov = ro.rearrange("p (r w two d) -> p w r two d", r=6, w=2, two=2)
            tt("dve", t1, x1, cs_, ALU.mult)
            tt("dve", t2, x2, sn_, ALU.mult)
            tt("dve", rov[:, :, :, 0, :], t1, t2, ALU.subtract)
            tt("dve", t1, x2, cs_, ALU.mult)
            tt("dve", t2, x1, sn_, ALU.mult)
            tt("dve", rov[:, :, :, 1, :], t1, t2, ALU.add)
            b1 = bank_bf(1)
            for r in range(6):
                trp(b1[:, r * 128:(r + 1) * 128], ro[:, r * 128:(r + 1) * 128], identb)
            qst = qTst[t % 2]
            copy("act", qst, b1[:, 0:512])
            dma("pool", qTs[t], qst)
            copy("dve", KX[0][0:64, t * 128:(t + 1) * 128], b1[0:64, 512:640])
            copy("dve", KX[1][64:128, t * 128:(t + 1) * 128], b1[64:128, 512:640])
            copy("dve", kwnT[:, t * 128:(t + 1) * 128], b1[:, 640:768])
            b2 = bank_bf(2)
            trp(b2[:, 0:128], rawb[:, 0:128], identb)
            trp(b2[:, 128:256], rawb[:, 128:256], identb)
            copy("act", kcrT[:, t * 128:(t + 1) * 128], b2[:, 0:128])
            copy("act", vcrT[:, t * 128:(t + 1) * 128], b2[:, 128:256])
            copy("dve", alT[0:16, :], psb[7][0:16, 256:384])
            mm(psb[7][:, 0:128], alT[0:16, :], wupb[0:16, :], True, False)
            mm(psb[7][:, 0:128], onesb[0:1, :], bupb[0:1, :], False, True)
            act(e1, psb[7][:, 0:128], AF.Exp, scale=-1.0)
            act(spl, e1, AF.Ln, bias=1.0)
            mm(psb[7][:, 128:256], bdmf, spl, True, True)
            act(epn[:, 0, :], psb[7][:, 128:256], AF.Exp)
            act(epn[:, 1, :], psb[7][:, 128:256], AF.Exp, scale=-1.0)
            copy("dve", bcs, psb[7][:, 128:256])
            stt(qkg[:, 0, :], psb[3][:, 280:408], 32.0 ** -0.5, epn[:, 0, :], ALU.mult, ALU.mult)
            tt("dve", qkg[:, 1, :], psb[4][:, 0:128], epn[:, 1, :], ALU.mult)
            copy("act", vg, psb[4][:, 128:384])
            trp(b2[:, 256:384], qkg[:, 0, :], identb)
            trp(b2[:, 384:512], qkg[:, 1, :], identb)
            copy("act", kgT, b2[:, 384:512])
            tt("dve", qTm, b2[:, 256:384].unsqueeze(1).broadcast_to([128, 4, 128]),
               hm4.unsqueeze(2).broadcast_to([128, 4, 128]), ALU.mult)
            copy("dve", qTc[:, 0, 0:64], b2[:, 256:320])
            copy("dve", qTc[:, 1, 64:128], b2[:, 320:384])
            tt("dve", kgc, qkg[:, 1, :].unsqueeze(1).broadcast_to([128, 2, 128]),
               cm2.unsqueeze(2).broadcast_to([128, 2, 128]), ALU.mult)
            trp(psb[7][:, 384:512], bcs, identf)
            act(ebc[:, 0:1], psb[7][:, 447:448], AF.Exp)
            act(ebc[:, 1:2], psb[7][:, 511:512], AF.Exp)
            mm(psb[3][:, 0:512], kgT, qTm.rearrange("p h i -> p (h i)"), True, True)
            tt("dve", attm, psb[3][:, 0:512].rearrange("p (h i) -> p h i", h=4),
               bdmask.unsqueeze(1).broadcast_to([128, 4, 128]), ALU.mult)
            first = True
            for h in range(4):
                mm(psb[4][:, h * 64:(h + 1) * 64], attm[:, h, :], vg[:, h * 64:(h + 1) * 64], first, False, sgc=True)
                first = False
            mm(psb[4][:, 0:256], qTc[:, 0, :], st_b[:, 0, :], False, False, sgc=True)
            mm(psb[5][:, 256:512], kgc[:, 0, :], vg, True, True)
            tt("dve", tS, psb[5][:, 256:512], bd256, ALU.mult)
            tt("dve", tS2, tS, st_f, ALU.add)
            ts("dve", st_f, tS2, ebc[:, 0:1], None, ALU.mult)
            copy("dve", st_b[:, 1, :], st_f)
            mm(psb[4][:, 0:256], qTc[:, 1, :], st_b[:, 1, :], False, True, sgc=True)
            mm(psb[5][:, 256:512], kgc[:, 1, :], vg, True, True)
            tt("dve", tS, psb[5][:, 256:512], bd256, ALU.mult)
            tt("dve", tS2, tS, st_f, ALU.add)
            ts("dve", st_f, tS2, ebc[:, 1:2], None, ALU.mult)
            copy("dve", st_b[:, 0, :], st_f)
            copy("act", og, psb[4][:, 0:256])
            tt("dve", og2, og, og, ALU.mult)
            P.op("dve", lambda e: e.tensor_reduce(out=gms[:, 0:4], in_=og2.rearrange("p (h d) -> p h d", h=4),
                                                  axis=AX.X, op=ALU.add),
                 reads=[og2], writes=[gms[:, 0:4]])
            act(gms[:, 4:8], gms[:, 0:4], AF.Sqrt, scale=1.0 / 64, bias=EPS)
            recip(gms[:, 0:4], gms[:, 4:8])
            tt("dve", og2.rearrange("p (h d) -> p h d", h=4), og.rearrange("p (h d) -> p h d", h=4),
               gms[:, 0:4].unsqueeze(2).broadcast_to([128, 4, 64]), ALU.mult)
            tt("dve", og.rearrange("p (h d) -> p h d", h=4), og2.rearrange("p (h d) -> p h d", h=4),
               gng.unsqueeze(1).broadcast_to([128, 4, 64]), ALU.mult)
            mxt = mixt[t % 2]
            tt("dve", mxt[:, 0:256], og, sir, ALU.mult)
            act(guv, psb[6][:, 0:512], AF.Gelu_apprx_tanh)
            P.op("dve", lambda e: e.bn_stats(out=bns[:, 0:6], in_=guv[:, 256:512]), reads=[guv[:, 256:512]], writes=[bns[:, 0:6]])
            P.op("dve", lambda e: e.bn_aggr(out=bna[:, 0:2], in_=bns[:, 0:6]), reads=[bns[:, 0:6]], writes=[bna[:, 0:2]])
            act(bna[:, 2:3], bna[:, 1:2], AF.Sqrt, bias=EPS)
            recip(bna[:, 3:4], bna[:, 2:3])
            ts("dve", vn, guv[:, 256:512], bna[:, 0:1], bna[:, 3:4], ALU.subtract, ALU.mult)
            tt("dve", vn, vn, lng, ALU.mult)
            tt("dve", vnb, vn, lnb, ALU.add)
            for g in range(4):
                mm(psb[6][:, g * 64:(g + 1) * 64], sgWT[:, g, :], vnb[:, g * 64:(g + 1) * 64], True, True)
            for g in range(4):
                stt(mxt[:, 256 + g * 64: 256 + (g + 1) * 64], psb[6][:, g * 64:(g + 1) * 64], ptT[:, 16 + g:17 + g],
                    guv[:, g * 64:(g + 1) * 64], ALU.add, ALU.mult)
            dtoks.append(dma("pool", mix[t * 128:(t + 1) * 128, 512:1024], mxt))
        if dbg and NT == 32:
            dtoks.append(dma("pool", d_kslT[:, :], KX[0]))
            dtoks.append(dma("pool", d_vsl[:, :], vsl.rearrange("p a b c -> p (a b c)")))
            dtoks.append(dma("pool", d_gates[:, :], gates.rearrange("p a b -> p (a b)")))
        if stop_after == "A":
            return dtoks

        cc = Carver(w_off, TOT)
        w1d = [alloc(cc, [32, 256], BF16), alloc(cc, [32, 256], BF16)]
        w2b = alloc(cc, [2, 2, 64], BF16)
        posTb = alloc(cc, [64], BF16, parts=64)
        hbias = alloc(cc, [2, 2], F32)
        hidT = alloc(cc, [2, 255], BF16)
        kct = alloc(cc, [2, 128], F32)
        kcr = alloc(cc, [2, 128], BF16)
        cst = [alloc(cc, [8, 256], F32), alloc(cc, [8, 256], F32)]
        ci = 0
        for kv, nm in enumerate(["cmp_w1_k", "cmp_w1_v"]):
            src = Wd[nm].rearrange("(l d) h -> d l h", d=64)
            for part in range(2):
                for l0 in range(0, 32, 8):
                    s = cst[ci % 2]
                    ci += 1
                    dma("sp", s[64 * part:64 * part + 64], src[:, l0:l0 + 8, :])
                    ts("pool", w1d[kv][64 * part:64 * part + 64, l0:l0 + 8, :], s[64 * part:64 * part + 64], 1.0, 1.0, ALU.mult, ALU.mult)
        for kv, nm in enumerate(["cmp_w2_k", "cmp_w2_v"]):
            s = cst[ci % 2]
            ci += 1
            sv = s[:, 0, 0:128].rearrange("p (c d) -> p c d", c=2)
            dma("sp", sv, Wd[nm].rearrange("(c p) d -> p c d", p=128))
            copy("pool", w2b[:, kv, :, :], sv)
        copy("dve", posTb[0:64, :], ptT[0:64, 32:96])
        for kv in range(2):
            rawT = kcrT if kv == 0 else vcrT
            for c in range(2):
                for l in range(32):
                    mm(psb[7][:, 2 * kv + c: 2 * kv + c + 1], w1d[kv][0:64, l, c * 128:(c + 1) * 128],
                       posTb[0:64, kv * 32 + l: kv * 32 + l + 1], l == 0, l == 31)
            copy("dve", hbias[:, kv, :], psb[7][:, 2 * kv: 2 * kv + 2])
            for g in range(2):
                for c in range(2):
                    for l in range(32):
                        rv = rawT[64 * g:64 * g + 64, :].rearrange("p (c s) -> p s c", s=16)
                        rhs = rv[:, l, 0:255] if l < 16 else rv[:, l - 16, 1:256]
                        mm(psb[c + 2 * g][:, 0:255], w1d[kv][64 * g:64 * g + 64, l, c * 128:(c + 1) * 128], rhs, l == 0, l == 31)
                    act(hidT[:, c, :], psb[c + 2 * g][:, 0:255], AF.Gelu_apprx_tanh, bias=hbias[:, kv, c:c + 1])
                for (c0, cn, ch) in [(0, 128, 0), (128, 127, 1)]:
                    for c in range(2):
                        mm(psb[4 + ch][0:cn, g * 64:(g + 1) * 64], hidT[:, c, c0:c0 + cn], w2b[:, kv, c, :], c == 0, c == 1)
                    if kv == 0:
                        copy("dve", kct[0:cn, ch, g * 64:(g + 1) * 64], psb[4 + ch][0:cn, g * 64:(g + 1) * 64])
                    else:
                        copy("dve", vca[0:cn, ch, g, 0:64], psb[4 + ch][0:cn, g * 64:(g + 1) * 64])
        k5 = kct.rearrange("p c (g two d) -> p c g two d", g=2, two=2)
        o5 = kcr.rearrange("p c (g two d) -> p c g two d", g=2, two=2)
        ta = alloc(cc, [2, 2, 32], F32)
        tb = alloc(cc, [2, 2, 32], F32)
        for (cn, ch) in [(128, 0), (127, 1)]:
            a1 = k5[0:cn, ch, :, 0, :]
            a2 = k5[0:cn, ch, :, 1, :]
            cb_ = ccos[0:cn, ch, :].unsqueeze(1).broadcast_to([cn, 2, 32])
            sb_ = csin[0:cn, ch, :].unsqueeze(1).broadcast_to([cn, 2, 32])
            tt("dve", ta[0:cn, ch], a1, cb_, ALU.mult)
            tt("dve", tb[0:cn, ch], a2, sb_, ALU.mult)
            tt("dve", o5[0:cn, ch, :, 0, :], ta[0:cn, ch], tb[0:cn, ch], ALU.subtract)
            tt("dve", ta[0:cn, ch], a2, cb_, ALU.mult)
            tt("dve", tb[0:cn, ch], a1, sb_, ALU.mult)
            tt("dve", o5[0:cn, ch, :, 1, :], ta[0:cn, ch], tb[0:cn, ch], ALU.add)
            b6_ = bank_bf(6)
            trp(b6_[:, ch * 128: ch * 128 + cn], kcr[0:cn, ch, :], identb[0:cn, 0:cn])
            copy("dve", kcT[:, ch * 128: ch * 128 + cn], b6_[:, ch * 128: ch * 128 + cn])

        if dbg:
            dtoks.append(dma("pool", d_kcT[:, :], kcT))
            dtoks.append(dma("pool", d_vca[:, :], vca.rearrange("p a b c -> p (a b c)")))
        if stop_after == "K":
            return dtoks
        cb2 = Carver(cc.off, TOT)
        qt = [alloc(cb2, [2, 512], BF16), alloc(cb2, [2, 512], BF16)]
        memset("pool", qt[0], 0.0)
        memset("pool", qt[1], 0.0)
        ET = [alloc(cb2, [512], BF16) for _ in range(4)]
        et_i = [0]
        sc = alloc(cb2, [64], F32)
        sc2 = alloc(cb2, [64], F32)
        impa = alloc(cb2, [64], F32)
        m8 = alloc(cb2, [16], F32)
        mbw = [alloc(cb2, [128], BF16), alloc(cb2, [128], BF16)]
        memset("pool", mbw[0], 0.0)
        memset("pool", mbw[1], 0.0)
        qsel = [[alloc(cb2, [512], BF16) for _ in range(2)] for _ in range(2)]
        dn2 = [alloc(cb2, [3, 4], F32), alloc(cb2, [3, 4], F32)]
        sg2 = [alloc(cb2, [3, 4], F32), alloc(cb2, [3, 4], F32)]
        oacc = [alloc(cb2, [512], F32), alloc(cb2, [512], F32)]
        otmp2 = [alloc(cb2, [256], F32), alloc(cb2, [256], F32)]
        onb = [alloc(cb2, [512], BF16), alloc(cb2, [512], BF16)]
        sbank = [0]

        def score_bank():
            b = sbank[0] % 2
            sbank[0] += 1
            return psb[b]

        def next_et():
            e = ET[et_i[0] % 4]
            et_i[0] += 1
            return e

        pend = [None]

        def push(item):
            item["score"]()
            if pend[0] is not None:
                pend[0]["pv"]()
                for f in pend[0]["post"]:
                    f()
            pend[0] = item

        def flush():
            if pend[0] is not None:
                pend[0]["pv"]()
                for f in pend[0]["post"]:
                    f()
                pend[0] = None

        for t in range(NT):
            q_t = qt[t % 2]
            dma("sp", q_t[0:64, 0, :], qTs[t, 0:64, :])
            dma("sp", q_t[64:128, 1, :], qTs[t, 64:128, :])
            oa = oacc[t % 2]
            for w in range(2):
                par = (2 * t + w) % 2
                dn = dn2[par]
                sg_ = sg2[par]
                otmp = otmp2[par]
                qw = q_t[:, w, :]
                gsl = gates[:, t, :].rearrange("p (h b) -> p h b", b=3)[:, 4 * w:4 * w + 4, :]
                oav = oa[:, w * 256:(w + 1) * 256].rearrange("p (h d) -> p h d", h=4)
                use_sel = t >= 8
                qs_w = qsel[w][t % 2]
                if use_sel:
                    dma("sp", qs_w[64 * w:64 * w + 64, :], qTs[t, 64 * w:64 * w + 64, :])
                ncv = min(8 * t + 7, 255)
                chs = [(0, 128, 0)] + ([(128, 127, 1)] if ncv > 128 else [])
                for ci_, (c0, cn, ch) in enumerate(chs):
                    st = {}

                    def c_score(c0=c0, cn=cn, ch=ch, st=st, qw=qw, t=t):
                        pb = score_bank()
                        mm(pb[0:cn, :], kcT[:, c0:c0 + cn], qw, True, False)
                        mm(pb[0:cn, :].rearrange("p (h q) -> p h q", h=4), identb[:, 0:cn],
                           cmask[:, ch, t * 128:(t + 1) * 128].unsqueeze(1).broadcast_to([128, 4, 128]), False, True)
                        e = next_et()
                        act(e[0:cn, :], pb[0:cn, :], AF.Exp)
                        st["e"] = e

                    def c_pv(cn=cn, ch=ch, st=st, w=w, ci_=ci_, nch=len(chs)):
                        e = st["e"]
                        for hp in range(2):
                            ob = psb[4 + hp][:, 0:258].rearrange("p (h c) -> p h c", h=2)
                            for hh in range(2):
                                h = hp * 2 + hh
                                mm(ob[:, hh, :], e[0:cn, h * 128:(h + 1) * 128], vca[0:cn, ch, w, :],
                                   ci_ == 0 and hh == 0, ci_ == nch - 1 and hh == 1, sgc=True)

                    posts = []
                    if ci_ == len(chs) - 1:
                        def c_post(t=t, w=w, dn=dn, sg_=sg_, oav=oav, gsl=gsl, use_sel=use_sel, qs_w=qs_w):
                            for hp in range(2):
                                ob = psb[4 + hp][:, 0:258].rearrange("p (h c) -> p h c", h=2)
                                ts("dve", dn[:, 0, 2 * hp:2 * hp + 2], ob[:, :, 64], 1e-30, None, ALU.max)
                            recip(dn[:, 0, :], dn[:, 0, :])
                            if use_sel:
                                for h in range(4):
                                    ob = psb[4 + h // 2][:, 0:258].rearrange("p (h c) -> p h c", h=2)
                                    if h == 0:
                                        ts("dve", impa, ob[:, 0, 65:129], dn[:, 0, 0:1], None, ALU.mult)
                                    else:
                                        stt(impa, ob[:, h % 2, 65:129], dn[:, 0, h:h + 1], impa, ALU.mult, ALU.add)
                                tt("dve", sc, impa, rtab[:, 62 - 2 * t: 62 - 2 * t + 64], ALU.add)
                                ts("dve", sc[:, 0:1], sc[:, 0:1], 1e4, None, ALU.add)
                                P.op("dve", lambda e: e.max(out=m8[:, 0:8], in_=sc), reads=[sc], writes=[m8[:, 0:8]])
                                P.op("dve", lambda e: e.match_replace(out=sc2, in_to_replace=m8[:, 0:8], in_values=sc,
                                                                      imm_value=-1e30),
                                     reads=[sc, m8[:, 0:8]], writes=[sc2])
                                P.op("dve", lambda e: e.max(out=m8[:, 8:16], in_=sc2), reads=[sc2], writes=[m8[:, 8:16]])
                                o0 = 64 * (1 - w)
                                ts("dve", mbw[w][:, o0:o0 + 64], sc, m8[:, 15:16], 1.0, ALU.is_ge, ALU.subtract)
                                b6 = bank_bf(6)
                                trp(b6[:, 0:128], mbw[w], identb)
                                copy("act", qs_w[o0:o0 + 64, :].rearrange("p (h q) -> p h q", h=4),
                                     b6[o0:o0 + 64, 0:128].unsqueeze(1).broadcast_to([64, 4, 128]))
                            tt("dve", sg_[:, 0, :], dn[:, 0, :], gsl[:, :, 0], ALU.mult)
                            for hp in range(2):
                                ob = psb[4 + hp][:, 0:258].rearrange("p (h c) -> p h c", h=2)
                                tt("dve", oav[:, 2 * hp:2 * hp + 2, :], ob[:, :, 0:64],
                                   sg_[:, 0, 2 * hp:2 * hp + 2].unsqueeze(2).broadcast_to([128, 2, 64]), ALU.mult)
                        posts.append(c_post)
                    push(dict(score=c_score, pv=c_pv, post=posts))
                for br in (2, 1):
                    kT = KX[w] if br == 1 else kwnT
                    vv = vsl if br == 1 else vwn
                    js = list(range(0, t + 1)) if br == 1 else list(range(max(0, t - 4), t + 1))
                    obank = 3 if br == 2 else (2 if par == 0 else 7)
                    ob = psb[obank][:, 0:260].rearrange("p (h c) -> p h c", h=4)
                    for ji, j in enumerate(js):
                        st = {}

                        def b_score(br=br, j=j, t=t, kT=kT, qw=qw, st=st, use_sel=use_sel, qs_w=qs_w):
                            pb = score_bank()
                            extra = []
                            if br == 1 and use_sel:
                                qw = qs_w
                            if j == t:
                                extra.append(("m", diagm))
                            if br == 2 and j == t - 4:
                                extra.append(("m", winm))
                            mm(pb[:, :], kT[:, j * 128:(j + 1) * 128], qw, True, len(extra) == 0)
                            for xi, (kind, mk_) in enumerate(extra):
                                lastx = xi == len(extra) - 1
                                mm(pb[:, :], identb, mk_, False, lastx)
                            e = next_et()
                            act(e, pb[:, :], AF.Exp)
                            st["e"] = e

                        def b_pv(ob=ob, vv=vv, j=j, w=w, st=st, ji=ji, nj=len(js)):
                            e = st["e"]
                            for h in range(4):
                                mm(ob[:, h, :], e[:, h * 128:(h + 1) * 128], vv[:, j, w, :], ji == 0 and h == 0,
                                   (ji == nj - 1) and h == 3, sgc=True)

                        posts = []
                        if ji == len(js) - 1:
                            def b_post(br=br, ob=ob, dn=dn, sg_=sg_, otmp=otmp, oav=oav, gsl=gsl):
                                ts("dve", dn[:, br, :], ob[:, :, 64], 1e-30, None, ALU.max)
                                recip(dn[:, br, :], dn[:, br, :])
                                tt("dve", sg_[:, br, :], dn[:, br, :], gsl[:, :, br], ALU.mult)
                                ov_ = otmp.rearrange("p (h d) -> p h d", h=4)
                                tt("dve", ov_, ob[:, :, 0:64], sg_[:, br, :].unsqueeze(2).broadcast_to([128, 4, 64]), ALU.mult)
                                tt("dve", oav, oav, ov_, ALU.add)
                            posts.append(b_post)
                            if br == 1 and w == 1:
                                def t_post(t=t, oa=oa):
                                    ob_ = onb[t % 2]
                                    copy("act", ob_, oa)
                                    dtoks.append(dma("pool", mix[t * 128:(t + 1) * 128, 0:512], ob_))
                                posts.append(t_post)
                        push(dict(score=b_score, pv=b_pv, post=posts))
        flush()
        if stop_after == "B":
            return dtoks

        c3 = Carver(KEND, TOT)
        wg_b = alloc(c3, [8, DFF], BF16)
        wu_b = alloc(c3, [8, DFF], BF16)
        wd_b = alloc(c3, [22, D], BF16)
        wo_b = alloc(c3, [8, D], BF16)
        HW = DFF // 2
        fng = alloc(c3, [D], F32) if is_final else None
        stg_off3 = c3.off
        stg3 = [alloc(c3, [HW], F32) for _ in range(4)]
        c3 = Carver(stg_off3, TOT)
        si = 0
        for k in range(8):
            s = stg3[si % 4]; si += 1
            dma("sp", s[:, 0:D], Wd["w_out"][k * 128:(k + 1) * 128, :])
            ts("pool", wo_b[:, k, :], s[:, 0:D], 1.0, 1.0, ALU.mult, ALU.mult)
        for (wsrc, wdst) in (("w_gate", wg_b), ("w_up", wu_b)):
            for k in range(8):
                for hh in range(2):
                    s = stg3[si % 4]; si += 1
                    dma("sp", s, Wd[wsrc][k * 128:(k + 1) * 128, hh * HW:(hh + 1) * HW])
                    ts("pool", wdst[:, k, hh * HW:(hh + 1) * HW], s, ptT[:, 8 + k:9 + k], 1.0, ALU.mult, ALU.mult)
        for c2 in range(22):
            s = stg3[si % 4]; si += 1
            dma("sp", s[:, 0:D], Wd["w_down"][c2 * 128:(c2 + 1) * 128, :])
            ts("pool", wd_b[:, c2, :], s[:, 0:D], 1.0, 1.0, ALU.mult, ALU.mult)
        if is_final:
            dma("sp", fng, fnorm_d.partition_broadcast(128))
        mxb = [alloc(c3, [D], BF16), alloc(c3, [D], BF16)]
        xb3 = [alloc(c3, [D], F32), alloc(c3, [D], F32)]
        x1b2 = [alloc(c3, [D], F32), alloc(c3, [D], F32)]
        h3T = alloc(c3, [8, 128], BF16)
        mxT = alloc(c3, [8, 128], BF16)
        h3 = alloc(c3, [D], BF16)
        ss32 = [alloc(c3, [4], F32), alloc(c3, [4], F32)]
        sil = [alloc(c3, [512], F32), alloc(c3, [512], F32)]
        aT_in = alloc(c3, [DFF], BF16)
        junk3 = aT_in[:, 0:D]
        aT = alloc(c3, [22, 128], BF16)

        toks = []

        def L(t):
            dma("sp", mxb[t % 2], mix[t * 128:(t + 1) * 128, :])
            dma("sp", xb3[t % 2], x_src[t * 128:(t + 1) * 128, :])

        def Fa(t):
            mx = mxb[t % 2]
            b0 = bank_bf(0)
            for k in range(8):
                trp(b0[:, k * 128:(k + 1) * 128], mx[:, k * 128:(k + 1) * 128], identb)
            copy("act", mxT, b0.rearrange("p (k t) -> p k t", k=8))

        def Fb(t):
            xs = xb3[t % 2]
            x1b = x1b2[t % 2]
            ss3 = ss32[t % 2]
            for hf in range(2):
                for k in range(8):
                    mm(psb[4 + hf][:, :], mxT[:, k, :], wo_b[:, k, hf * 512:(hf + 1) * 512], k == 0, k == 7)
            for hf in range(2):
                tt("dve", x1b[:, hf * 512:(hf + 1) * 512], psb[4 + hf][:, :], xs[:, hf * 512:(hf + 1) * 512], ALU.add)
            act(junk3, x1b, AF.Square, accum=ss3[:, 0:1])
            act(ss3[:, 1:2], ss3[:, 0:1], AF.Sqrt, scale=1.0 / D, bias=EPS)
            recip(ss3[:, 2:3], ss3[:, 1:2])
            ts("dve", h3, x1b, ss3[:, 2:3], None, ALU.mult)

        def Fc(t):
            b1 = bank_bf(1)
            for k in range(8):
                trp(b1[:, k * 128:(k + 1) * 128], h3[:, k * 128:(k + 1) * 128], identb)
            copy("act", h3T, b1.rearrange("p (k t) -> p k t", k=8))

        def G(t):
            for n in range(6):
                n0 = n * 512
                nn = min(512, DFF - n0)
                pg = psb[2]
                pu = psb[3]
                for k in range(8):
                    mm(pg[:, 0:nn], h3T[:, k, :], wg_b[:, k, n0:n0 + nn], k == 0, k == 7)
                for k in range(8):
                    mm(pu[:, 0:nn], h3T[:, k, :], wu_b[:, k, n0:n0 + nn], k == 0, k == 7)
                sl = sil[n % 2]
                act(sl[:, 0:nn], pg[:, 0:nn], AF.Silu)
                tt("dve", aT_in[:, n0:n0 + nn], sl[:, 0:nn], pu[:, 0:nn], ALU.mult)

        def T(t):
            for r in range(3):
                bb = bank_bf((r + 1) % 2)
                c_lo = r * 8
                c_hi = min(22, c_lo + 8)
                for c in range(c_lo, c_hi):
                    trp(bb[:, (c - c_lo) * 128:(c - c_lo + 1) * 128], aT_in[:, c * 128:(c + 1) * 128], identb)
                copy("act" if r != 1 else "dve", aT[:, c_lo:c_hi, :],
                     bb[:, 0:(c_hi - c_lo) * 128].rearrange("p (c t) -> p c t", c=c_hi - c_lo))

        def Dn(t):
            x1b = x1b2[t % 2]
            ss3 = ss32[t % 2]
            for hf in range(2):
                for c in range(22):
                    mm(psb[6 + hf][:, :], aT[:, c, :], wd_b[:, c, hf * 512:(hf + 1) * 512], c == 0, c == 21)
            x2 = xb3[t % 2]
            for hf in range(2):
                tt("dve", x2[:, hf * 512:(hf + 1) * 512], psb[6 + hf][:, :], x1b[:, hf * 512:(hf + 1) * 512], ALU.add)
            if is_final:
                act(x1b, x2, AF.Square, accum=ss3[:, 0:1])
                act(ss3[:, 1:2], ss3[:, 0:1], AF.Sqrt, scale=1.0 / D, bias=EPS)
                recip(ss3[:, 2:3], ss3[:, 1:2])
                stt(x2, x2, ss3[:, 2:3], fng, ALU.mult, ALU.mult)
            toks.append(dma("pool", x_dst[t * 128:(t + 1) * 128, :], x2))

        L(0)
        Fa(0)
        Fb(0)
        Fc(0)
        G(0)
        for t in range(NT):
            nx = t + 1 < NT
            if nx:
                L(t + 1)
                Fa(t + 1)
            T(t)
            if nx:
                Fb(t + 1)
            Dn(t)
            if nx:
                Fc(t + 1)
                G(t + 1)
        return toks + (dtoks if dbg else [])

    if stacked:
        xmid = [P.dram("xs%d" % i, [S, D], F32) for i in range(2)]
        toks = None
        for l in range(nlayers):
            P.group = l
            for n, _ in WNAMES:
                Wd[n] = Wfull[n][l]
            src = x_in if l == 0 else xmid[(l - 1) % 2]
            dst = y_out if l == nlayers - 1 else xmid[l % 2]
            toks = layer(src, dst, final and l == nlayers - 1)
    else:
        toks = layer(x_src, y_out, final)
    P.emit(final_waits=toks)
    nc._prog_stats = {e: len(P.instrs[e]) for e in P.ENGS}
    return nc


_CACHE = {}


def kernel(**inputs):
    x = np.ascontiguousarray(np.asarray(inputs["x"], dtype=np.float32))
    B = x.shape[0]
    cb, cf = _consts()
    if "nc" not in _CACHE:
        _CACHE["nc"] = build_layer(final=True, nlayers=DEPTH, stacked=True)
    nc = _CACHE["nc"]
    shared = {"cb": cb, "cf": cf,
              "final_norm": np.ascontiguousarray(np.asarray(inputs["final_norm"], dtype=np.float32))}
    for n, _ in WNAMES:
        shared[n] = np.ascontiguousarray(np.asarray(inputs[n], dtype=np.float32))
    in_maps = []
    for b in range(B):
        m = dict(shared)
        m["x"] = x[b]
        in_maps.append(m)
    res = run_bass_kernel_spmd(nc, in_maps, core_ids=list(range(B)))
    return np.stack([np.asarray(res.results[b]["y"], dtype=np.float32) for b in range(B)], axis=0)
```
